# Optimizing a Trainium2 kernel written in Bass

```python
import functools
import jax, jax.numpy as jnp
from jax import lax
import numpy as np

D_MODEL = 1024
BATCH = 16
SEQ = 2048
DEPTH = 1
DEC_BATCH = 32
DEC_SEQ = 4
PAST_LEN = 16384
PAGE_SIZE = 128

N_HEADS = 8
HEAD_DIM = 64
ATT_W = N_HEADS * HEAD_DIM
MOBA_BLOCK = 256
MOBA_TOPK = 3
Q_CHUNK = 128
POOL_WINDOWS = (2, 4, 8, 16)
POOL_GROUPS = len(POOL_WINDOWS)
POOL_W = D_MODEL // 2
POOL_GW = POOL_W // POOL_GROUPS
POOL_CTX = max(POOL_WINDOWS) - 1
D_FF = ((8 * D_MODEL // 3 + 255) // 256) * 256
CONV_W = 3
N_MOD = 6
EPS = 1e-6
IN_SPLITS = (POOL_W, ATT_W, ATT_W, ATT_W, D_MODEL, D_MODEL)
ADA_INIT = 0.5
NOISE = 0.05

kernel_name = 'moba_pool_hybrid_decode_step'


def rms_norm(x, gain):
    xf = x.astype(jnp.float32)
    y = xf * lax.rsqrt(jnp.mean(xf * xf, axis=-1, keepdims=True) + EPS)
    return (y * gain.astype(jnp.float32)).astype(x.dtype)


def ada_modulation(c, w_ada, b_ada):
    mod = (jax.nn.silu(c) @ w_ada + b_ada).reshape(c.shape[0], N_MOD, D_MODEL)
    return [mod[:, None, i] for i in range(N_MOD)]


def pool_mixer(u, u_past, pos0, w_pool_group, pool_scale):
    n, s, _ = u.shape
    ext = jnp.concatenate([jnp.zeros((n, 1, POOL_W), u.dtype), u_past, u], axis=1).astype(jnp.float32)
    cs = jnp.cumsum(ext, axis=1)
    end = cs[:, POOL_CTX + 1:]
    pos = pos0 + jnp.arange(s)
    groups = []
    for g, w in enumerate(POOL_WINDOWS):
        sl = slice(g * POOL_GW, (g + 1) * POOL_GW)
        start = cs[:, POOL_CTX + 1 - w:POOL_CTX + 1 - w + s, sl]
        cnt = jnp.minimum(w, pos + 1).astype(jnp.float32)[None, :, None]
        groups.append((end[..., sl] - start) / cnt)
    pooled = jnp.concatenate(groups, axis=-1) - u.astype(jnp.float32)
    pg = pooled.astype(u.dtype).reshape(n, s, POOL_GROUPS, POOL_GW)
    y = jnp.einsum('nsgc,gcd->nsgd', pg, w_pool_group).reshape(n, s, POOL_W)
    return y * pool_scale


def causal_dwconv(a, a_past, w_conv, b_conv):
    s = a.shape[1]
    ext = jnp.concatenate([a_past, a], axis=1)
    y = b_conv
    for j in range(CONV_W):
        y = y + ext[:, j:j + s] * w_conv[j]
    return y


def moba_prompt(q, k, v):
    n, t = q.shape[:2]
    n_blk = -(-t // MOBA_BLOCK)
    t_pad = n_blk * MOBA_BLOCK
    scale = HEAD_DIM ** -0.5
    qh, kh, vh = (a.transpose(0, 2, 1, 3) for a in (q, k, v))
    pad = ((0, 0), (0, 0), (0, t_pad - t), (0, 0))
    kb = jnp.pad(kh, pad).reshape(n, N_HEADS, n_blk, MOBA_BLOCK, HEAD_DIM)
    vb = jnp.pad(vh, pad).reshape(n, N_HEADS, n_blk, MOBA_BLOCK, HEAD_DIM)
    k_mean = jnp.mean(kb.astype(jnp.float32), axis=3).astype(q.dtype)
    blk_score = jnp.einsum('nhtd,nhbd->nhtb', qh, k_mean).astype(jnp.float32)
    q_blk = jnp.arange(t) // MOBA_BLOCK
    past_blk = jnp.arange(n_blk)[None, :] < q_blk[:, None]
    blk_score = jnp.where(past_blk, blk_score, -jnp.inf)
    top = min(MOBA_TOPK, n_blk)
    _, sel = lax.top_k(blk_score, top)
    sel_ok = jnp.arange(top)[None, :] < q_blk[:, None]
    n_chunk = t // Q_CHUNK
    q_c = qh.reshape(n, N_HEADS, n_chunk, Q_CHUNK, HEAD_DIM).transpose(2, 0, 1, 3, 4)
    sel_c = sel.reshape(n, N_HEADS, n_chunk, Q_CHUNK, top).transpose(2, 0, 1, 3, 4)
    ok_c = sel_ok.reshape(n_chunk, Q_CHUNK, top)
    bi = jnp.arange(n)[:, None, None, None]
    hi = jnp.arange(N_HEADS)[None, :, None, None]

    def chunk(args):
        qc, selc, okc, ci = args
        q_pos = ci * Q_CHUNK + jnp.arange(Q_CHUNK)
        own = (ci * Q_CHUNK) // MOBA_BLOCK
        k_own = lax.dynamic_index_in_dim(kb, own, axis=2, keepdims=False)
        v_own = lax.dynamic_index_in_dim(vb, own, axis=2, keepdims=False)
        k_pos = own * MOBA_BLOCK + jnp.arange(MOBA_BLOCK)
        k_sel = kb[bi, hi, selc].reshape(n, N_HEADS, Q_CHUNK, top * MOBA_BLOCK, HEAD_DIM)
        v_sel = vb[bi, hi, selc].reshape(n, N_HEADS, Q_CHUNK, top * MOBA_BLOCK, HEAD_DIM)
        ls = jnp.einsum('nhqd,nhqkd->nhqk', qc, k_sel).astype(jnp.float32) * scale
        ls = jnp.where(jnp.repeat(okc, MOBA_BLOCK, axis=-1)[None, None], ls, -jnp.inf)
        lo = jnp.einsum('nhqd,nhjd->nhqj', qc, k_own).astype(jnp.float32) * scale
        lo = jnp.where((k_pos[None, :] <= q_pos[:, None])[None, None], lo, -jnp.inf)
        p = jax.nn.softmax(jnp.concatenate([ls, lo], axis=-1), axis=-1).astype(vb.dtype)
        p_sel, p_own = p[..., :top * MOBA_BLOCK], p[..., top * MOBA_BLOCK:]
        return (jnp.einsum('nhqk,nhqkd->nhqd', p_sel, v_sel)
                + jnp.einsum('nhqj,nhjd->nhqd', p_own, v_own))

    out = lax.map(chunk, (q_c, sel_c, ok_c, jnp.arange(n_chunk)))
    return out.transpose(1, 0, 3, 2, 4).reshape(n, t, N_HEADS, HEAD_DIM)


def moba_sample(q, k, v, cache_k, cache_v, page_table, layer):
    n, s = q.shape[:2]
    past = page_table.shape[1] * PAGE_SIZE
    ppb = MOBA_BLOCK // PAGE_SIZE
    n_blk = past // MOBA_BLOCK
    own_past = past - n_blk * MOBA_BLOCK
    scale = HEAD_DIM ** -0.5
    qh, kh, vh = (a.transpose(0, 2, 1, 3) for a in (q, k, v))
    logits, values = [], []
    if n_blk > 0:
        k_past = cache_k[layer, page_table[:, :n_blk * ppb]]
        k_mean = jnp.mean(k_past.reshape(n, n_blk, MOBA_BLOCK, N_HEADS, HEAD_DIM).astype(jnp.float32),
                          axis=2).astype(q.dtype)
        blk_score = jnp.einsum('nhsd,nbhd->nhsb', qh, k_mean).astype(jnp.float32)
        top = min(MOBA_TOPK, n_blk)
        _, sel = lax.top_k(blk_score, top)
        logical = sel[..., None] * ppb + jnp.arange(ppb)
        phys = page_table[jnp.arange(n)[:, None, None, None, None], logical]
        hi = jnp.arange(N_HEADS)[None, :, None, None, None]
        k_sel = cache_k[layer, phys, :, hi].reshape(n, N_HEADS, s, top * MOBA_BLOCK, HEAD_DIM)
        v_sel = cache_v[layer, phys, :, hi].reshape(n, N_HEADS, s, top * MOBA_BLOCK, HEAD_DIM)
        logits.append(jnp.einsum('nhsd,nhskd->nhsk', qh, k_sel).astype(jnp.float32) * scale)
        values.append(v_sel)
    if own_past > 0:
        first = n_blk * ppb
        k_op = cache_k[layer, page_table[:, first:]].reshape(n, own_past, N_HEADS, HEAD_DIM).transpose(0, 2, 1, 3)
        v_op = cache_v[layer, page_table[:, first:]].reshape(n, own_past, N_HEADS, HEAD_DIM).transpose(0, 2, 1, 3)
        k_own = jnp.concatenate([k_op, kh], axis=2)
        v_own = jnp.concatenate([v_op, vh], axis=2)
    else:
        k_own, v_own = kh, vh
    j = jnp.arange(own_past + s)
    own_mask = (j[None, :] < own_past) | ((j[None, :] - own_past) <= jnp.arange(s)[:, None])
    lo = jnp.einsum('nhsd,nhjd->nhsj', qh, k_own).astype(jnp.float32) * scale
    logits.append(jnp.where(own_mask[None, None], lo, -jnp.inf))
    p = jax.nn.softmax(jnp.concatenate(logits, axis=-1), axis=-1).astype(v.dtype)
    n_sel = p.shape[-1] - (own_past + s)
    o = jnp.einsum('nhsj,nhjd->nhsd', p[..., n_sel:], v_own)
    if n_blk > 0:
        o = o + jnp.einsum('nhsk,nhskd->nhsd', p[..., :n_sel], values[0])
    return o.transpose(0, 2, 1, 3)


def decoder_layer(x, c, pool_past, conv_past, pos0, attend,
                  w_ada, b_ada, g_norm_mix, w_in, g_q, g_k, w_pool_group, pool_scale,
                  w_branch_pool, w_branch_attn, w_out, g_norm_ffn, w_up, w_conv, b_conv, w_down):
    n, s, _ = x.shape
    sh_m, sc_m, gt_m, sh_f, sc_f, gt_f = ada_modulation(c, w_ada, b_ada)
    h = rms_norm(x, g_norm_mix) * (1 + sc_m) + sh_m
    cuts = np.cumsum(IN_SPLITS)[:-1].tolist()
    u, q, k, v, g_a, g_b = jnp.split(h @ w_in, cuts, axis=-1)
    a = pool_mixer(u, pool_past, pos0, w_pool_group, pool_scale) @ w_branch_pool
    q = rms_norm(q.reshape(n, s, N_HEADS, HEAD_DIM), g_q)
    k = rms_norm(k.reshape(n, s, N_HEADS, HEAD_DIM), g_k)
    v = v.reshape(n, s, N_HEADS, HEAD_DIM)
    o = attend(q, k, v).reshape(n, s, ATT_W) @ w_branch_attn
    merged = jax.nn.sigmoid(g_a) * a + jax.nn.sigmoid(g_b) * o
    x = x + gt_m * (merged @ w_out)
    h2 = rms_norm(x, g_norm_ffn) * (1 + sc_f) + sh_f
    f_gate, f_val = jnp.split(h2 @ w_up, 2, axis=-1)
    f_conv = causal_dwconv(f_gate, conv_past, w_conv, b_conv)
    x = x + gt_f * ((jax.nn.silu(f_conv) * f_val) @ w_down)
    new_pool = jnp.concatenate([pool_past, u], axis=1)[:, -POOL_CTX:]
    new_conv = jnp.concatenate([conv_past, f_gate], axis=1)[:, -(CONV_W - 1):]
    return x, k, v, new_pool, new_conv


def setup_inputs(seed: int = 0) -> dict:
    key = jax.random.key(seed)
    ks = jax.random.split(key, 32)
    nrm = jax.random.normal
    n_pages = PAST_LEN // PAGE_SIZE
    n_used = DEC_BATCH * n_pages
    n_phys = n_used + max(1, n_used // 4)
    page_table = jax.random.permutation(ks[0], n_phys)[:n_used].reshape(DEC_BATCH, n_pages).astype(jnp.int32)
    d_in = sum(IN_SPLITS)
    return {
        'x_prompt': nrm(ks[1], (BATCH, SEQ, D_MODEL), jnp.float32),
        'x_sample': nrm(ks[2], (DEC_BATCH, DEC_SEQ, D_MODEL), jnp.float32),
        'cache_k': nrm(ks[3], (DEPTH, n_phys, PAGE_SIZE, N_HEADS, HEAD_DIM), jnp.float32),
        'cache_v': nrm(ks[4], (DEPTH, n_phys, PAGE_SIZE, N_HEADS, HEAD_DIM), jnp.float32),
        'state_pool': nrm(ks[5], (DEPTH, DEC_BATCH, POOL_CTX, POOL_W), jnp.float32),
        'state_ffn_conv': nrm(ks[6], (DEPTH, DEC_BATCH, CONV_W - 1, D_FF), jnp.float32),
        'page_table': page_table,
        'c_prompt': nrm(ks[7], (BATCH, D_MODEL), jnp.float32),
        'c_sample': nrm(ks[8], (DEC_BATCH, D_MODEL), jnp.float32),
        'w_ada': nrm(ks[9], (DEPTH, D_MODEL, N_MOD * D_MODEL), jnp.float32) * (ADA_INIT * D_MODEL ** -0.5),
        'b_ada': nrm(ks[10], (DEPTH, N_MOD * D_MODEL), jnp.float32) * 0.02,
        'g_norm_mix': 1.0 + NOISE * nrm(ks[11], (DEPTH, D_MODEL), jnp.float32),
        'w_in': nrm(ks[12], (DEPTH, D_MODEL, d_in), jnp.float32) * D_MODEL ** -0.5,
        'g_q': 1.0 + NOISE * nrm(ks[13], (DEPTH, N_HEADS, HEAD_DIM), jnp.float32),
        'g_k': 1.0 + NOISE * nrm(ks[14], (DEPTH, N_HEADS, HEAD_DIM), jnp.float32),
        'w_pool_group': nrm(ks[15], (DEPTH, POOL_GROUPS, POOL_GW, POOL_GW), jnp.float32) * POOL_GW ** -0.5,
        'pool_scale': 1.0 + NOISE * nrm(ks[16], (DEPTH, POOL_W), jnp.float32),
        'w_branch_pool': nrm(ks[17], (DEPTH, POOL_W, D_MODEL), jnp.float32) * POOL_W ** -0.5,
        'w_branch_attn': nrm(ks[18], (DEPTH, ATT_W, D_MODEL), jnp.float32) * ATT_W ** -0.5,
        'w_out': nrm(ks[19], (DEPTH, D_MODEL, D_MODEL), jnp.float32) * D_MODEL ** -0.5,
        'g_norm_ffn': 1.0 + NOISE * nrm(ks[20], (DEPTH, D_MODEL), jnp.float32),
        'w_up': nrm(ks[21], (DEPTH, D_MODEL, 2 * D_FF), jnp.float32) * D_MODEL ** -0.5,
        'w_conv': nrm(ks[22], (DEPTH, CONV_W, D_FF), jnp.float32) * CONV_W ** -0.5,
        'b_conv': nrm(ks[23], (DEPTH, D_FF), jnp.float32) * 0.02,
        'w_down': nrm(ks[24], (DEPTH, D_FF, D_MODEL), jnp.float32) * D_FF ** -0.5,
    }


def reference(x_prompt, x_sample, cache_k, cache_v, state_pool, state_ffn_conv, page_table,
              c_prompt, c_sample, w_ada, b_ada, g_norm_mix, w_in, g_q, g_k, w_pool_group, pool_scale,
              w_branch_pool, w_branch_attn, w_out, g_norm_ffn, w_up, w_conv, b_conv, w_down):
    y_p, y_s = x_prompt, x_sample
    k_p, v_p, pool_p, conv_p = [], [], [], []
    k_s, v_s, pool_s, conv_s = [], [], [], []
    nb = x_prompt.shape[0]
    zero_pool = jnp.zeros((nb, POOL_CTX, POOL_W), x_prompt.dtype)
    zero_conv = jnp.zeros((nb, CONV_W - 1, D_FF), x_prompt.dtype)
    for l in range(DEPTH):
        lw = (w_ada[l], b_ada[l], g_norm_mix[l], w_in[l], g_q[l], g_k[l], w_pool_group[l], pool_scale[l],
              w_branch_pool[l], w_branch_attn[l], w_out[l], g_norm_ffn[l], w_up[l], w_conv[l], b_conv[l], w_down[l])
        y_p, kl, vl, pl, cl = decoder_layer(y_p, c_prompt, zero_pool, zero_conv, 0, moba_prompt, *lw)
        k_p.append(kl); v_p.append(vl); pool_p.append(pl); conv_p.append(cl)
        attend_s = functools.partial(moba_sample, cache_k=cache_k, cache_v=cache_v,
                                     page_table=page_table, layer=l)
        y_s, kl, vl, pl, cl = decoder_layer(y_s, c_sample, state_pool[l], state_ffn_conv[l], PAST_LEN,
                                            attend_s, *lw)
        k_s.append(kl); v_s.append(vl); pool_s.append(pl); conv_s.append(cl)
    return (y_p, y_s, jnp.stack(k_p), jnp.stack(v_p), jnp.stack(pool_p), jnp.stack(conv_p),
            jnp.stack(k_s), jnp.stack(v_s), jnp.stack(pool_s), jnp.stack(conv_s))
```

```python
import numpy as np
from contextlib import ExitStack
import concourse.bass as bass
import concourse.mybir as mybir
from concourse.bass_utils import run_bass_kernel_spmd

F32 = mybir.dt.float32
BF16 = mybir.dt.bfloat16
I32 = mybir.dt.int32
AF = mybir.ActivationFunctionType
ALU = mybir.AluOpType
AX = mybir.AxisListType

D = 1024
SEQ = 2048
NPB = 2
NSB = 4
NST = 16
DFF = 2816
NFC = 22
G = 256
NG = SEQ // G
EPS = 1e-6
NEG = -30000.0
NPAGES = 128
RING = 4


class View:
    __slots__ = ("tile", "ap")

    def __init__(self, tile, ap):
        self.tile = tile
        self.ap = ap

    def __getitem__(self, idx):
        return View(self.tile, self.ap[idx])

    def bitcast(self, dt):
        return View(self.tile, self.ap.bitcast(dt))

    def rearrange(self, pattern, **kw):
        return View(self.tile, self.ap.rearrange(pattern, **kw))

    def bcast(self, shape):
        return View(self.tile, self.ap.to_broadcast(list(shape)))

    def unsq(self, ax):
        return View(self.tile, self.ap.unsqueeze(ax))


class Tile:
    def __init__(self, name, ap):
        self.name = name
        self.ap = ap
        self.last_write = None
        self.readers = {}
        self.dsem = None

    def __getitem__(self, idx):
        return View(self, self.ap[idx])

    def v(self, ap=None):
        return View(self, self.ap if ap is None else ap)


class Ctx:
    def __init__(self, nc, es):
        self.nc = nc
        self.es = es
        self.E = {"pe": nc.tensor, "act": nc.scalar, "dve": nc.vector, "pool": nc.gpsimd, "sp": nc.sync}
        self.sems = {}
        self.semval = {}
        self.seen = {e: {} for e in self.E}
        for e in self.E:
            self.sems[e] = es.enter_context(nc.semaphore("sem_" + e))
            self.semval[e] = 0
        self.ndsem = 0
        self.out_events = {}
        self.dsem_pool = {}

    def sbuf(self, name, shape, dtype):
        t = self.es.enter_context(self.nc.sbuf_tensor(name, list(shape), dtype))
        return Tile(name, t[:])

    def psum(self, name, shape, dtype):
        t = self.es.enter_context(self.nc.psum_tensor(name, list(shape), dtype))
        return Tile(name, t[:])

    @staticmethod
    def _tiles(views):
        out = []
        for v in views:
            if v is None:
                continue
            t = v.tile if isinstance(v, View) else (v if isinstance(v, Tile) else None)
            if t is not None and t not in out:
                out.append(t)
        return out

    def _deps(self, rt, wt):
        deps = {}

        def add(d):
            if d is None:
                return
            k, v = d
            if deps.get(k, 0) < v:
                deps[k] = v

        for t in rt:
            add(t.last_write)
        for t in wt:
            add(t.last_write)
            for d in t.readers.values():
                add(d)
        return deps

    def _wait(self, eng, deps, skip_own=False):
        for k, v in deps.items():
            if skip_own and k == eng:
                continue
            if self.seen[eng].get(k, 0) >= v:
                continue
            self.E[eng].wait_ge(self.sems[k], v)
            self.seen[eng][k] = v

    def op(self, eng, fn, reads, writes):
        rt = self._tiles(reads)
        wt = self._tiles(writes)
        deps = self._deps(rt, wt)
        self._wait(eng, deps, skip_own=(eng == "pe"))
        ins = fn()
        self.semval[eng] += 1
        ins.then_inc(self.sems[eng], 1)
        ev = (eng, self.semval[eng])
        for t in rt:
            t.readers[eng] = ev
        for t in wt:
            t.last_write = ev
            t.readers = {}
        return ins

    def barrier(self):
        engs = ["pe", "act", "dve", "pool"]
        for e in engs:
            for f in engs:
                if e == f:
                    continue
                v = self.semval[f]
                if v > 0 and self.seen[e].get(f, 0) < v:
                    self.E[e].wait_ge(self.sems[f], v)
                    self.seen[e][f] = v

    def _dsem(self, t):
        if t.dsem is None:
            k = "d%d" % self.ndsem
            self.ndsem += 1
            self.sems[k] = self.es.enter_context(self.nc.semaphore("sem_" + k))
            self.semval[k] = 0
            t.dsem = k
        return t.dsem

    def dma(self, q, out, in_, is_output=False, indirect_in=None, extra_reads=(), no_waw=False, **kw):
        ot = out.tile if isinstance(out, View) else None
        it = in_.tile if isinstance(in_, View) else None
        rt = ([it] if it else []) + self._tiles(list(extra_reads) + ([indirect_in] if indirect_in is not None else []))
        wt = [ot] if ot else []
        deps = self._deps(rt, [] if no_waw else wt)
        self._wait(q, deps)
        st = ot or it
        k = self._dsem(st)
        self.semval[k] += 16
        oap = out.ap if isinstance(out, View) else out
        iap = in_.ap if isinstance(in_, View) else in_
        if indirect_in is not None:
            ins = self.E[q].indirect_dma_start(
                out=oap, out_offset=None, in_=iap,
                in_offset=bass.IndirectOffsetOnAxis(ap=indirect_in.ap, axis=0), **kw)
        else:
            ins = self.E[q].dma_start(out=oap, in_=iap, **kw)
        ins.then_inc(self.sems[k], 16)
        ev = (k, self.semval[k])
        for t in rt:
            t.readers[k] = ev
        if ot:
            ot.last_write = ev
            ot.readers = {}
        if is_output:
            self.out_events[k] = self.semval[k]
        return ins

    def finish(self, eng="sp"):
        for k, v in self.out_events.items():
            if self.seen[eng].get(k, 0) < v:
                self.E[eng].wait_ge(self.sems[k], v)
                self.seen[eng][k] = v

    def mm(self, out, pairs, start=True, stop=True):
        reads = []
        for l, r in pairs:
            reads += [l, r]
        n = len(pairs)

        def fn():
            ins = None
            for i, (l, r) in enumerate(pairs):
                ins = self.nc.tensor.matmul(out.ap, l.ap, r.ap,
                                            start=(start and i == 0), stop=(stop and i == n - 1))
            return ins

        return self.op("pe", fn, reads, [out])

    def transposes(self, items, ident):
        reads = [ident] + [i for _, i in items]
        writes = [o for o, _ in items]

        def fn():
            ins = None
            for o, i in items:
                k = i.ap.shape[0]
                ins = self.nc.tensor.transpose(o.ap, i.ap, ident.ap[0:k, 0:k])
            return ins

        return self.op("pe", fn, reads, writes)

    def act(self, out, in_, func, bias=None, scale=None, accum_out=None):
        kw = {}
        reads = [in_]
        for nm, val in (("bias", bias), ("scale", scale)):
            if val is None:
                continue
            if isinstance(val, View):
                kw[nm] = val.ap
                reads.append(val)
            else:
                kw[nm] = val
        writes = [out]
        if accum_out is not None:
            kw["accum_out"] = accum_out.ap
            writes.append(accum_out)
        return self.op("act", lambda: self.nc.scalar.activation(out=out.ap, in_=in_.ap, func=func, **kw), reads, writes)

    def _ve(self, eng):
        return self.nc.vector if eng == "dve" else self.nc.gpsimd

    def tt(self, eng, out, in0, in1, op):
        return self.op(eng, lambda: self._ve(eng).tensor_tensor(out=out.ap, in0=in0.ap, in1=in1.ap, op=op), [in0, in1], [out])

    def ts(self, eng, out, in0, s1, s2, op0, op1=None):
        reads = [in0]
        a1, a2 = s1, s2
        if isinstance(s1, View):
            a1 = s1.ap
            reads.append(s1)
        if isinstance(s2, View):
            a2 = s2.ap
            reads.append(s2)
        kw = {}
        if op1 is not None:
            kw["op1"] = op1
        return self.op(eng, lambda: self._ve(eng).tensor_scalar(out=out.ap, in0=in0.ap, scalar1=a1, scalar2=a2, op0=op0, **kw), reads, [out])

    def stt(self, eng, out, in0, scalar, in1, op0, op1):
        reads = [in0, in1]
        a = scalar
        if isinstance(scalar, View):
            a = scalar.ap
            reads.append(scalar)
        return self.op(eng, lambda: self._ve(eng).scalar_tensor_tensor(out=out.ap, in0=in0.ap, scalar=a, in1=in1.ap, op0=op0, op1=op1), reads, [out])

    def copy(self, eng, out, in_):
        if eng == "act":
            return self.op("act", lambda: self.nc.scalar.copy(out=out.ap, in_=in_.ap), [in_], [out])
        return self.op(eng, lambda: self._ve(eng).tensor_copy(out=out.ap, in_=in_.ap), [in_], [out])

    def memset(self, eng, out, val):
        return self.op(eng, lambda: self._ve(eng).memset(out.ap, val), [], [out])

    def reduce(self, eng, out, in_, op, axis=AX.X):
        return self.op(eng, lambda: self._ve(eng).tensor_reduce(out=out.ap, in_=in_.ap, axis=axis, op=op), [in_], [out])

    def recip(self, out, in_):
        return self.op("dve", lambda: self.nc.vector.reciprocal(out=out.ap, in_=in_.ap), [in_], [out])


def host_consts():
    c = {}
    c["ident"] = np.eye(128, dtype=np.float32)
    kk = np.arange(128)[:, None]
    qq = np.arange(128)[None, :]
    c["tri"] = np.where(kk <= qq, 0.0, NEG).astype(np.float32)
    c["kext"] = (np.arange(SEQ)[None, :] // G == np.arange(8)[:, None]).astype(np.float32)
    inv = np.zeros((128, 4, 16), np.float32)
    for g, w in enumerate((2, 4, 8, 16)):
        inv[:, g, :] = 1.0 / np.minimum(w, np.arange(16) + 1)
    c["invcnt"] = inv.reshape(128, 64)
    sel = np.zeros((16, 4, 128), np.float32)
    for b in range(4):
        for s in range(4):
            sel[4 * b + s, b, s] = 1.0
    c["sel"] = sel.reshape(16, 512)
    caus = np.full((128, 32), NEG, np.float32)
    for s in range(4):
        for h in range(8):
            for key in range(4):
                if key <= s:
                    caus[key, s * 8 + h] = 0.0
    c["caus"] = caus
    hm = np.zeros((32, 8), np.float32)
    gs = np.zeros((32, 4), np.float32)
    for s in range(4):
        for h in range(8):
            hm[s * 8 + h, h] = 1.0
            gs[s * 8 + h, s] = 1.0
    c["hmask"] = hm
    c["gsum"] = gs
    c["i32"] = np.eye(32, dtype=np.float32)
    c["iota"] = np.arange(128, dtype=np.float32).reshape(128, 1)
    return c


CONST_SHAPES = {"ident": [128, 128], "tri": [128, 128], "kext": [8, 2048], "invcnt": [128, 64],
                "sel": [16, 512], "caus": [128, 32], "hmask": [32, 8], "gsum": [32, 4], "i32": [32, 32],
                "iota": [128, 1]}

IN_SHAPES = {
    "xp": ([NPB * SEQ, D], F32), "xs": ([NST, D], F32), "c_all": ([NPB + NSB, D], F32),
    "w_ada": ([D, 6 * D], F32), "b_ada": ([1, 6 * D], F32), "g_mix": ([1, D], F32),
    "w_in": ([D, 4096], F32), "g_q": ([1, 512], F32), "g_k": ([1, 512], F32),
    "w_pool": ([512, 128], F32), "pool_scale": ([1, 512], F32),
    "w_bp": ([512, D], F32), "w_ba": ([512, D], F32), "w_out": ([D, D], F32), "g_ffn": ([1, D], F32),
    "w_up": ([D, 2 * DFF], F32), "w_conv": ([3, DFF], F32), "b_conv": ([1, DFF], F32), "w_down": ([DFF, D], F32),
    "st_pool": ([NSB * 15, 512], F32), "st_conv": ([NSB * 2, DFF], F32), "ptab": ([1, NSB * NPAGES], I32),
    "cache_k": ([5120 * 128, 512], F32), "cache_v": ([5120 * 128, 512], F32),
}
A_OUTS = ("y_p", "k_p", "v_p", "pool_p", "conv_p", "k_s", "v_s", "pool_s", "q_s")
C_OUTS = ("y_s", "conv_s")
OUT_SHAPES = {
    "q_s": [NST, 512],
    "y_p": [NPB * SEQ, D], "y_s": [NST, D], "k_p": [NPB * SEQ, 512], "v_p": [NPB * SEQ, 512],
    "pool_p": [NPB * 15, 512], "conv_p": [NPB * 2, DFF], "k_s": [NST, 512], "v_s": [NST, 512],
    "pool_s": [NSB * 15, 512], "conv_s": [NSB * 2, DFF],
}


class Kern:
    def __init__(self, nc, es, mode="A"):
        self.nc = nc
        self.C = Ctx(nc, es)
        self.mode = mode
        self.din = {}
        for k, (shp, dt) in IN_SHAPES.items():
            if k in ("cache_k", "cache_v", "ptab"):
                continue
            if mode == "C" and k == "xp":
                continue
            self.din[k] = nc.dram_tensor(k, shp, dt, kind="ExternalInput").ap()
        if mode == "C":
            self.din["o_in"] = nc.dram_tensor("o_in", [NST, 512], F32, kind="ExternalInput").ap()
        for k, shp in CONST_SHAPES.items():
            self.din["c_" + k] = nc.dram_tensor("c_" + k, shp, F32, kind="ExternalInput").ap()
        outs = A_OUTS if mode == "A" else C_OUTS
        self.dout = {k: nc.dram_tensor(k, OUT_SHAPES[k], F32, kind="ExternalOutput").ap() for k in outs}
        self.wscr = {}
        for k, shp in (("w_in", [D, 4096]), ("w_bp", [512, D]), ("w_ba", [512, D]), ("w_out", [D, D]),
                       ("w_up", [D, 2 * DFF]), ("w_down", [DFF, D]), ("w_adag", [D, 2 * D])):
            ap = nc.dram_tensor("scr_" + k, shp, BF16, kind="Internal").ap()
            self.wscr[k] = (ap, Tile("scr_" + k, ap))
        self.alloc()

    def alloc(self):
        C = self.C
        sb = C.sbuf
        self.ident = sb("ident", [128, 128], BF16)
        self.tri = sb("tri", [128, 128], BF16)
        self.invcnt = sb("invcnt", [128, 4, 16], F32)
        self.gq = sb("gq", [128, 512], F32)
        self.gk = sb("gk", [128, 512], F32)
        self.wpool = sb("wpool", [128, 4, 128], BF16)
        self.pscaleT = sb("pscaleT", [128, 4], F32)
        self.wconvT = sb("wconvT", [128, NFC, 3], F32)
        self.bconvT = sb("bconvT", [128, NFC], F32)
        self.gmixT = sb("gmixT", [128, 8], F32)
        self.gffnT = sb("gffnT", [128, 8], F32)
        self.modT = sb("modT", [128, 48, 6], F32)
        self.scl = sb("scl", [128, 2, 8, 6], F32)
        self.badaT = sb("badaT", [128, 48], F32)
        self.cT = sb("cT", [128, 8, 6], F32)
        self.siluT = sb("siluT", [128, 8, 6], BF16)
        self.silurep = sb("silurep", [128, 8, 128], BF16)
        self.bada_g = sb("bada_g", [1, 2 * D], BF16)
        self.ones1 = sb("ones1", [1, 128], BF16)
        self.gate = sb("gate", [128, 2, D], F32)
        self.KTf = sb("KT", [128, 8 * SEQ], BF16)
        self.KT = Tile("KTv", self.KTf.ap.rearrange("p (h t) -> p h t", h=8))
        self.KT = self.KTf if False else self.KT
        self.stH = sb("stH", [128, 4, 4, 15], F32)
        self.chs = sb("chs", [128, NFC, 4, 2], F32)
        self.VEf = sb("VE", [128, 16 * 8 * 65], BF16)
        self.VE = Tile("VEv", self.VEf.ap.rearrange("p (a h d) -> p a h d", a=16, h=8))
        self.kmT = sb("kmT", [64, 8, 8], BF16)
        self.ring = [sb("ring%d" % i, [128, 4096], BF16) for i in range(RING)]
        self.xs0f = sb("xs0", [128, 2 * D], F32)
        self.xs1f = sb("xs1", [128, 2 * D], F32)
        self.xs = [Tile("xsv%d" % i, f.ap.rearrange("p (t d) -> p t d", t=2)) for i, f in enumerate((self.xs0f, self.xs1f))]
        self.xn = sb("xn", [128, 2, D], BF16)
        self.ss = sb("ss", [128, 2], F32)
        self.rstd = sb("rstd", [128, 2], F32)
        self.hT = sb("hT", [128, 8, G], BF16)
        self.uT = sb("uT", [128, 4, 16 + G], F32)
        self.ptmp = [sb("ptmp%d" % i, [128, 16 + G], F32) for i in range(2)]
        self.pooled = sb("pooled", [128, 4, G], BF16)
        self.ypool = sb("ypool", [128, 4, G], BF16)
        self.th = [sb("th%d" % i, [128, G], F32) for i in range(2)]
        self.m = sb("m", [128, 8, G], BF16)
        self.thb = sb("thb", [128, 8, G], BF16)
        self.vout = sb("vout", [128, 512], F32)
        self.kout = sb("kout", [128, 512], F32)
        self.sq = sb("sq", [128, 512], F32)
        self.ntmp = sb("ntmp", [128, 512], F32)
        self.ssq = sb("ssq", [128, 8], F32)
        self.rsq = sb("rsq", [128, 8], F32)
        self.kbf = sb("kbf", [128, 512], BF16)
        self.qext = sb("qext", [128, 8, 72], BF16)
        self.qT64 = sb("qT64", [64, 8, 128], BF16)
        self.bsc = sb("bsc", [128, 8, 8], F32)
        self.cmp = sb("cmp", [128, 8, 8, 8], F32)
        self.rank = sb("rank", [128, 8, 8], F32)
        self.QT = sb("QT", [72, 8, G], BF16)
        self.PT = [sb("PT%d" % i, [128, 2, G], BF16) for i in range(4)]
        self.rden = sb("rden", [128, 4], F32)
        self.obf = sb("obf", [128, 512], BF16)
        self.oT = sb("oT", [128, 4, G], BF16)
        self.mb = [sb("mb%d" % i, [128, G], F32) for i in range(2)]
        self.merged = sb("merged", [128, 8, G], BF16)
        self.rtmp = [sb("rtmp%d" % i, [128, 512], F32) for i in range(2)]
        self.gT = [sb("gT%d" % i, [128, 2 + G], F32) for i in range(2)]
        self.ct = [sb("ct%d" % i, [128, G], F32) for i in range(4)]
        self.actT = sb("actT", [128, NFC, G], BF16)
        self.chist = sb("chist", [128, NFC, 2], F32)
        self.banks = [C.psum("bank%d" % i, [128, 512], F32) for i in range(8)]
        self.held = set()
        self.bank_i = 0
        self.ring_i = 0
        self.ring_emit = 0
        self.plan = []

    def bank(self, hold=False):
        for _ in range(16):
            b = self.bank_i % 8
            self.bank_i += 1
            if b not in self.held:
                if hold:
                    self.held.add(b)
                return self.banks[b]
        raise RuntimeError("no free psum bank")

    def release(self, bank):
        self.held.discard(self.banks.index(bank))

    def plan_group(self, ctx):
        p = []
        p.append(("w_in", "k8", 0, 512))
        p.append(("w_bp", "k4", 0, 1024))
        p.append(("w_in", "k8", 2048, 512))
        p.append(("w_in", "k8", 2560, 512))
        p.append(("w_in", "k8", 3072, 512))
        p.append(("w_in", "k8", 3584, 512))
        p.append(("w_in", "k8", 512, 512))
        p.append(("w_in", "k8", 1024, 512))
        p.append(("w_in", "k8", 1536, 512))
        p.append(("w_ba", "k4", 0, 1024))
        p.append(("w_out", "k8", 0, 512))
        p.append(("w_out", "k8", 512, 512))
        for j in range(6):
            n = 512 if j < 5 else 256
            p.append(("w_up", "k8", 512 * j, n))
            p.append(("w_up", "k8", DFF + 512 * j, n))
        for j in range(6):
            p.append(("w_down", "d4", 4 * j, 4 if j < 5 else 2))
        return p

    def plan_gate(self):
        return [("w_adag", "k8", 512 * j, 512) for j in range(4)]

    def ring_fetch(self, base):
        C = self.C
        while self.ring_emit < min(len(self.plan), base + RING):
            name, kind, a, b = self.plan[self.ring_emit]
            slot = self.ring[self.ring_emit % RING]
            ap, wtile = self.wscr[name]
            rtiles = [wtile]
            q = "sp"
            if self.mode == "C":
                q = "pool"
                rtiles = []
                if name == "w_adag":
                    ap = self.din["w_ada"][:, 2 * D:3 * D] if a < D else self.din["w_ada"][:, 5 * D:6 * D]
                    a = a % D
                else:
                    ap = self.din[name]
            if kind == "k8":
                src = ap[:, a:a + b].rearrange("(kc p) n -> p kc n", p=128)
                dst = slot[:, 0:8 * b].rearrange("p (kc n) -> p kc n", kc=8)
                reads = rtiles
            elif kind == "k4":
                src = ap[:, a:a + b].rearrange("(kc p) n -> p kc n", p=128)
                dst = slot[:, 0:4 * b].rearrange("p (kc n) -> p kc n", kc=4)
                reads = rtiles
            else:
                src = ap[a * 128:(a + b) * 128, :].rearrange("(j p) n -> p j n", p=128)
                dst = slot[:, 0:b * 1024].rearrange("p (j n) -> p j n", j=b)
                reads = rtiles
            C.dma(q, dst, src, extra_reads=[t.v() for t in reads])
            self.ring_emit += 1

    def wget(self, expect, keep=0):
        assert self.plan[self.ring_i] == expect, (self.plan[self.ring_i], expect)
        self.ring_fetch(self.ring_i - keep)
        slot = self.ring[self.ring_i % RING]
        self.ring_i += 1
        return slot

    def setup(self):
        C, nc, din = self.C, self.nc, self.din
        C.dma("pool", self.ident[:], din["c_ident"])
        C.dma("pool", self.tri[:], din["c_tri"])
        C.dma("sp", self.invcnt[:].rearrange("p g t -> p (g t)"), din["c_invcnt"])
        C.dma("sp", self.gq[:], din["g_q"].partition_broadcast(128))
        C.dma("sp", self.gk[:], din["g_k"].partition_broadcast(128))
        C.dma("pool", self.wpool[:], din["w_pool"].rearrange("(g c) d -> c g d", c=128))
        for h in range(8):
            C.dma("pool", self.KT[64:72, h, :], din["c_kext"])
        with nc.allow_non_contiguous_dma(reason="tiny one-time transposed parameter loads"):
            C.dma("sp", self.pscaleT[:], din["pool_scale"].rearrange("o (g c) -> c (o g)", c=128))
            C.dma("sp", self.bconvT[:], din["b_conv"].rearrange("o (j c) -> c (o j)", c=128))
            for t in range(3):
                C.dma("sp", self.wconvT[:, :, t], din["w_conv"][t:t + 1, :].rearrange("o (j c) -> c (o j)", c=128))
            C.dma("sp", self.gmixT[:], din["g_mix"].rearrange("o (j c) -> c (o j)", c=128))
            C.dma("sp", self.gffnT[:], din["g_ffn"].rearrange("o (j c) -> c (o j)", c=128))
            C.dma("sp", self.badaT[:], din["b_ada"].rearrange("o (j c) -> c (o j)", c=128))
            for b in range(6):
                C.dma("sp", self.cT[:, :, b], din["c_all"][b:b + 1, :].rearrange("o (j c) -> c (o j)", c=128))
            for bb in range(NSB):
                for g in range(4):
                    C.dma("sp", self.stH[:, g, bb, :],
                          din["st_pool"][bb * 15:(bb + 1) * 15, g * 128:(g + 1) * 128].rearrange("t c -> c t"))
                for t in range(2):
                    C.dma("sp", self.chs[:, :, bb, t],
                          din["st_conv"][bb * 2 + t:bb * 2 + t + 1, :].rearrange("o (j c) -> c (o j)", c=128))
        C.dma("pool", self.bada_g[:, 0:D], din["b_ada"][:, 2 * D:3 * D])
        C.dma("pool", self.bada_g[:, D:2 * D], din["b_ada"][:, 5 * D:6 * D])
        C.memset("dve", self.ones1[:], 1.0)
        C.memset("dve", self.VE[:, :, :, 64:65], 1.0)
        C.ts("dve", self.gq[:], self.gq[:], 0.125, None, ALU.mult)
        th = self.ct[0][:, 0:48].rearrange("p (j b) -> p j b", j=8)
        C.act(th, self.cT[:], AF.Tanh, scale=0.5)
        C.ts("dve", th, th, 0.5, 0.5, ALU.mult, ALU.add)
        C.tt("dve", self.siluT[:], th, self.cT[:], ALU.mult)
        srcs = {"w_in": din["w_in"], "w_bp": din["w_bp"], "w_ba": din["w_ba"], "w_out": din["w_out"],
                "w_up": din["w_up"], "w_down": din["w_down"]}
        t = self.wscr["w_adag"][1]
        for r in range(8 if self.mode == "A" else 0):
            C.dma("pool", t[r * 128:(r + 1) * 128, 0:D], din["w_ada"][r * 128:(r + 1) * 128, 2 * D:3 * D], no_waw=True)
            C.dma("pool", t[r * 128:(r + 1) * 128, D:2 * D], din["w_ada"][r * 128:(r + 1) * 128, 5 * D:6 * D], no_waw=True)
        for j12 in range(12):
            slot = self.ring[j12 % RING]
            C.dma("pool", slot[:].rearrange("p (kc n) -> p kc n", kc=8),
                  din["w_ada"][:, j12 * 512:(j12 + 1) * 512].rearrange("(kc p) n -> p kc n", p=128))
            wv = slot[:].rearrange("p (kc n) -> p kc n", kc=8)
            bk = self.bank()
            for jj in range(4):
                j = j12 * 4 + jj
                C.mm(bk[:, jj * 8:jj * 8 + 6],
                     [(wv[:, kc, jj * 128:(jj + 1) * 128], self.siluT[:, kc, :]) for kc in range(8)])
            C.tt("dve", self.modT[:, j12 * 4:(j12 + 1) * 4, :],
                 bk[:, 0:32].rearrange("p (j b) -> p j b", j=4)[:, :, 0:6],
                 self.badaT[:, j12 * 4:(j12 + 1) * 4].unsq(2).bcast([128, 4, 6]), ALU.add)
        for name in (("w_in", "w_bp", "w_ba", "w_out", "w_up", "w_down") if self.mode == "A" else ()):
            t = self.wscr[name][1]
            for r in range(srcs[name].shape[0] // 128):
                C.dma("pool", t[r * 128:(r + 1) * 128, :], srcs[name][r * 128:(r + 1) * 128, :], no_waw=True)
        for s, (gT, mi) in enumerate(((self.gmixT, 1), (self.gffnT, 4))):
            C.ts("dve", self.scl[:, s], self.modT[:, mi * 8:(mi + 1) * 8, :], 1.0, None, ALU.add)
            C.tt("dve", self.scl[:, s], self.scl[:, s], gT[:].unsq(2).bcast([128, 8, 6]), ALU.mult)

    def shiftT(self, s, c, b):
        mi = 0 if s == 0 else 3
        return self.modT[:, mi * 8 + c, b:b + 1]

    def load_gate(self, cols):
        C = self.C
        runs = []
        st = 0
        for i in range(1, len(cols) + 1):
            if i == len(cols) or cols[i] != cols[st]:
                runs.append((st, i, cols[st]))
                st = i
        for (a, b, cb) in runs:
            C.copy("dve", self.silurep[:, :, a:b], self.siluT[:, :, cb:cb + 1].bcast([128, 8, b - a]))
        n = len(cols)
        for j in range(4):
            slot = self.wget(("w_adag", "k8", 512 * j, 512))
            wv = slot[:].rearrange("p (kc n) -> p kc n", kc=8)
            bk = self.bank()
            C.mm(bk[0:n, :], [(self.silurep[:, kc, 0:n], wv[:, kc, :]) for kc in range(8)]
                 + [(self.ones1[:, 0:n], self.bada_g[:, j * 512:(j + 1) * 512])])
            C.ts("dve", self.gate[0:n, j // 2, (j % 2) * 512:(j % 2 + 1) * 512], bk[0:n, :], 0.5, None, ALU.mult)

    def norm_to_hT(self, xs, s, ntile, rows, mod_b=None, mod_tiles=None):
        C = self.C
        for t in range(ntile):
            C.act(self.xn[0:rows, t, :], xs[0:rows, t, :], AF.Square, accum_out=self.ss[0:rows, t:t + 1])
        C.act(self.rstd[0:rows, 0:ntile], self.ss[0:rows, 0:ntile], AF.Sqrt, scale=1.0 / D, bias=EPS)
        C.recip(self.rstd[0:rows, 0:ntile], self.rstd[0:rows, 0:ntile])
        for t in range(ntile):
            C.ts("dve", self.xn[0:rows, t, :], xs[0:rows, t, :], self.rstd[0:rows, t:t + 1], None, ALU.mult)
        for half in range(2):
            bk = self.bank()
            bv = bk[:].bitcast(BF16).rearrange("p (c t n) -> p c t n", c=4, t=2)
            items = []
            for cc in range(4):
                c = half * 4 + cc
                for t in range(ntile):
                    items.append((bv[:, cc, t, 0:rows], self.xn[0:rows, t, c * 128:(c + 1) * 128]))
            C.transposes(items, self.ident)
            for cc in range(4):
                c = half * 4 + cc
                if mod_b is not None:
                    C.act(self.hT[:, c, :].rearrange("p (t n) -> p t n", t=ntile),
                          bv[:, cc, 0:ntile, :], AF.Identity,
                          bias=self.shiftT(s, c, mod_b), scale=self.scl[:, s, c, mod_b:mod_b + 1])
                else:
                    sc_t, sh_t = mod_tiles
                    C.tt("dve", self.ct[0][:, 0:rows], bv[:, cc, 0, 0:rows], sc_t[:, s, c, :], ALU.mult)
                    C.tt("dve", self.hT[:, c, 0:rows], self.ct[0][:, 0:rows], sh_t[:, s, c, :], ALU.add)

    def mixer_front(self, n, first, sample=False):
        C = self.C
        wu = self.wget(("w_in", "k8", 0, 512))[:].rearrange("p (kc n) -> p kc n", kc=8)
        for g in range(4):
            bk = self.bank()
            C.mm(bk[:, 0:n], [(wu[:, kc, g * 128:(g + 1) * 128], self.hT[:, kc, 0:n]) for kc in range(8)])
            if sample:
                C.copy("act", self.uTs[:, g, :, 15:19], bk[:, 0:n].rearrange("p (b t) -> p b t", b=4))
            else:
                C.copy("act", self.uT[:, g, 16:16 + n], bk[:, 0:n])
        L = 16 + n
        for g in range(4):
            if sample:
                cur = self.uTs[:, g]
                for l in range(1, g + 2):
                    sh = 1 << (l - 1)
                    lo = (1 << l) - 1
                    dst = self.ptmp[(l - 1) % 2][:, 0:76].rearrange("p (b t) -> p b t", b=4)
                    C.tt("pool", dst[:, :, lo:19], cur[:, :, lo:19], cur[:, :, lo - sh:19 - sh], ALU.add)
                    cur = dst
                C.ts("pool", cur[:, :, 15:19], cur[:, :, 15:19], 1.0 / (2 << g), None, ALU.mult)
                C.tt("pool", self.pooled[:, g, 0:n].rearrange("p (b t) -> p b t", b=4), cur[:, :, 15:19],
                     self.uTs[:, g, :, 15:19], ALU.subtract)
                bk = self.bank()
                C.mm(bk[:, 0:n], [(self.wpool[:, g, :], self.pooled[:, g, 0:n])])
                C.act(self.ypool[:, g, 0:n], bk[:, 0:n], AF.Copy, scale=self.pscaleT[:, g:g + 1])
                continue
            cur = self.uT[:, g, :]
            for l in range(1, g + 2):
                sh = 1 << (l - 1)
                lo = (1 << l) - 1
                dst = self.ptmp[(l - 1) % 2]
                C.tt("dve", dst[:, lo:L], cur[:, lo:L], cur[:, lo - sh:L - sh], ALU.add)
                cur = dst[:]
            w = 2 << g
            if first:
                C.tt("dve", cur[:, 16:32], cur[:, 16:32], self.invcnt[:, g, :], ALU.mult)
                C.ts("dve", cur[:, 32:L], cur[:, 32:L], 1.0 / w, None, ALU.mult)
                C.tt("dve", self.pooled[:, g, 0:n], cur[:, 16:L], self.uT[:, g, 16:L], ALU.subtract)
            else:
                C.ts("dve", cur[:, 16:L], cur[:, 16:L], 1.0 / w, None, ALU.mult)
                C.tt("dve", self.pooled[:, g, 0:n], cur[:, 16:L], self.uT[:, g, 16:L], ALU.subtract)
            bk = self.bank()
            C.mm(bk[:, 0:n], [(self.wpool[:, g, :], self.pooled[:, g, 0:n])])
            C.act(self.ypool[:, g, 0:n], bk[:, 0:n], AF.Copy, scale=self.pscaleT[:, g:g + 1])
        wbp = self.wget(("w_bp", "k4", 0, 1024))[:].rearrange("p (kc n) -> p kc n", kc=4)
        for half in range(2):
            wga = self.wget(("w_in", "k8", 2048 + 512 * half, 512), keep=1 + half)[:].rearrange("p (kc n) -> p kc n", kc=8)
            for cc in range(4):
                c = half * 4 + cc
                bk = self.bank()
                C.mm(bk[:, 0:n], [(wga[:, kc, cc * 128:(cc + 1) * 128], self.hT[:, kc, 0:n]) for kc in range(8)])
                th = self.th[c % 2]
                C.act(th[:, 0:n], bk[:, 0:n], AF.Tanh, scale=0.5)
                bk2 = self.bank()
                C.mm(bk2[:, 0:n], [(wbp[:, kc, c * 128:(c + 1) * 128], self.ypool[:, kc, 0:n]) for kc in range(4)])
                C.stt("dve", self.m[:, c, 0:n], th[:, 0:n], 1.0, bk2[:, 0:n], ALU.add, ALU.mult)
        for half in range(2):
            wgb = self.wget(("w_in", "k8", 3072 + 512 * half, 512))[:].rearrange("p (kc n) -> p kc n", kc=8)
            for cc in range(4):
                c = half * 4 + cc
                bk = self.bank()
                C.mm(bk[:, 0:n], [(wgb[:, kc, cc * 128:(cc + 1) * 128], self.hT[:, kc, 0:n]) for kc in range(8)])
                C.act(self.thb[:, c, 0:n], bk[:, 0:n], AF.Tanh, scale=0.5)

    def qkv_tile(self, t, rows, wq, wk, wv, k_dst, v_dst, v_bf_dst, q_dst=None):
        C = self.C
        bq, bk_, bv = self.bank(), self.bank(), self.bank()
        lhs = [self.hT[:, kc, t * 128:t * 128 + rows] for kc in range(8)]
        C.mm(bq[0:rows, :], [(lhs[kc], wq[:, kc, :]) for kc in range(8)])
        C.mm(bk_[0:rows, :], [(lhs[kc], wk[:, kc, :]) for kc in range(8)])
        C.mm(bv[0:rows, :], [(lhs[kc], wv[:, kc, :]) for kc in range(8)])
        C.copy("act", self.vout[0:rows, :], bv[0:rows, :])
        C.dma("pool", v_dst, self.vout[0:rows, :], is_output=True)
        C.copy("dve", v_bf_dst, self.vout[0:rows, :].rearrange("p (h d) -> p h d", h=8))
        for (bank, gain, which) in ((bk_, self.gk, "k"), (bq, self.gq, "q")):
            C.act(self.sq[0:rows, :], bank[0:rows, :], AF.Square)
            C.reduce("dve", self.ssq[0:rows, :], self.sq[0:rows, :].rearrange("p (h d) -> p h d", h=8), ALU.add)
            C.act(self.rsq[0:rows, :], self.ssq[0:rows, :], AF.Sqrt, scale=1.0 / 64, bias=EPS)
            C.recip(self.rsq[0:rows, :], self.rsq[0:rows, :])
            C.tt("dve", self.ntmp[0:rows, :].rearrange("p (h d) -> p h d", h=8),
                 bank[0:rows, :].rearrange("p (h d) -> p h d", h=8),
                 self.rsq[0:rows, :].unsq(2).bcast([rows, 8, 64]), ALU.mult)
            if which == "k":
                C.tt("dve", self.kout[0:rows, :], self.ntmp[0:rows, :], gain[0:rows, :], ALU.mult)
                C.dma("pool", k_dst, self.kout[0:rows, :], is_output=True)
                C.copy("act", self.kbf[0:rows, :], self.kout[0:rows, :])
            else:
                C.tt("dve", self.qext[0:rows, :, 0:64] if q_dst is None else q_dst,
                     self.ntmp[0:rows, :].rearrange("p (h d) -> p h d", h=8),
                     gain[0:rows, :].rearrange("p (h d) -> p h d", h=8), ALU.mult)

    def mixer_back(self, n, ntile, rows, xs):
        C = self.C
        wba = self.wget(("w_ba", "k4", 0, 1024))[:].rearrange("p (kc n) -> p kc n", kc=4)
        for c in range(8):
            bk = self.bank()
            C.mm(bk[:, 0:n], [(wba[:, kc, c * 128:(c + 1) * 128], self.oT[:, kc, 0:n]) for kc in range(4)])
            mb = self.mb[c % 2]
            C.stt("dve", mb[:, 0:n], self.thb[:, c, 0:n], 1.0, bk[:, 0:n], ALU.add, ALU.mult)
            C.tt("dve", self.merged[:, c, 0:n], mb[:, 0:n], self.m[:, c, 0:n], ALU.add)
        for hf in range(2):
            wo = self.wget(("w_out", "k8", 512 * hf, 512))[:].rearrange("p (kc n) -> p kc n", kc=8)
            for t in range(ntile):
                bk = self.bank()
                C.mm(bk[0:rows, :], [(self.merged[:, kc, t * 128:t * 128 + rows], wo[:, kc, :]) for kc in range(8)])
                rt = self.rtmp[(hf * 2 + t) % 2]
                C.tt("dve", rt[0:rows, :], bk[0:rows, :], self.gate[0:rows, 0, hf * 512:(hf + 1) * 512], ALU.mult)
                C.tt("dve", xs[0:rows, t, hf * 512:(hf + 1) * 512], xs[0:rows, t, hf * 512:(hf + 1) * 512], rt[0:rows, :], ALU.add)

    def ffn(self, n, ntile, rows, xs, hist_cols=None):
        C = self.C
        pend = None

        def stage2(j, bk, t1, t2):
            C.act(t2[:, 0:n], t1[:, 0:n], AF.Tanh, scale=0.5)
            C.stt("dve", t2[:, 0:n], t2[:, 0:n], 1.0, t1[:, 0:n], ALU.add, ALU.mult)
            C.tt("dve", self.actT[:, j, 0:n], t2[:, 0:n], bk[:, 256:256 + n], ALU.mult)

        for jj in range(6):
            ncols = 512 if jj < 5 else 256
            wg = self.wget(("w_up", "k8", 512 * jj, ncols))[:, 0:8 * ncols].rearrange("p (kc n) -> p kc n", kc=8)
            wv = self.wget(("w_up", "k8", DFF + 512 * jj, ncols), keep=1)[:, 0:8 * ncols].rearrange("p (kc n) -> p kc n", kc=8)
            for cc in range(ncols // 128):
                j = jj * 4 + cc
                bk = self.bank()
                C.mm(bk[:, 0:n], [(wg[:, kc, cc * 128:(cc + 1) * 128], self.hT[:, kc, 0:n]) for kc in range(8)])
                C.mm(bk[:, 256:256 + n], [(wv[:, kc, cc * 128:(cc + 1) * 128], self.hT[:, kc, 0:n]) for kc in range(8)])
                gT = self.gT[j % 2]
                t1, t2 = self.ct[(j % 2) * 2], self.ct[(j % 2) * 2 + 1]
                if hist_cols is None:
                    C.copy("pool", gT[:, 0:2], self.chist[:, j, :])
                    C.copy("act", gT[:, 2:2 + n], bk[:, 0:n])
                    C.copy("pool", self.chist[:, j, :], gT[:, n:n + 2])
                    C.act(t1[:, 0:n], gT[:, 0:n], AF.Identity, bias=self.bconvT[:, j:j + 1], scale=self.wconvT[:, j, 0:1])
                    C.stt("dve", t2[:, 0:n], gT[:, 1:1 + n], self.wconvT[:, j, 1:2], t1[:, 0:n], ALU.mult, ALU.add)
                    C.stt("dve", t1[:, 0:n], gT[:, 2:2 + n], self.wconvT[:, j, 2:3], t2[:, 0:n], ALU.mult, ALU.add)
                else:
                    self.sample_conv(j, bk, t1, t2)
                if pend is not None:
                    stage2(*pend)
                pend = (j, bk, t1, t2)
        stage2(*pend)
        banks = [[self.bank(hold=True) for t in range(ntile)] for hf in range(2)]
        for jj in range(6):
            nj = 4 if jj < 5 else 2
            wd = self.wget(("w_down", "d4", 4 * jj, nj))[:, 0:nj * 1024].rearrange("p (j n) -> p j n", j=nj)
            for hf in range(2):
                for t in range(ntile):
                    C.mm(banks[hf][t][0:rows, :],
                         [(self.actT[:, jj * 4 + q, t * 128:t * 128 + rows], wd[:, q, hf * 512:(hf + 1) * 512]) for q in range(nj)],
                         start=(jj == 0), stop=(jj == 5))
        for hf in range(2):
            for t in range(ntile):
                bk = banks[hf][t]
                rt = self.rtmp[(hf * 2 + t) % 2]
                C.tt("dve", rt[0:rows, :], bk[0:rows, :], self.gate[0:rows, 1, hf * 512:(hf + 1) * 512], ALU.mult)
                C.tt("dve", xs[0:rows, t, hf * 512:(hf + 1) * 512], xs[0:rows, t, hf * 512:(hf + 1) * 512], rt[0:rows, :], ALU.add)
                self.release(bk)

    def prompt_seq(self, seq):
        C, nc, din, dout = self.C, self.nc, self.din, self.dout
        self.plan += self.plan_gate()
        for qb in range(NG):
            self.plan += self.plan_group(None)
        self.load_gate([seq] * 128)
        C.memset("pool", self.uT[:, :, 0:16], 0.0)
        C.memset("pool", self.chist[:], 0.0)
        for qb in range(NG):
            self.prompt_group(seq, qb)

    def load_x(self, seq, qb):
        r0 = seq * SEQ + qb * G
        xs = self.xs[(seq * NG + qb) % 2]
        self.C.dma("sp", xs[:], self.din["xp"][r0:r0 + G, :].rearrange("(t p) d -> p t d", p=128))
        return xs

    def prompt_group(self, seq, qb):
        C, din, dout = self.C, self.din, self.dout
        r0 = seq * SEQ + qb * G
        if seq == 0 and qb == 0:
            xs = self.load_x(seq, qb)
        else:
            xs = self.xs[(seq * NG + qb) % 2]
        nxt = (seq, qb + 1) if qb + 1 < NG else ((seq + 1, 0) if seq + 1 < NPB else None)
        if nxt is not None:
            self.load_x(*nxt)
        self.norm_to_hT(xs, 0, 2, 128, mod_b=seq)
        self.mixer_front(G, first=(qb == 0))
        if qb == NG - 1:
            with self.nc.allow_non_contiguous_dma(reason="15-row pooling state, transposed store"):
                for g in range(4):
                    C.dma("pool", dout["pool_p"][seq * 15:(seq + 1) * 15, g * 128:(g + 1) * 128].rearrange("t c -> c t"),
                          self.uT[:, g, G + 1:G + 16], is_output=True)
        C.copy("pool", self.uT[:, :, 0:16], self.uT[:, :, G:G + 16])
        wq = self.wget(("w_in", "k8", 512, 512))[:].rearrange("p (kc n) -> p kc n", kc=8)
        wk = self.wget(("w_in", "k8", 1024, 512), keep=1)[:].rearrange("p (kc n) -> p kc n", kc=8)
        wv = self.wget(("w_in", "k8", 1536, 512), keep=2)[:].rearrange("p (kc n) -> p kc n", kc=8)
        for t in range(2):
            kt = 2 * qb + t
            self.qkv_tile(t, 128, wq, wk, wv,
                          dout["k_p"][r0 + t * 128:r0 + (t + 1) * 128, :],
                          dout["v_p"][r0 + t * 128:r0 + (t + 1) * 128, :],
                          self.VE[:, kt, :, 0:64])
            bk = self.bank()
            bv = bk[:].bitcast(BF16).rearrange("p (h n) -> p h n", h=8)
            C.transposes([(bv[0:64, h, :], self.kbf[:, h * 64:(h + 1) * 64]) for h in range(8)], self.ident)
            C.copy("dve", self.KT[0:64, :, kt * 128:(kt + 1) * 128], bv[0:64, :, :])
            C.memset("pool", self.qext[:, :, 64:72], 0.0)
            if qb >= 4:
                bk = self.bank()
                bv = bk[:].bitcast(BF16).rearrange("p (h n) -> p h n", h=8)
                C.transposes([(bv[0:64, h, :], self.qext[:, h, 0:64]) for h in range(8)], self.ident)
                C.copy("act", self.qT64[:], bv[0:64, :, :])
                bk2 = self.bank()
                sv = bk2[:, 0:64].rearrange("p (h b) -> p h b", h=8)
                for h in range(8):
                    C.mm(sv[:, h, 0:qb], [(self.qT64[:, h, :], self.kmT[:, h, 0:qb])])
                C.copy("dve", self.bsc[:, :, 0:qb], sv[:, :, 0:qb])
                S = self.bsc[:, :, 0:qb]
                C.tt("dve", self.cmp[:, :, 0:qb, 0:qb], S.unsq(2).bcast([128, 8, qb, qb]),
                     S.unsq(3).bcast([128, 8, qb, qb]), ALU.is_gt)
                C.reduce("dve", self.rank[:, :, 0:qb], self.cmp[:, :, 0:qb, 0:qb], ALU.add)
                C.ts("dve", self.qext[:, :, 64:64 + qb], self.rank[:, :, 0:qb], 2.5, NEG, ALU.is_ge, ALU.mult)
            bk = self.bank()
            bv = bk[:].bitcast(BF16).rearrange("p (h n) -> p h n", h=8)
            C.transposes([(bv[0:72, h, :], self.qext[:, h, :]) for h in range(8)], self.ident)
            C.copy("act", self.QT[:, :, t * 128:(t + 1) * 128], bv[0:72, :, :])
        if qb < NG - 1:
            C.reduce("dve", self.ntmp[0:64, 0:8], self.KT[0:64, :, qb * G:(qb + 1) * G], ALU.add)
            C.copy("dve", self.kmT[:, :, qb], self.ntmp[0:64, 0:8])
        self.attention(qb)
        self.mixer_back(G, 2, 128, xs)
        self.norm_to_hT(xs, 1, 2, 128, mod_b=seq)
        self.ffn(G, 2, 128, xs)
        C.dma("pool", dout["y_p"][r0:r0 + G, :].rearrange("(t p) d -> p t d", p=128), xs[:], is_output=True)
        if qb == NG - 1:
            with self.nc.allow_non_contiguous_dma(reason="2-row conv state, transposed store"):
                for t in range(2):
                    C.dma("pool", dout["conv_p"][seq * 2 + t:seq * 2 + t + 1, :].rearrange("o (j c) -> c (o j)", c=128),
                          self.chist[:, :, t], is_output=True)

    def attention(self, qb):
        C = self.C
        nkt = 2 * qb + 2
        obanks = [[self.bank(hold=True) for hg in range(2)] for qt in range(2)]
        work = [(h, a, min(a + 2, nkt)) for h in range(8) for a in range(0, nkt, 2)]
        pend_q = []
        for item in work + [None, None]:
            cur = None
            if item is not None:
                h, a, b = item
                bk = self.bank()
                pt = self.PT[(h * 8 + a // 2) % 4]
                for kt in range(a, b):
                    col = (kt - a) * G
                    kview = self.KT[0:72, h, kt * 128:(kt + 1) * 128]
                    if kt < 2 * qb:
                        C.mm(bk[:, col:col + G], [(kview, self.QT[:, h, :])])
                    elif kt == 2 * qb:
                        C.mm(bk[:, col:col + 128], [(kview, self.QT[:, h, 0:128]), (self.ident[:], self.tri[:])])
                        C.mm(bk[:, col + 128:col + 256], [(kview, self.QT[:, h, 128:256])])
                    else:
                        C.mm(bk[:, col + 128:col + 256], [(kview, self.QT[:, h, 128:256]), (self.ident[:], self.tri[:])])
                cur = (h, a, b, bk, pt)
            pend_q.append(cur)
            pend = pend_q.pop(0) if len(pend_q) > 2 else None
            if pend is not None:
                ph, pa, pb, pbk, ppt = pend
                if pb - 1 == 2 * qb + 1:
                    C.act(ppt[:, 0, :], pbk[:, 0:G], AF.Exp)
                    C.act(ppt[:, 1, 128:256], pbk[:, G + 128:G + 256], AF.Exp)
                else:
                    C.act(ppt[:].rearrange("p a n -> p (a n)"), pbk[:, 0:2 * G], AF.Exp)
                for kt in range(pa, pb):
                    for qt in range(2):
                        if kt > 2 * qb + qt:
                            continue
                        ob = obanks[qt][ph // 4]
                        C.mm(ob[:, (ph % 4) * 65:(ph % 4) * 65 + 65],
                             [(ppt[:, kt - pa, qt * 128:(qt + 1) * 128], self.VE[:, kt, ph, :])],
                             start=(kt == 0), stop=(kt == 2 * qb + qt))
        for qt in range(2):
            for hg in range(2):
                ob = obanks[qt][hg]
                ov = ob[:, 0:260].rearrange("p (h d) -> p h d", h=4)
                C.recip(self.rden[:, :], ov[:, :, 64])
                C.tt("dve", self.obf[:, hg * 256:(hg + 1) * 256].rearrange("p (h d) -> p h d", h=4),
                     ov[:, :, 0:64], self.rden[:, :].unsq(2).bcast([128, 4, 64]), ALU.mult)
                self.release(ob)
            bk = self.bank()
            bv = bk[:].bitcast(BF16).rearrange("p (c n) -> p c n", c=8)
            C.transposes([(bv[:, c, :], self.obf[:, c * 128:(c + 1) * 128]) for c in range(4)], self.ident)
            C.copy("act", self.oT[:, :, qt * 128:(qt + 1) * 128], bv[:, 0:4, :])

    def sample_conv(self, j, bk, t1, t2):
        C = self.C
        gTs = self.gT[j % 2][:, 0:24].rearrange("p (b t) -> p b t", b=4)
        v4 = lambda tl: tl[:, 0:16].rearrange("p (b t) -> p b t", b=4)
        C.copy("pool", gTs[:, :, 0:2], self.chs[:, j, :, :])
        C.copy("act", gTs[:, :, 2:6], bk[:, 0:16].rearrange("p (b t) -> p b t", b=4))
        C.copy("pool", self.cso[:, j, :, :], gTs[:, :, 4:6])
        C.ts("dve", v4(t1), gTs[:, :, 0:4], self.wconvT[:, j, 0:1], self.bconvT[:, j:j + 1], ALU.mult, ALU.add)
        C.stt("dve", v4(t2), gTs[:, :, 1:5], self.wconvT[:, j, 1:2], v4(t1), ALU.mult, ALU.add)
        C.stt("dve", v4(t1), gTs[:, :, 2:6], self.wconvT[:, j, 2:3], v4(t2), ALU.mult, ALU.add)

    def carve(self):
        ktf = self.KTf.ap
        self.S_all = Tile("S_all", ktf[:, 0:8256].bitcast(F32).rearrange("p (a n) -> p a n", n=32))
        self.PTs = Tile("PTs", ktf[:, 8256:12384].rearrange("p (a n) -> p a n", n=32))
        self.Dexp = Tile("Dexp", ktf[:, 12384:14432].rearrange("p (k n) -> p k n", n=32))
        self.IDX = Tile("IDX", ktf[:, 14432:15456].bitcast(I32))
        vef = self.VEf.ap
        self.kpg = [Tile("kpg%d" % i, vef[:, i * 512:(i + 1) * 512]) for i in range(6)]
        self.kTp = [Tile("kTp%d" % i, vef[:, 3072 + i * 512:3072 + (i + 1) * 512]) for i in range(3)]
        self.ptI = Tile("ptI", vef[:, 4608:5632].bitcast(I32))
        self.ptF = Tile("ptF", vef[:, 5632:6656].bitcast(F32))
        self.sc_t = Tile("sc_t", vef[:, 6656:7168].bitcast(F32).rearrange("p (s c t) -> p s c t", s=2, c=8))
        self.sh_t = Tile("sh_t", vef[:, 7168:7680].bitcast(F32).rearrange("p (s c t) -> p s c t", s=2, c=8))
        self.Qb = Tile("Qb", vef[:, 7680:8192].rearrange("p (b c n) -> p b c n", b=4, c=4))
        xf = self.xs1f.ap
        pos = [0]

        def f32(name, n, shape=None):
            ap = xf[:, pos[0]:pos[0] + n]
            pos[0] += n
            return Tile(name, ap)

        def bf(name, n):
            ap = xf[:, pos[0]:pos[0] + n // 2].bitcast(BF16)
            pos[0] += n // 2
            return Tile(name, ap)

        self.cso = Tile("cso", f32("cso_", 176).ap.rearrange("p (j b t) -> p j b t", j=NFC, b=4))
        self.qs_bf = bf("qs_bf", 512)
        self.vsb = bf("vsb", 512)
        self.qTs = Tile("qTs", bf("qTs_", 64).ap.rearrange("p (c t) -> p c t", c=4))
        self.kmTs = Tile("kmTs", bf("kmTs_", 256).ap.rearrange("p (c k) -> p c k", c=4))
        self.sc_sb = f32("sc_sb", 64)
        self.top8 = f32("top8", 8)
        self.negb = f32("negb", 64)
        self.omask = bf("omask", 512)
        self.rden_s = f32("rden_s", 2)
        self.ones_col = bf("ones_col", 4)
        self.selb = Tile("selb", bf("selb_", 512).ap.rearrange("p (b k) -> p b k", b=4))
        self.caus = f32("caus", 32)
        self.hmask = f32("hmask", 8)
        self.gsum = bf("gsum", 4)
        self.i32f = f32("i32f", 32)
        self.iota = f32("iota", 2)
        self.ones32 = bf("ones32", 128)
        assert pos[0] <= 2048, pos[0]
        self.uTs = View(self.uT, self.uT.ap[:, :, 0:76].rearrange("p g (b t) -> p g b t", b=4))

    def sample(self):
        C, nc, din, dout = self.C, self.nc, self.din, self.dout
        C.memset("dve", self.xs[1][0:1, 0, 0:1], 0.0)
        C.memset("dve", self.KT[0:1, 0, 0:1], 0.0)
        C.memset("dve", self.VE[0:1, 0, 0, 0:1], 0.0)
        C.barrier()
        self.carve()
        pg = self.plan_group(None)
        if self.mode == "A":
            pg = pg[:9]
        else:
            pg = pg[:6] + pg[9:]
        self.plan += self.plan_gate() + pg
        xs = self.xs[0]
        C.dma("sp", xs[0:16, 0, :], din["xs"])
        self.load_gate([2 + t // 4 for t in range(16)])
        for b in range(4):
            C.copy("dve", self.sc_t[:, :, :, 4 * b:4 * b + 4], self.scl[:, :, :, 2 + b:3 + b].bcast([128, 2, 8, 4]))
            for s, mi in ((0, 0), (1, 3)):
                C.copy("dve", self.sh_t[:, s, :, 4 * b:4 * b + 4],
                       self.modT[:, mi * 8:(mi + 1) * 8, 2 + b:3 + b].bcast([128, 8, 4]))
        self.norm_to_hT(xs, 0, 1, 16, mod_tiles=(self.sc_t, self.sh_t))
        C.copy("pool", self.uTs[:, :, :, 0:15], self.stH[:])
        self.mixer_front(16, first=False, sample=True)
        if self.mode == "A":
            with nc.allow_non_contiguous_dma(reason="15-row pooling state, transposed store"):
                for b in range(4):
                    for g in range(4):
                        C.dma("pool", dout["pool_s"][b * 15:(b + 1) * 15, g * 128:(g + 1) * 128].rearrange("t c -> c t"),
                              self.uTs[:, g, b, 4:19], is_output=True)
        if self.mode == "A":
            wq = self.wget(("w_in", "k8", 512, 512))[:].rearrange("p (kc n) -> p kc n", kc=8)
            wk = self.wget(("w_in", "k8", 1024, 512), keep=1)[:].rearrange("p (kc n) -> p kc n", kc=8)
            wv = self.wget(("w_in", "k8", 1536, 512), keep=2)[:].rearrange("p (kc n) -> p kc n", kc=8)
            self.qkv_tile(0, 16, wq, wk, wv, dout["k_s"], dout["v_s"],
                          self.vsb[0:16, :].rearrange("p (h d) -> p h d", h=8),
                          q_dst=self.rtmp[0][0:16, :].rearrange("p (h d) -> p h d", h=8))
            C.dma("pool", dout["q_s"], self.rtmp[0][0:16, :], is_output=True)
            return
        C.dma("sp", self.rtmp[0][0:16, :], din["o_in"])
        C.copy("dve", self.obf[0:16, :], self.rtmp[0][0:16, :])
        bk = self.bank()
        bv = bk[:].bitcast(BF16).rearrange("p (c n) -> p c n", c=8)
        C.transposes([(bv[:, c, 0:16], self.obf[0:16, c * 128:(c + 1) * 128]) for c in range(4)], self.ident)
        C.copy("act", self.oT[:, :, 0:16], bv[:, 0:4, 0:16])
        self.mixer_back(16, 1, 16, xs)
        self.norm_to_hT(xs, 1, 1, 16, mod_tiles=(self.sc_t, self.sh_t))
        self.ffn(16, 1, 16, xs, hist_cols=True)
        C.dma("pool", dout["y_s"], xs[0:16, 0, :], is_output=True)
        with nc.allow_non_contiguous_dma(reason="2-row conv state, transposed store"):
            for b in range(4):
                for t in range(2):
                    C.dma("pool", dout["conv_s"][b * 2 + t:b * 2 + t + 1, :].rearrange("o (j c) -> c (o j)", c=128),
                          self.cso[:, :, b, t], is_output=True)

    def build(self):
        self.setup()
        if self.mode == "A":
            for seq in range(NPB):
                self.prompt_seq(seq)
        self.sample()
        self.C.finish()


NBS = 32
NB_CORES = 4


def host_consts_B(nb=NBS):
    c = {}
    c["ident"] = np.eye(128, dtype=np.float32)
    sel = np.zeros((4 * nb, nb, 128), np.float32)
    for b in range(nb):
        for s in range(4):
            sel[4 * b + s, b, s] = 1.0
    c["selB"] = sel.reshape(4 * nb, nb * 128)
    hc = host_consts()
    for k in ("caus", "hmask", "gsum", "i32"):
        c[k] = hc[k]
    c["iota"] = (2.0 * (np.arange(128) % 64)).astype(np.float32).reshape(128, 1)
    return c


def B_in_shapes(nb, npool):
    nt = 4 * nb
    return {"cache_k": ([npool * 128, 512], F32), "cache_v": ([npool * 128, 512], F32), "ptab": ([1, nb * NPAGES], I32),
            "q_all": ([nt, 512], F32), "k_all": ([nt, 512], F32), "v_all": ([nt, 512], F32),
            "c_ident": ([128, 128], F32), "c_selB": ([nt, nb * 128], F32), "c_caus": ([128, 32], F32),
            "c_hmask": ([32, 8], F32), "c_gsum": ([32, 4], F32), "c_i32": ([32, 32], F32), "c_iota": ([128, 1], F32)}


class KernB:
    def __init__(self, nc, es, nb=NBS, npool=5120, debug=False):
        self.nc = nc
        self.nb = nb
        C = self.C = Ctx(nc, es)
        self.din = {}
        nt = 4 * nb
        self.nt = nt
        for k, (shp, dt) in B_in_shapes(nb, npool).items():
            self.din[k] = nc.dram_tensor(k, shp, dt, kind="ExternalInput").ap()
        self.o_all = nc.dram_tensor("o_all", [nt, 512], F32, kind="ExternalOutput").ap()
        self.debug = debug
        if debug:
            self.dbg = {k: nc.dram_tensor(k, shp, F32, kind="ExternalOutput").ap() for k, shp in
                        (("d_S", [128, 129 * 32]), ("d_S2", [128, 129 * 32]), ("d_sc", [32, 64]), ("d_negb", [32, 64]),
                         ("d_oacc", [32, 512]), ("d_den", [32, 2]))}
        sb = C.sbuf
        self.ident = sb("ident", [128, 128], BF16)
        self.selB = sb("selB", [nt, nb, 128], BF16)
        self.caus = sb("caus", [128, 32], F32)
        self.hmask = sb("hmask", [32, 8], F32)
        self.gsum = sb("gsum", [32, 4], BF16)
        self.i32f = sb("i32f", [32, 32], F32)
        self.iota = sb("iota", [128, 1], F32)
        self.ones_col = sb("ones_col", [128, 2], BF16)
        self.ones32 = sb("ones32", [32, 128], BF16)
        self.ptI = sb("ptI", [128, nb * NPAGES], I32)
        self.ptF = sb("ptF", [128, nb * NPAGES], F32)
        self.ptB = sb("ptB", [128, nb * 64], F32)
        self.IDX = sb("IDX", [128, nb * 64], I32)
        self.qbf = sb("qbf", [nt, 512], BF16)
        self.kbf = sb("kbf", [nt, 512], BF16)
        self.vbf = sb("vbf", [nt, 512], BF16)
        self.qTs = sb("qTs", [128, 4, nt], BF16)
        self.Qb = sb("Qb", [128, nb, 4, 32], BF16)
        self.S_all = sb("S_all", [128, NPAGES + 1, 32], F32)
        self.PTs = sb("PTs", [128, NPAGES + 1, 32], BF16)
        self.Dexp = sb("Dexp", [32, 64, 32], BF16)
        self.kpg = [sb("kpg%d" % i, [128, 1024], BF16) for i in range(6)]
        self.kTp = [sb("kTp%d" % i, [128, 512], BF16) for i in range(4)]
        self.kmTs = sb("kmTs", [128, 4, 64], BF16)
        self.sc_sb = sb("sc_sb", [32, 64], F32)
        self.top8 = sb("top8", [32, 8], F32)
        self.negb = sb("negb", [32, 64], F32)
        self.omask = sb("omask", [32, 512], BF16)
        self.rden_s = sb("rden_s", [32, 2], F32)
        self.osb = [sb("osb%d" % i, [4, 512], F32) for i in range(2)]
        self.banks = [C.psum("bank%d" % i, [128, 512], F32) for i in range(8)]
        self.held = set()
        self.bank_i = 0

    bank = Kern.bank
    release = Kern.release

    def build(self):
        C, nc, din = self.C, self.nc, self.din
        C.dma("pool", self.ident[:], din["c_ident"])
        C.dma("pool", self.selB[:].rearrange("p b k -> p (b k)"), din["c_selB"])
        C.dma("sp", self.caus[:], din["c_caus"])
        C.dma("sp", self.hmask[:], din["c_hmask"])
        C.dma("pool", self.gsum[:], din["c_gsum"])
        C.dma("sp", self.i32f[:], din["c_i32"])
        C.dma("sp", self.iota[:], din["c_iota"])
        C.dma("pool", self.qbf[:], din["q_all"])
        C.dma("pool", self.kbf[:], din["k_all"])
        C.dma("pool", self.vbf[:], din["v_all"])
        C.memset("dve", self.ones_col[:], 1.0)
        C.memset("dve", self.ones32[:], 1.0)
        C.dma("sp", self.ptI[:], din["ptab"].partition_broadcast(128))
        C.copy("dve", self.ptF[:], self.ptI[:])
        pf3 = self.ptF[:].rearrange("p (k two) -> p k two", two=2)
        for j in range(2):
            C.ts("dve", self.ptB[j * 64:(j + 1) * 64, :], pf3[j * 64:(j + 1) * 64, :, j], 128.0,
                 self.iota[j * 64:(j + 1) * 64, 0:1], ALU.mult, ALU.add)
        C.copy("dve", self.IDX[:], self.ptB[:])
        bk = self.bank()
        bv = bk[:].bitcast(BF16)
        nt = self.nt
        C.transposes([(bv[:, c * nt:(c + 1) * nt], self.qbf[:, c * 128:(c + 1) * 128]) for c in range(4)], self.ident)
        C.copy("dve", self.qTs[:], bv[:, 0:4 * nt].rearrange("p (c t) -> p c t", c=4))
        C.memset("dve", self.Qb[:], 0.0)
        qb5 = self.Qb[:].rearrange("p b c (s h) -> p b c s h", h=8)
        for c in range(4):
            for e in range(2):
                C.copy("pool" if e else "dve", qb5[e * 64:(e + 1) * 64, :, c, :, 2 * c + e],
                       self.qTs[e * 64:(e + 1) * 64, c, :].rearrange("p (b s) -> p b s", s=4))
        for b in range(self.nb):
            self.attn_b(b)
        C.finish()

    def attn_b(self, b):
        C, nc, din = self.C, self.nc, self.din
        kmacc = self.bank(hold=True)
        sbank = None
        sb_state = {"bank": None}

        def qk(lp, kt):
            if lp % 16 == 0:
                sb_state["bank"] = self.bank(hold=True)
            sbank = sb_state["bank"]
            C.mm(sbank[:, (lp % 16) * 32:(lp % 16 + 1) * 32],
                 [(kt[:, c * 128:(c + 1) * 128], self.Qb[:, b, c, :]) for c in range(4)])
            if lp % 16 == 15:
                C.copy("act", self.S_all[:, lp - 15:lp + 1, :], sbank[:, :].rearrange("p (a n) -> p a n", a=16))
                self.release(sbank)

        prev = None
        for lp in range(NPAGES):
            blk, e = lp // 2, lp % 2
            kp = self.kpg[blk % 6]
            if e == 0:
                C.dma("pool", kp[:, :], din["cache_k"], indirect_in=self.IDX[:, b * 64 + blk:b * 64 + blk + 1])
                for c in range(4):
                    col = c * 64 + blk
                    C.mm(kmacc[:, col:col + 1], [(kp[:, ee * 512 + c * 128:ee * 512 + (c + 1) * 128], self.ones_col[:, 0:1])
                                                 for ee in range(2)])
            tb = self.bank()
            tv = tb[:].bitcast(BF16)
            C.transposes([(tv[:, c * 128:(c + 1) * 128], kp[:, e * 512 + c * 128:e * 512 + (c + 1) * 128]) for c in range(4)],
                         self.ident)
            kt = self.kTp[lp % 4]
            C.copy("act" if lp % 2 else "dve", kt[:, :], tv[:, 0:512])
            if prev is not None:
                qk(*prev)
            prev = (lp, kt)
        qk(*prev)
        xb = self.bank()
        for c in range(4):
            C.mm(xb[:, c * 128:(c + 1) * 128], [(self.kbf[:, c * 128:(c + 1) * 128], self.selB[:, b, :])])
        kt = self.kTp[NPAGES % 4]
        C.copy("dve", kt[:, :], xb[:, :])
        sb2 = self.bank()
        C.mm(sb2[:, 0:32], [(kt[:, c * 128:(c + 1) * 128], self.Qb[:, b, c, :]) for c in range(4)])
        C.copy("act", self.S_all[:, NPAGES, :], sb2[:, 0:32])
        if self.debug and b == 0:
            C.dma("sp", self.dbg["d_S"], self.S_all[:].rearrange("p a n -> p (a n)"), is_output=True)
        C.copy("dve", self.kmTs[:], kmacc[:, 0:256].rearrange("p (c k) -> p c k", c=4))
        self.release(kmacc)
        scb = self.bank()
        C.mm(scb[0:32, 0:64], [(self.Qb[:, b, c, :], self.kmTs[:, c, :]) for c in range(4)])
        C.copy("dve", self.sc_sb[:, :], scb[0:32, 0:64])
        C.op("dve", lambda: nc.vector.max(out=self.top8.ap, in_=self.sc_sb.ap), [self.sc_sb.v()], [self.top8.v()])
        C.ts("dve", self.negb[:, :], self.sc_sb[:, :], self.top8[:, 2:3], NEG, ALU.is_lt, ALU.mult)
        C.tt("dve", self.Dexp[:], self.negb[:, :].unsq(2).bcast([32, 64, 32]),
             self.i32f[:, :].unsq(1).bcast([32, 64, 32]), ALU.mult)
        for q in range(4):
            bb = self.bank()
            C.mm(bb[:, :], [(self.ones32[:, :], self.Dexp[:, q * 16:(q + 1) * 16, :].rearrange("p k n -> p (k n)"))])
            Sv = self.S_all[:, q * 32:(q + 1) * 32, :].rearrange("p (k two) n -> p k two n", two=2)
            C.tt("dve", Sv, Sv, bb[:, :].rearrange("p (k n) -> p k n", k=16).unsq(2).bcast([128, 16, 2, 32]), ALU.add)
        C.tt("dve", self.S_all[:, NPAGES, :], self.S_all[:, NPAGES, :], self.caus[:, :], ALU.add)
        if self.debug and b == 0:
            C.dma("sp", self.dbg["d_S2"], self.S_all[:].rearrange("p a n -> p (a n)"), is_output=True)
            C.dma("sp", self.dbg["d_sc"], self.sc_sb[:], is_output=True)
            C.dma("sp", self.dbg["d_negb"], self.negb[:], is_output=True)
        for a in range(0, NPAGES + 1, 43):
            C.act(self.PTs[:, a:a + 43, :], self.S_all[:, a:a + 43, :], AF.Exp)
        oacc = self.bank(hold=True)
        dacc = self.bank(hold=True)
        for lp in range(NPAGES):
            blk, e = lp // 2, lp % 2
            vp = self.kpg[blk % 6]
            if e == 0:
                C.dma("pool", vp[:, :], din["cache_v"], indirect_in=self.IDX[:, b * 64 + blk:b * 64 + blk + 1])
            C.mm(oacc[0:32, :], [(self.PTs[:, lp, :], vp[:, e * 512:(e + 1) * 512])], start=(lp == 0), stop=False)
            C.mm(dacc[0:32, 0:1], [(self.PTs[:, lp, :], self.ones_col[:, 0:1])], start=(lp == 0), stop=False)
        vx = self.bank()
        C.mm(vx[:, :], [(self.selB[:, b, :], self.vbf[:, :])])
        vp = self.kpg[(NPAGES // 2) % 6]
        C.copy("dve", vp[:, 0:512], vx[:, :])
        C.mm(oacc[0:32, :], [(self.PTs[:, NPAGES, :], vp[:, 0:512])], start=False, stop=True)
        C.mm(dacc[0:32, 0:1], [(self.PTs[:, NPAGES, :], self.ones_col[:, 0:1])], start=False, stop=True)
        if self.debug and b == 0:
            C.copy("dve", self.sc_sb[:, 0:2], dacc[0:32, 0:2])
            C.dma("sp", self.dbg["d_den"], self.sc_sb[:, 0:2], is_output=True)
        C.recip(self.rden_s[:, 0:1], dacc[0:32, 0:1])
        C.stt("dve", self.omask[:, :].rearrange("p (h d) -> p h d", h=8),
              oacc[0:32, :].rearrange("p (h d) -> p h d", h=8), self.rden_s[:, 0:1],
              self.hmask[:, :].unsq(2).bcast([32, 8, 64]), ALU.mult, ALU.mult)
        self.release(oacc)
        self.release(dacc)
        ob = self.bank()
        C.mm(ob[0:4, :], [(self.gsum[:, :], self.omask[:, :])])
        osb = self.osb[b % 2]
        C.copy("act", osb[:, :], ob[0:4, :])
        C.dma("sp", self.o_all[4 * b:4 * b + 4, :], osb[:, :], is_output=True)


def build_program_B(nb=NBS, npool=5120, debug=False):
    nc = bass.Bass("TRN2", target_bir_lowering=False)
    es = ExitStack()
    with es:
        k = KernB(nc, es, nb=nb, npool=npool, debug=debug)
        k.build()
    return nc


def build_program(mode="A"):
    nc = bass.Bass("TRN2", target_bir_lowering=False)
    es = ExitStack()
    with es:
        k = Kern(nc, es, mode=mode)
        k.build()
    return nc


_CACHE = {}


def _in_maps(inputs, mode="A"):
    consts = host_consts()
    f = lambda a: np.ascontiguousarray(a, dtype=np.float32)
    maps = []
    shared = {
        "w_ada": f(inputs["w_ada"][0]), "b_ada": f(inputs["b_ada"][0]).reshape(1, -1),
        "g_mix": f(inputs["g_norm_mix"][0]).reshape(1, -1), "w_in": f(inputs["w_in"][0]),
        "g_q": f(inputs["g_q"][0]).reshape(1, -1), "g_k": f(inputs["g_k"][0]).reshape(1, -1),
        "w_pool": f(inputs["w_pool_group"][0]).reshape(512, 128),
        "pool_scale": f(inputs["pool_scale"][0]).reshape(1, -1),
        "w_bp": f(inputs["w_branch_pool"][0]), "w_ba": f(inputs["w_branch_attn"][0]),
        "w_out": f(inputs["w_out"][0]), "g_ffn": f(inputs["g_norm_ffn"][0]).reshape(1, -1),
        "w_up": f(inputs["w_up"][0]), "w_conv": f(inputs["w_conv"][0]),
        "b_conv": f(inputs["b_conv"][0]).reshape(1, -1), "w_down": f(inputs["w_down"][0]),
    }
    for k, v in consts.items():
        shared["c_" + k] = v
    for i in range(8):
        m = dict(shared)
        if mode == "A":
            m["xp"] = f(inputs["x_prompt"][NPB * i:NPB * (i + 1)]).reshape(NPB * SEQ, D)
        m["xs"] = f(inputs["x_sample"][NSB * i:NSB * (i + 1)]).reshape(NST, D)
        m["c_all"] = np.concatenate([f(inputs["c_prompt"][NPB * i:NPB * (i + 1)]),
                                     f(inputs["c_sample"][NSB * i:NSB * (i + 1)])], axis=0)
        m["st_pool"] = f(inputs["state_pool"][0, NSB * i:NSB * (i + 1)]).reshape(NSB * 15, 512)
        m["st_conv"] = f(inputs["state_ffn_conv"][0, NSB * i:NSB * (i + 1)]).reshape(NSB * 2, DFF)
        maps.append(m)
    return maps


def _prog(key, fn):
    if key not in _CACHE:
        _CACHE[key] = fn()
    return _CACHE[key]


def kernel(**inputs):
    cat = lambda res, k: np.concatenate([r[k] for r in res], axis=0)
    maps = _in_maps(inputs, "A")
    ra = run_bass_kernel_spmd(_prog("A", lambda: build_program("A")), maps, core_ids=list(range(8))).results
    q_all, k_all, v_all = cat(ra, "q_s"), cat(ra, "k_s"), cat(ra, "v_s")
    ck = np.ascontiguousarray(inputs["cache_k"][0], dtype=np.float32).reshape(5120 * 128, 512)
    cv = np.ascontiguousarray(inputs["cache_v"][0], dtype=np.float32).reshape(5120 * 128, 512)
    nbc = NBS // NB_CORES
    cb = host_consts_B(nbc)
    mbs = []
    for i in range(NB_CORES):
        r = slice(4 * nbc * i, 4 * nbc * (i + 1))
        mb = {"cache_k": ck, "cache_v": cv,
              "ptab": np.ascontiguousarray(inputs["page_table"][nbc * i:nbc * (i + 1)], dtype=np.int32).reshape(1, -1),
              "q_all": np.ascontiguousarray(q_all[r]), "k_all": np.ascontiguousarray(k_all[r]),
              "v_all": np.ascontiguousarray(v_all[r])}
        for k, v in cb.items():
            mb["c_" + k] = v
        mbs.append(mb)
    rb = run_bass_kernel_spmd(_prog("B", lambda: build_program_B(nb=nbc)), mbs, core_ids=list(range(NB_CORES))).results
    o_all = np.concatenate([r["o_all"] for r in rb], axis=0)
    maps = _in_maps(inputs, "C")
    for i in range(8):
        maps[i]["o_in"] = np.ascontiguousarray(o_all[NST * i:NST * (i + 1)])
    rc = run_bass_kernel_spmd(_prog("C", lambda: build_program("C")), maps, core_ids=list(range(8))).results
    y_p = cat(ra, "y_p").reshape(16, SEQ, D)
    y_s = cat(rc, "y_s").reshape(32, 4, D)
    k_p = cat(ra, "k_p").reshape(1, 16, SEQ, 8, 64)
    v_p = cat(ra, "v_p").reshape(1, 16, SEQ, 8, 64)
    pool_p = cat(ra, "pool_p").reshape(1, 16, 15, 512)
    conv_p = cat(ra, "conv_p").reshape(1, 16, 2, DFF)
    k_s = k_all.reshape(1, 32, 4, 8, 64)
    v_s = v_all.reshape(1, 32, 4, 8, 64)
    pool_s = cat(ra, "pool_s").reshape(1, 32, 15, 512)
    conv_s = cat(rc, "conv_s").reshape(1, 32, 2, DFF)
    return (y_p, y_s, k_p, v_p, pool_p, conv_p, k_s, v_s, pool_s, conv_s)
```

```python
import numpy as np
from contextlib import ExitStack
import concourse.bass as bass
import concourse.mybir as mybir
from concourse.bass_utils import run_bass_kernel_spmd

F32 = mybir.dt.float32
BF16 = mybir.dt.bfloat16
I32 = mybir.dt.int32
AF = mybir.ActivationFunctionType
ALU = mybir.AluOpType
AX = mybir.AxisListType

D = 1024
SEQ = 2048
NPB = 2
NSB = 4
NST = 16
DFF = 2816
NFC = 22
G = 256
NG = SEQ // G
EPS = 1e-6
NEG = -30000.0
NPAGES = 128
RING = 4


class View:
    __slots__ = ("tile", "ap")

    def __init__(self, tile, ap):
        self.tile = tile
        self.ap = ap

    def __getitem__(self, idx):
        return View(self.tile, self.ap[idx])

    def bitcast(self, dt):
        return View(self.tile, self.ap.bitcast(dt))

    def rearrange(self, pattern, **kw):
        return View(self.tile, self.ap.rearrange(pattern, **kw))

    def bcast(self, shape):
        return View(self.tile, self.ap.to_broadcast(list(shape)))

    def unsq(self, ax):
        return View(self.tile, self.ap.unsqueeze(ax))


class Tile:
    def __init__(self, name, ap):
        self.name = name
        self.ap = ap
        self.last_write = None
        self.readers = {}
        self.dsem = None

    def __getitem__(self, idx):
        return View(self, self.ap[idx])

    def v(self, ap=None):
        return View(self, self.ap if ap is None else ap)


class Ctx:
    def __init__(self, nc, es):
        self.nc = nc
        self.es = es
        self.E = {"pe": nc.tensor, "act": nc.scalar, "dve": nc.vector, "pool": nc.gpsimd, "sp": nc.sync}
        self.sems = {}
        self.semval = {}
        self.seen = {e: {} for e in self.E}
        for e in self.E:
            self.sems[e] = es.enter_context(nc.semaphore("sem_" + e))
            self.semval[e] = 0
        self.ndsem = 0
        self.out_events = {}
        self.dsem_pool = {}

    def sbuf(self, name, shape, dtype):
        t = self.es.enter_context(self.nc.sbuf_tensor(name, list(shape), dtype))
        return Tile(name, t[:])

    def psum(self, name, shape, dtype):
        t = self.es.enter_context(self.nc.psum_tensor(name, list(shape), dtype))
        return Tile(name, t[:])

    @staticmethod
    def _tiles(views):
        out = []
        for v in views:
            if v is None:
                continue
            t = v.tile if isinstance(v, View) else (v if isinstance(v, Tile) else None)
            if t is not None and t not in out:
                out.append(t)
        return out

    def _deps(self, rt, wt):
        deps = {}

        def add(d):
            if d is None:
                return
            k, v = d
            if deps.get(k, 0) < v:
                deps[k] = v

        for t in rt:
            add(t.last_write)
        for t in wt:
            add(t.last_write)
            for d in t.readers.values():
                add(d)
        return deps

    def _wait(self, eng, deps, skip_own=False):
        for k, v in deps.items():
            if skip_own and k == eng:
                continue
            if self.seen[eng].get(k, 0) >= v:
                continue
            self.E[eng].wait_ge(self.sems[k], v)
            self.seen[eng][k] = v

    def op(self, eng, fn, reads, writes):
        rt = self._tiles(reads)
        wt = self._tiles(writes)
        deps = self._deps(rt, wt)
        self._wait(eng, deps, skip_own=(eng == "pe"))
        ins = fn()
        self.semval[eng] += 1
        ins.then_inc(self.sems[eng], 1)
        ev = (eng, self.semval[eng])
        for t in rt:
            t.readers[eng] = ev
        for t in wt:
            t.last_write = ev
            t.readers = {}
        return ins

    def barrier(self):
        engs = ["pe", "act", "dve", "pool"]
        for e in engs:
            for f in engs:
                if e == f:
                    continue
                v = self.semval[f]
                if v > 0 and self.seen[e].get(f, 0) < v:
                    self.E[e].wait_ge(self.sems[f], v)
                    self.seen[e][f] = v

    def _dsem(self, t):
        if t.dsem is None:
            k = "d%d" % self.ndsem
            self.ndsem += 1
            self.sems[k] = self.es.enter_context(self.nc.semaphore("sem_" + k))
            self.semval[k] = 0
            t.dsem = k
        return t.dsem

    def dma(self, q, out, in_, is_output=False, indirect_in=None, extra_reads=(), no_waw=False, **kw):
        ot = out.tile if isinstance(out, View) else None
        it = in_.tile if isinstance(in_, View) else None
        rt = ([it] if it else []) + self._tiles(list(extra_reads) + ([indirect_in] if indirect_in is not None else []))
        wt = [ot] if ot else []
        deps = self._deps(rt, [] if no_waw else wt)
        self._wait(q, deps)
        st = ot or it
        k = self._dsem(st)
        self.semval[k] += 16
        oap = out.ap if isinstance(out, View) else out
        iap = in_.ap if isinstance(in_, View) else in_
        if indirect_in is not None:
            ins = self.E[q].indirect_dma_start(
                out=oap, out_offset=None, in_=iap,
                in_offset=bass.IndirectOffsetOnAxis(ap=indirect_in.ap, axis=0), **kw)
        else:
            ins = self.E[q].dma_start(out=oap, in_=iap, **kw)
        ins.then_inc(self.sems[k], 16)
        ev = (k, self.semval[k])
        for t in rt:
            t.readers[k] = ev
        if ot:
            ot.last_write = ev
            ot.readers = {}
        if is_output:
            self.out_events[k] = self.semval[k]
        return ins

    def finish(self, eng="sp"):
        for k, v in self.out_events.items():
            if self.seen[eng].get(k, 0) < v:
                self.E[eng].wait_ge(self.sems[k], v)
                self.seen[eng][k] = v

    def mm(self, out, pairs, start=True, stop=True):
        reads = []
        for l, r in pairs:
            reads += [l, r]
        n = len(pairs)

        def fn():
            ins = None
            for i, (l, r) in enumerate(pairs):
                ins = self.nc.tensor.matmul(out.ap, l.ap, r.ap,
                                            start=(start and i == 0), stop=(stop and i == n - 1))
            return ins

        return self.op("pe", fn, reads, [out])

    def transposes(self, items, ident):
        reads = [ident] + [i for _, i in items]
        writes = [o for o, _ in items]

        def fn():
            ins = None
            for o, i in items:
                k = i.ap.shape[0]
                ins = self.nc.tensor.transpose(o.ap, i.ap, ident.ap[0:k, 0:k])
            return ins

        return self.op("pe", fn, reads, writes)

    def act(self, out, in_, func, bias=None, scale=None, accum_out=None):
        kw = {}
        reads = [in_]
        for nm, val in (("bias", bias), ("scale", scale)):
            if val is None:
                continue
            if isinstance(val, View):
                kw[nm] = val.ap
                reads.append(val)
            else:
                kw[nm] = val
        writes = [out]
        if accum_out is not None:
            kw["accum_out"] = accum_out.ap
            writes.append(accum_out)
        return self.op("act", lambda: self.nc.scalar.activation(out=out.ap, in_=in_.ap, func=func, **kw), reads, writes)

    def _ve(self, eng):
        return self.nc.vector if eng == "dve" else self.nc.gpsimd

    def tt(self, eng, out, in0, in1, op):
        return self.op(eng, lambda: self._ve(eng).tensor_tensor(out=out.ap, in0=in0.ap, in1=in1.ap, op=op), [in0, in1], [out])

    def ts(self, eng, out, in0, s1, s2, op0, op1=None):
        reads = [in0]
        a1, a2 = s1, s2
        if isinstance(s1, View):
            a1 = s1.ap
            reads.append(s1)
        if isinstance(s2, View):
            a2 = s2.ap
            reads.append(s2)
        kw = {}
        if op1 is not None:
            kw["op1"] = op1
        return self.op(eng, lambda: self._ve(eng).tensor_scalar(out=out.ap, in0=in0.ap, scalar1=a1, scalar2=a2, op0=op0, **kw), reads, [out])

    def stt(self, eng, out, in0, scalar, in1, op0, op1):
        reads = [in0, in1]
        a = scalar
        if isinstance(scalar, View):
            a = scalar.ap
            reads.append(scalar)
        return self.op(eng, lambda: self._ve(eng).scalar_tensor_tensor(out=out.ap, in0=in0.ap, scalar=a, in1=in1.ap, op0=op0, op1=op1), reads, [out])

    def copy(self, eng, out, in_):
        if eng == "act":
            return self.op("act", lambda: self.nc.scalar.copy(out=out.ap, in_=in_.ap), [in_], [out])
        return self.op(eng, lambda: self._ve(eng).tensor_copy(out=out.ap, in_=in_.ap), [in_], [out])

    def memset(self, eng, out, val):
        return self.op(eng, lambda: self._ve(eng).memset(out.ap, val), [], [out])

    def reduce(self, eng, out, in_, op, axis=AX.X):
        return self.op(eng, lambda: self._ve(eng).tensor_reduce(out=out.ap, in_=in_.ap, axis=axis, op=op), [in_], [out])

    def recip(self, out, in_):
        return self.op("dve", lambda: self.nc.vector.reciprocal(out=out.ap, in_=in_.ap), [in_], [out])


def host_consts():
    c = {}
    c["ident"] = np.eye(128, dtype=np.float32)
    kk = np.arange(128)[:, None]
    qq = np.arange(128)[None, :]
    c["tri"] = np.where(kk <= qq, 0.0, NEG).astype(np.float32)
    c["kext"] = (np.arange(SEQ)[None, :] // G == np.arange(8)[:, None]).astype(np.float32)
    inv = np.zeros((128, 4, 16), np.float32)
    for g, w in enumerate((2, 4, 8, 16)):
        inv[:, g, :] = 1.0 / np.minimum(w, np.arange(16) + 1)
    c["invcnt"] = inv.reshape(128, 64)
    sel = np.zeros((16, 4, 128), np.float32)
    for b in range(4):
        for s in range(4):
            sel[4 * b + s, b, s] = 1.0
    c["sel"] = sel.reshape(16, 512)
    caus = np.full((128, 32), NEG, np.float32)
    for s in range(4):
        for h in range(8):
            for key in range(4):
                if key <= s:
                    caus[key, s * 8 + h] = 0.0
    c["caus"] = caus
    hm = np.zeros((32, 8), np.float32)
    gs = np.zeros((32, 4), np.float32)
    for s in range(4):
        for h in range(8):
            hm[s * 8 + h, h] = 1.0
            gs[s * 8 + h, s] = 1.0
    c["hmask"] = hm
    c["gsum"] = gs
    c["i32"] = np.eye(32, dtype=np.float32)
    c["iota"] = np.arange(128, dtype=np.float32).reshape(128, 1)
    return c


CONST_SHAPES = {"ident": [128, 128], "tri": [128, 128], "kext": [8, 2048], "invcnt": [128, 64],
                "sel": [16, 512], "caus": [128, 32], "hmask": [32, 8], "gsum": [32, 4], "i32": [32, 32],
                "iota": [128, 1]}

IN_SHAPES = {
    "xp": ([NPB * SEQ, D], F32), "xs": ([NST, D], F32), "c_all": ([NPB + NSB, D], F32),
    "w_ada": ([D, 6 * D], F32), "b_ada": ([1, 6 * D], F32), "g_mix": ([1, D], F32),
    "w_in": ([D, 4096], F32), "g_q": ([1, 512], F32), "g_k": ([1, 512], F32),
    "w_pool": ([512, 128], F32), "pool_scale": ([1, 512], F32),
    "w_bp": ([512, D], F32), "w_ba": ([512, D], F32), "w_out": ([D, D], F32), "g_ffn": ([1, D], F32),
    "w_up": ([D, 2 * DFF], F32), "w_conv": ([3, DFF], F32), "b_conv": ([1, DFF], F32), "w_down": ([DFF, D], F32),
    "st_pool": ([NSB * 15, 512], F32), "st_conv": ([NSB * 2, DFF], F32), "ptab": ([1, NSB * NPAGES], I32),
    "cache_k": ([5120 * 128, 512], F32), "cache_v": ([5120 * 128, 512], F32),
}
A_OUTS = ("y_p", "k_p", "v_p", "pool_p", "conv_p", "k_s", "v_s", "pool_s", "q_s")
C_OUTS = ("y_s", "conv_s")
OUT_SHAPES = {
    "q_s": [NST, 512],
    "y_p": [NPB * SEQ, D], "y_s": [NST, D], "k_p": [NPB * SEQ, 512], "v_p": [NPB * SEQ, 512],
    "pool_p": [NPB * 15, 512], "conv_p": [NPB * 2, DFF], "k_s": [NST, 512], "v_s": [NST, 512],
    "pool_s": [NSB * 15, 512], "conv_s": [NSB * 2, DFF],
}


class Kern:
    def __init__(self, nc, es, mode="A"):
        self.nc = nc
        self.C = Ctx(nc, es)
        self.mode = mode
        self.din = {}
        for k, (shp, dt) in IN_SHAPES.items():
            if k in ("cache_k", "cache_v", "ptab"):
                continue
            if mode == "C" and k == "xp":
                continue
            self.din[k] = nc.dram_tensor(k, shp, dt, kind="ExternalInput").ap()
        if mode == "C":
            self.din["o_in"] = nc.dram_tensor("o_in", [NST, 512], F32, kind="ExternalInput").ap()
        for k, shp in CONST_SHAPES.items():
            self.din["c_" + k] = nc.dram_tensor("c_" + k, shp, F32, kind="ExternalInput").ap()
        outs = A_OUTS if mode == "A" else C_OUTS
        self.dout = {k: nc.dram_tensor(k, OUT_SHAPES[k], F32, kind="ExternalOutput").ap() for k in outs}
        self.wscr = {}
        for k, shp in (("w_in", [D, 4096]), ("w_bp", [512, D]), ("w_ba", [512, D]), ("w_out", [D, D]),
                       ("w_up", [D, 2 * DFF]), ("w_down", [DFF, D]), ("w_adag", [D, 2 * D])):
            ap = nc.dram_tensor("scr_" + k, shp, BF16, kind="Internal").ap()
            self.wscr[k] = (ap, Tile("scr_" + k, ap))
        self.alloc()

    def alloc(self):
        C = self.C
        sb = C.sbuf
        self.ident = sb("ident", [128, 128], BF16)
        self.tri = sb("tri", [128, 128], BF16)
        self.invcnt = sb("invcnt", [128, 4, 16], F32)
        self.gq = sb("gq", [128, 512], F32)
        self.gk = sb("gk", [128, 512], F32)
        self.wpool = sb("wpool", [128, 4, 128], BF16)
        self.pscaleT = sb("pscaleT", [128, 4], F32)
        self.wconvT = sb("wconvT", [128, NFC, 3], F32)
        self.bconvT = sb("bconvT", [128, NFC], F32)
        self.gmixT = sb("gmixT", [128, 8], F32)
        self.gffnT = sb("gffnT", [128, 8], F32)
        self.modT = sb("modT", [128, 48, 6], F32)
        self.scl = sb("scl", [128, 2, 8, 6], F32)
        self.badaT = sb("badaT", [128, 48], F32)
        self.cT = sb("cT", [128, 8, 6], F32)
        self.siluT = sb("siluT", [128, 8, 6], BF16)
        self.silurep = sb("silurep", [128, 8, 128], BF16)
        self.bada_g = sb("bada_g", [1, 2 * D], BF16)
        self.ones1 = sb("ones1", [1, 128], BF16)
        self.gate = sb("gate", [128, 2, D], F32)
        self.KTf = sb("KT", [128, 8 * SEQ], BF16)
        self.KT = Tile("KTv", self.KTf.ap.rearrange("p (h t) -> p h t", h=8))
        self.KT = self.KTf if False else self.KT
        self.stH = sb("stH", [128, 4, 4, 15], F32)
        self.chs = sb("chs", [128, NFC, 4, 2], F32)
        self.VEf = sb("VE", [128, 16 * 8 * 65], BF16)
        self.VE = Tile("VEv", self.VEf.ap.rearrange("p (a h d) -> p a h d", a=16, h=8))
        self.kmT = sb("kmT", [64, 8, 8], BF16)
        self.ring = [sb("ring%d" % i, [128, 4096], BF16) for i in range(RING)]
        self.xs0f = sb("xs0", [128, 2 * D], F32)
        self.xs1f = sb("xs1", [128, 2 * D], F32)
        self.xs = [Tile("xsv%d" % i, f.ap.rearrange("p (t d) -> p t d", t=2)) for i, f in enumerate((self.xs0f, self.xs1f))]
        self.xn = sb("xn", [128, 2, D], BF16)
        self.ss = sb("ss", [128, 2], F32)
        self.rstd = sb("rstd", [128, 2], F32)
        self.hT = sb("hT", [128, 8, G], BF16)
        self.uT = sb("uT", [128, 4, 16 + G], F32)
        self.ptmp = [sb("ptmp%d" % i, [128, 16 + G], F32) for i in range(2)]
        self.pooled = sb("pooled", [128, 4, G], BF16)
        self.ypool = sb("ypool", [128, 4, G], BF16)
        self.th = [sb("th%d" % i, [128, G], F32) for i in range(2)]
        self.m = sb("m", [128, 8, G], BF16)
        self.thb = sb("thb", [128, 8, G], BF16)
        self.vout = sb("vout", [128, 512], F32)
        self.kout = sb("kout", [128, 512], F32)
        self.sq = sb("sq", [128, 512], F32)
        self.ntmp = sb("ntmp", [128, 512], F32)
        self.ssq = sb("ssq", [128, 8], F32)
        self.rsq = sb("rsq", [128, 8], F32)
        self.kbf = sb("kbf", [128, 512], BF16)
        self.qext = sb("qext", [128, 8, 72], BF16)
        self.qT64 = sb("qT64", [64, 8, 128], BF16)
        self.bsc = sb("bsc", [128, 8, 8], F32)
        self.cmp = sb("cmp", [128, 8, 8, 8], F32)
        self.rank = sb("rank", [128, 8, 8], F32)
        self.QT = sb("QT", [72, 8, G], BF16)
        self.PT = [sb("PT%d" % i, [128, 2, G], BF16) for i in range(4)]
        self.rden = sb("rden", [128, 4], F32)
        self.obf = sb("obf", [128, 512], BF16)
        self.oT = sb("oT", [128, 4, G], BF16)
        self.mb = [sb("mb%d" % i, [128, G], F32) for i in range(2)]
        self.merged = sb("merged", [128, 8, G], BF16)
        self.rtmp = [sb("rtmp%d" % i, [128, 512], F32) for i in range(2)]
        self.gT = [sb("gT%d" % i, [128, 2 + G], F32) for i in range(2)]
        self.ct = [sb("ct%d" % i, [128, G], F32) for i in range(4)]
        self.actT = sb("actT", [128, NFC, G], BF16)
        self.chist = sb("chist", [128, NFC, 2], F32)
        self.banks = [C.psum("bank%d" % i, [128, 512], F32) for i in range(8)]
        self.held = set()
        self.bank_i = 0
        self.ring_i = 0
        self.ring_emit = 0
        self.plan = []
        self.prenormed = False

    def bank(self, hold=False):
        for _ in range(16):
            b = self.bank_i % 8
            self.bank_i += 1
            if b not in self.held:
                if hold:
                    self.held.add(b)
                return self.banks[b]
        raise RuntimeError("no free psum bank")

    def release(self, bank):
        self.held.discard(self.banks.index(bank))

    def plan_group(self, ctx):
        p = []
        p.append(("w_in", "k8", 0, 512))
        p.append(("w_in", "k8", 3072, 512))
        p.append(("w_in", "k8", 3584, 512))
        p.append(("w_bp", "k4", 0, 1024))
        p.append(("w_in", "k8", 2048, 512))
        p.append(("w_in", "k8", 2560, 512))
        p.append(("w_in", "k8", 512, 512))
        p.append(("w_in", "k8", 1024, 512))
        p.append(("w_in", "k8", 1536, 512))
        p.append(("w_ba", "k4", 0, 1024))
        p.append(("w_out", "k8", 0, 512))
        p.append(("w_out", "k8", 512, 512))
        for j in range(6):
            n = 512 if j < 5 else 256
            p.append(("w_up", "k8", 512 * j, n))
            p.append(("w_up", "k8", DFF + 512 * j, n))
        for j in range(6):
            p.append(("w_down", "d4", 4 * j, 4 if j < 5 else 2))
        return p

    def plan_gate(self):
        return [("w_adag", "k8", 512 * j, 512) for j in range(4)]

    def ring_fetch(self, base):
        C = self.C
        while self.ring_emit < min(len(self.plan), base + RING):
            name, kind, a, b = self.plan[self.ring_emit]
            slot = self.ring[self.ring_emit % RING]
            ap, wtile = self.wscr[name]
            rtiles = [wtile]
            q = "sp"
            if self.mode == "C":
                q = "pool"
                rtiles = []
                if name == "w_adag":
                    ap = self.din["w_ada"][:, 2 * D:3 * D] if a < D else self.din["w_ada"][:, 5 * D:6 * D]
                    a = a % D
                else:
                    ap = self.din[name]
            if kind == "k8":
                src = ap[:, a:a + b].rearrange("(kc p) n -> p kc n", p=128)
                dst = slot[:, 0:8 * b].rearrange("p (kc n) -> p kc n", kc=8)
                reads = rtiles
            elif kind == "k4":
                src = ap[:, a:a + b].rearrange("(kc p) n -> p kc n", p=128)
                dst = slot[:, 0:4 * b].rearrange("p (kc n) -> p kc n", kc=4)
                reads = rtiles
            else:
                src = ap[a * 128:(a + b) * 128, :].rearrange("(j p) n -> p j n", p=128)
                dst = slot[:, 0:b * 1024].rearrange("p (j n) -> p j n", j=b)
                reads = rtiles
            C.dma(q, dst, src, extra_reads=[t.v() for t in reads])
            self.ring_emit += 1

    def wget(self, expect, keep=0):
        assert self.plan[self.ring_i] == expect, (self.plan[self.ring_i], expect)
        self.ring_fetch(self.ring_i - keep)
        slot = self.ring[self.ring_i % RING]
        self.ring_i += 1
        return slot

    def setup(self):
        C, nc, din = self.C, self.nc, self.din
        C.dma("pool", self.ident[:], din["c_ident"])
        C.dma("pool", self.tri[:], din["c_tri"])
        C.dma("sp", self.invcnt[:].rearrange("p g t -> p (g t)"), din["c_invcnt"])
        C.dma("sp", self.gq[:], din["g_q"].partition_broadcast(128))
        C.dma("sp", self.gk[:], din["g_k"].partition_broadcast(128))
        C.dma("pool", self.wpool[:], din["w_pool"].rearrange("(g c) d -> c g d", c=128))
        for h in range(8):
            C.dma("pool", self.KT[64:72, h, :], din["c_kext"])
        with nc.allow_non_contiguous_dma(reason="tiny one-time transposed parameter loads"):
            C.dma("sp", self.pscaleT[:], din["pool_scale"].rearrange("o (g c) -> c (o g)", c=128))
            C.dma("sp", self.bconvT[:], din["b_conv"].rearrange("o (j c) -> c (o j)", c=128))
            for t in range(3):
                C.dma("sp", self.wconvT[:, :, t], din["w_conv"][t:t + 1, :].rearrange("o (j c) -> c (o j)", c=128))
            C.dma("sp", self.gmixT[:], din["g_mix"].rearrange("o (j c) -> c (o j)", c=128))
            C.dma("sp", self.gffnT[:], din["g_ffn"].rearrange("o (j c) -> c (o j)", c=128))
            C.dma("sp", self.badaT[:], din["b_ada"].rearrange("o (j c) -> c (o j)", c=128))
            for b in range(6):
                C.dma("sp", self.cT[:, :, b], din["c_all"][b:b + 1, :].rearrange("o (j c) -> c (o j)", c=128))
            for bb in range(NSB):
                for g in range(4):
                    C.dma("sp", self.stH[:, g, bb, :],
                          din["st_pool"][bb * 15:(bb + 1) * 15, g * 128:(g + 1) * 128].rearrange("t c -> c t"))
                for t in range(2):
                    C.dma("sp", self.chs[:, :, bb, t],
                          din["st_conv"][bb * 2 + t:bb * 2 + t + 1, :].rearrange("o (j c) -> c (o j)", c=128))
        C.dma("pool", self.bada_g[:, 0:D], din["b_ada"][:, 2 * D:3 * D])
        C.dma("pool", self.bada_g[:, D:2 * D], din["b_ada"][:, 5 * D:6 * D])
        C.memset("dve", self.ones1[:], 1.0)
        C.memset("dve", self.VE[:, :, :, 64:65], 1.0)
        C.ts("dve", self.gq[:], self.gq[:], 0.125, None, ALU.mult)
        th = self.ct[0][:, 0:48].rearrange("p (j b) -> p j b", j=8)
        C.act(th, self.cT[:], AF.Tanh, scale=0.5)
        C.ts("dve", th, th, 0.5, 0.5, ALU.mult, ALU.add)
        C.tt("dve", self.siluT[:], th, self.cT[:], ALU.mult)
        srcs = {"w_in": din["w_in"], "w_bp": din["w_bp"], "w_ba": din["w_ba"], "w_out": din["w_out"],
                "w_up": din["w_up"], "w_down": din["w_down"]}
        t = self.wscr["w_adag"][1]
        for r in range(8 if self.mode == "A" else 0):
            C.dma("pool", t[r * 128:(r + 1) * 128, 0:D], din["w_ada"][r * 128:(r + 1) * 128, 2 * D:3 * D], no_waw=True)
            C.dma("pool", t[r * 128:(r + 1) * 128, D:2 * D], din["w_ada"][r * 128:(r + 1) * 128, 5 * D:6 * D], no_waw=True)
        t = self.wscr["w_in"][1]
        for r in range(8 if self.mode == "A" else 0):
            C.dma("pool", t[r * 128:(r + 1) * 128, :], din["w_in"][r * 128:(r + 1) * 128, :], no_waw=True)
        for jn, j12 in enumerate((0, 1, 2, 3, 6, 7, 8, 9)):
            slot = self.ring[jn % RING]
            C.dma("pool", slot[:].rearrange("p (kc n) -> p kc n", kc=8),
                  din["w_ada"][:, j12 * 512:(j12 + 1) * 512].rearrange("(kc p) n -> p kc n", p=128))
            wv = slot[:].rearrange("p (kc n) -> p kc n", kc=8)
            bk = self.bank()
            for jj in range(4):
                j = j12 * 4 + jj
                C.mm(bk[:, jj * 8:jj * 8 + 6],
                     [(wv[:, kc, jj * 128:(jj + 1) * 128], self.siluT[:, kc, :]) for kc in range(8)])
            C.tt("dve", self.modT[:, j12 * 4:(j12 + 1) * 4, :],
                 bk[:, 0:32].rearrange("p (j b) -> p j b", j=4)[:, :, 0:6],
                 self.badaT[:, j12 * 4:(j12 + 1) * 4].unsq(2).bcast([128, 4, 6]), ALU.add)
        for name in (("w_bp", "w_ba", "w_out", "w_up", "w_down") if self.mode == "A" else ()):
            t = self.wscr[name][1]
            for r in range(srcs[name].shape[0] // 128):
                C.dma("pool", t[r * 128:(r + 1) * 128, :], srcs[name][r * 128:(r + 1) * 128, :], no_waw=True)
        for s, (gT, mi) in enumerate(((self.gmixT, 1), (self.gffnT, 4))):
            C.ts("dve", self.scl[:, s], self.modT[:, mi * 8:(mi + 1) * 8, :], 1.0, None, ALU.add)
            C.tt("dve", self.scl[:, s], self.scl[:, s], gT[:].unsq(2).bcast([128, 8, 6]), ALU.mult)

    def shiftT(self, s, c, b):
        mi = 0 if s == 0 else 3
        return self.modT[:, mi * 8 + c, b:b + 1]

    def load_gate(self, cols):
        C = self.C
        runs = []
        st = 0
        for i in range(1, len(cols) + 1):
            if i == len(cols) or cols[i] != cols[st]:
                runs.append((st, i, cols[st]))
                st = i
        for (a, b, cb) in runs:
            C.copy("dve", self.silurep[:, :, a:b], self.siluT[:, :, cb:cb + 1].bcast([128, 8, b - a]))
        n = len(cols)
        for j in range(4):
            slot = self.wget(("w_adag", "k8", 512 * j, 512))
            wv = slot[:].rearrange("p (kc n) -> p kc n", kc=8)
            bk = self.bank()
            C.mm(bk[0:n, :], [(self.silurep[:, kc, 0:n], wv[:, kc, :]) for kc in range(8)]
                 + [(self.ones1[:, 0:n], self.bada_g[:, j * 512:(j + 1) * 512])])
            C.ts("dve", self.gate[0:n, j // 2, (j % 2) * 512:(j % 2 + 1) * 512], bk[0:n, :], 0.5, None, ALU.mult)

    def norm_to_hT(self, xs, s, ntile, rows, mod_b=None, mod_tiles=None):
        self.norm_stats(xs, ntile, rows)
        self.norm_transposes(s, ntile, rows, mod_b=mod_b, mod_tiles=mod_tiles)

    def norm_stats(self, xs, ntile, rows):
        C = self.C
        for t in range(ntile):
            C.act(self.xn[0:rows, t, :], xs[0:rows, t, :], AF.Square, accum_out=self.ss[0:rows, t:t + 1])
        C.act(self.rstd[0:rows, 0:ntile], self.ss[0:rows, 0:ntile], AF.Sqrt, scale=1.0 / D, bias=EPS)
        C.recip(self.rstd[0:rows, 0:ntile], self.rstd[0:rows, 0:ntile])
        for t in range(ntile):
            C.ts("dve", self.xn[0:rows, t, :], xs[0:rows, t, :], self.rstd[0:rows, t:t + 1], None, ALU.mult)

    def norm_transposes(self, s, ntile, rows, mod_b=None, mod_tiles=None):
        C = self.C
        for half in range(2):
            bk = self.bank()
            bv = bk[:].bitcast(BF16).rearrange("p (c t n) -> p c t n", c=4, t=2)
            items = []
            for cc in range(4):
                c = half * 4 + cc
                for t in range(ntile):
                    items.append((bv[:, cc, t, 0:rows], self.xn[0:rows, t, c * 128:(c + 1) * 128]))
            C.transposes(items, self.ident)
            for cc in range(4):
                c = half * 4 + cc
                if mod_b is not None:
                    C.act(self.hT[:, c, :].rearrange("p (t n) -> p t n", t=ntile),
                          bv[:, cc, 0:ntile, :], AF.Identity,
                          bias=self.shiftT(s, c, mod_b), scale=self.scl[:, s, c, mod_b:mod_b + 1])
                else:
                    sc_t, sh_t = mod_tiles
                    C.tt("dve", self.ct[0][:, 0:rows], bv[:, cc, 0, 0:rows], sc_t[:, s, c, :], ALU.mult)
                    C.tt("dve", self.hT[:, c, 0:rows], self.ct[0][:, 0:rows], sh_t[:, s, c, :], ALU.add)

    def mixer_front(self, n, first, sample=False):
        C = self.C
        wu = self.wget(("w_in", "k8", 0, 512))[:].rearrange("p (kc n) -> p kc n", kc=8)
        for g in range(4):
            bk = self.bank()
            C.mm(bk[:, 0:n], [(wu[:, kc, g * 128:(g + 1) * 128], self.hT[:, kc, 0:n]) for kc in range(8)])
            if sample:
                C.copy("act", self.uTs[:, g, :, 15:19], bk[:, 0:n].rearrange("p (b t) -> p b t", b=4))
            else:
                C.copy("act", self.uT[:, g, 16:16 + n], bk[:, 0:n])
        L = 16 + n
        for g in range(4):
            if sample:
                cur = self.uTs[:, g]
                for l in range(1, g + 2):
                    sh = 1 << (l - 1)
                    lo = (1 << l) - 1
                    dst = self.ptmp[(l - 1) % 2][:, 0:76].rearrange("p (b t) -> p b t", b=4)
                    C.tt("pool", dst[:, :, lo:19], cur[:, :, lo:19], cur[:, :, lo - sh:19 - sh], ALU.add)
                    cur = dst
                C.ts("pool", cur[:, :, 15:19], cur[:, :, 15:19], 1.0 / (2 << g), None, ALU.mult)
                C.tt("pool", self.pooled[:, g, 0:n].rearrange("p (b t) -> p b t", b=4), cur[:, :, 15:19],
                     self.uTs[:, g, :, 15:19], ALU.subtract)
                continue
            cur = self.uT[:, g, :]
            for l in range(1, g + 2):
                sh = 1 << (l - 1)
                lo = (1 << l) - 1
                dst = self.ptmp[(l - 1) % 2]
                C.tt("dve", dst[:, lo:L], cur[:, lo:L], cur[:, lo - sh:L - sh], ALU.add)
                cur = dst[:]
            w = 2 << g
            if first:
                C.tt("dve", cur[:, 16:32], cur[:, 16:32], self.invcnt[:, g, :], ALU.mult)
                C.ts("dve", cur[:, 32:L], cur[:, 32:L], 1.0 / w, None, ALU.mult)
                C.tt("dve", self.pooled[:, g, 0:n], cur[:, 16:L], self.uT[:, g, 16:L], ALU.subtract)
            else:
                C.ts("dve", cur[:, 16:L], cur[:, 16:L], 1.0 / w, None, ALU.mult)
                C.tt("dve", self.pooled[:, g, 0:n], cur[:, 16:L], self.uT[:, g, 16:L], ALU.subtract)
        for half in range(2):
            wgb = self.wget(("w_in", "k8", 3072 + 512 * half, 512))[:].rearrange("p (kc n) -> p kc n", kc=8)
            for cc in range(4):
                c = half * 4 + cc
                bk = self.bank()
                C.mm(bk[:, 0:n], [(wgb[:, kc, cc * 128:(cc + 1) * 128], self.hT[:, kc, 0:n]) for kc in range(8)])
                C.act(self.thb[:, c, 0:n], bk[:, 0:n], AF.Tanh, scale=0.5)
        for g in range(4):
            bk = self.bank()
            C.mm(bk[:, 0:n], [(self.wpool[:, g, :], self.pooled[:, g, 0:n])])
            C.act(self.ypool[:, g, 0:n], bk[:, 0:n], AF.Copy, scale=self.pscaleT[:, g:g + 1])
        wbp = self.wget(("w_bp", "k4", 0, 1024))[:].rearrange("p (kc n) -> p kc n", kc=4)
        for half in range(2):
            wga = self.wget(("w_in", "k8", 2048 + 512 * half, 512), keep=1 + half)[:].rearrange("p (kc n) -> p kc n", kc=8)
            for cc in range(4):
                c = half * 4 + cc
                bk = self.bank()
                C.mm(bk[:, 0:n], [(wga[:, kc, cc * 128:(cc + 1) * 128], self.hT[:, kc, 0:n]) for kc in range(8)])
                th = self.th[c % 2]
                C.act(th[:, 0:n], bk[:, 0:n], AF.Tanh, scale=0.5)
                bk2 = self.bank()
                C.mm(bk2[:, 0:n], [(wbp[:, kc, c * 128:(c + 1) * 128], self.ypool[:, kc, 0:n]) for kc in range(4)])
                C.stt("dve", self.m[:, c, 0:n], th[:, 0:n], 1.0, bk2[:, 0:n], ALU.add, ALU.mult)

    def qkv_tile(self, t, rows, wq, wk, wv, k_dst, v_dst, v_bf_dst, q_dst=None):
        C = self.C
        bq, bk_, bv = self.bank(), self.bank(), self.bank()
        lhs = [self.hT[:, kc, t * 128:t * 128 + rows] for kc in range(8)]
        C.mm(bq[0:rows, :], [(lhs[kc], wq[:, kc, :]) for kc in range(8)])
        C.mm(bk_[0:rows, :], [(lhs[kc], wk[:, kc, :]) for kc in range(8)])
        C.mm(bv[0:rows, :], [(lhs[kc], wv[:, kc, :]) for kc in range(8)])
        C.copy("act", self.vout[0:rows, :], bv[0:rows, :])
        C.dma("pool", v_dst, self.vout[0:rows, :], is_output=True)
        C.copy("dve", v_bf_dst, self.vout[0:rows, :].rearrange("p (h d) -> p h d", h=8))
        for (bank, gain, which) in ((bk_, self.gk, "k"), (bq, self.gq, "q")):
            C.act(self.sq[0:rows, :], bank[0:rows, :], AF.Square)
            C.reduce("dve", self.ssq[0:rows, :], self.sq[0:rows, :].rearrange("p (h d) -> p h d", h=8), ALU.add)
            C.act(self.rsq[0:rows, :], self.ssq[0:rows, :], AF.Sqrt, scale=1.0 / 64, bias=EPS)
            C.recip(self.rsq[0:rows, :], self.rsq[0:rows, :])
            C.tt("dve", self.ntmp[0:rows, :].rearrange("p (h d) -> p h d", h=8),
                 bank[0:rows, :].rearrange("p (h d) -> p h d", h=8),
                 self.rsq[0:rows, :].unsq(2).bcast([rows, 8, 64]), ALU.mult)
            if which == "k":
                C.tt("dve", self.kout[0:rows, :], self.ntmp[0:rows, :], gain[0:rows, :], ALU.mult)
                C.dma("pool", k_dst, self.kout[0:rows, :], is_output=True)
                C.copy("act", self.kbf[0:rows, :], self.kout[0:rows, :])
            else:
                C.tt("dve", self.qext[0:rows, :, 0:64] if q_dst is None else q_dst,
                     self.ntmp[0:rows, :].rearrange("p (h d) -> p h d", h=8),
                     gain[0:rows, :].rearrange("p (h d) -> p h d", h=8), ALU.mult)

    def mixer_back(self, n, ntile, rows, xs):
        C = self.C
        wba = self.wget(("w_ba", "k4", 0, 1024))[:].rearrange("p (kc n) -> p kc n", kc=4)
        for c in range(8):
            bk = self.bank()
            C.mm(bk[:, 0:n], [(wba[:, kc, c * 128:(c + 1) * 128], self.oT[:, kc, 0:n]) for kc in range(4)])
            mb = self.mb[c % 2]
            C.stt("dve", mb[:, 0:n], self.thb[:, c, 0:n], 1.0, bk[:, 0:n], ALU.add, ALU.mult)
            C.tt("dve", self.merged[:, c, 0:n], mb[:, 0:n], self.m[:, c, 0:n], ALU.add)
        for hf in range(2):
            wo = self.wget(("w_out", "k8", 512 * hf, 512))[:].rearrange("p (kc n) -> p kc n", kc=8)
            for t in range(ntile):
                bk = self.bank()
                C.mm(bk[0:rows, :], [(self.merged[:, kc, t * 128:t * 128 + rows], wo[:, kc, :]) for kc in range(8)])
                rt = self.rtmp[(hf * 2 + t) % 2]
                C.tt("dve", rt[0:rows, :], bk[0:rows, :], self.gate[0:rows, 0, hf * 512:(hf + 1) * 512], ALU.mult)
                C.tt("dve", xs[0:rows, t, hf * 512:(hf + 1) * 512], xs[0:rows, t, hf * 512:(hf + 1) * 512], rt[0:rows, :], ALU.add)

    def ffn(self, n, ntile, rows, xs, hist_cols=None, mid_cb=None, post_cb=None):
        C = self.C
        pend = None

        def stage2(j, bk, t1, t2):
            C.act(t2[:, 0:n], t1[:, 0:n], AF.Tanh, scale=0.5)
            C.stt("dve", t2[:, 0:n], t2[:, 0:n], 1.0, t1[:, 0:n], ALU.add, ALU.mult)
            C.tt("dve", self.actT[:, j, 0:n], t2[:, 0:n], bk[:, 256:256 + n], ALU.mult)

        for jj in range(6):
            ncols = 512 if jj < 5 else 256
            wg = self.wget(("w_up", "k8", 512 * jj, ncols))[:, 0:8 * ncols].rearrange("p (kc n) -> p kc n", kc=8)
            wv = self.wget(("w_up", "k8", DFF + 512 * jj, ncols), keep=1)[:, 0:8 * ncols].rearrange("p (kc n) -> p kc n", kc=8)
            for cc in range(ncols // 128):
                j = jj * 4 + cc
                bk = self.bank()
                C.mm(bk[:, 0:n], [(wg[:, kc, cc * 128:(cc + 1) * 128], self.hT[:, kc, 0:n]) for kc in range(8)])
                C.mm(bk[:, 256:256 + n], [(wv[:, kc, cc * 128:(cc + 1) * 128], self.hT[:, kc, 0:n]) for kc in range(8)])
                gT = self.gT[j % 2]
                t1, t2 = self.ct[(j % 2) * 2], self.ct[(j % 2) * 2 + 1]
                if hist_cols is None:
                    C.copy("pool", gT[:, 0:2], self.chist[:, j, :])
                    C.copy("act", gT[:, 2:2 + n], bk[:, 0:n])
                    C.copy("pool", self.chist[:, j, :], gT[:, n:n + 2])
                    C.act(t1[:, 0:n], gT[:, 0:n], AF.Identity, bias=self.bconvT[:, j:j + 1], scale=self.wconvT[:, j, 0:1])
                    C.stt("dve", t2[:, 0:n], gT[:, 1:1 + n], self.wconvT[:, j, 1:2], t1[:, 0:n], ALU.mult, ALU.add)
                    C.stt("dve", t1[:, 0:n], gT[:, 2:2 + n], self.wconvT[:, j, 2:3], t2[:, 0:n], ALU.mult, ALU.add)
                else:
                    self.sample_conv(j, bk, t1, t2)
                if pend is not None:
                    stage2(*pend)
                pend = (j, bk, t1, t2)
        stage2(*pend)
        if mid_cb is not None:
            mid_cb()
        banks = [[self.bank(hold=True) for t in range(ntile)] for hf in range(2)]
        for jj in range(6):
            nj = 4 if jj < 5 else 2
            wd = self.wget(("w_down", "d4", 4 * jj, nj))[:, 0:nj * 1024].rearrange("p (j n) -> p j n", j=nj)
            for hf in range(2):
                for t in range(ntile):
                    C.mm(banks[hf][t][0:rows, :],
                         [(self.actT[:, jj * 4 + q, t * 128:t * 128 + rows], wd[:, q, hf * 512:(hf + 1) * 512]) for q in range(nj)],
                         start=(jj == 0), stop=(jj == 5))
        if post_cb is not None:
            post_cb()
        for hf in range(2):
            for t in range(ntile):
                bk = banks[hf][t]
                rt = self.rtmp[(hf * 2 + t) % 2]
                C.tt("dve", rt[0:rows, :], bk[0:rows, :], self.gate[0:rows, 1, hf * 512:(hf + 1) * 512], ALU.mult)
                C.tt("dve", xs[0:rows, t, hf * 512:(hf + 1) * 512], xs[0:rows, t, hf * 512:(hf + 1) * 512], rt[0:rows, :], ALU.add)
                self.release(bk)

    def prompt_seq(self, seq):
        C, nc, din, dout = self.C, self.nc, self.din, self.dout
        self.plan += self.plan_gate()
        for qb in range(NG):
            self.plan += self.plan_group(None)
        self.load_gate([seq] * 128)
        C.memset("pool", self.uT[:, :, 0:16], 0.0)
        C.memset("pool", self.chist[:], 0.0)
        for qb in range(NG):
            self.prompt_group(seq, qb)

    def load_x(self, seq, qb):
        r0 = seq * SEQ + qb * G
        xs = self.xs[(seq * NG + qb) % 2]
        self.C.dma("sp", xs[:], self.din["xp"][r0:r0 + G, :].rearrange("(t p) d -> p t d", p=128))
        return xs

    def prompt_group(self, seq, qb):
        C, din, dout = self.C, self.din, self.dout
        r0 = seq * SEQ + qb * G
        if seq == 0 and qb == 0:
            xs = self.load_x(seq, qb)
        else:
            xs = self.xs[(seq * NG + qb) % 2]
        nxt = (seq, qb + 1) if qb + 1 < NG else ((seq + 1, 0) if seq + 1 < NPB else None)
        if nxt is not None:
            self.load_x(*nxt)
        if not self.prenormed:
            self.norm_to_hT(xs, 0, 2, 128, mod_b=seq)
        self.prenormed = False
        self.mixer_front(G, first=(qb == 0))
        if qb == NG - 1:
            with self.nc.allow_non_contiguous_dma(reason="15-row pooling state, transposed store"):
                for g in range(4):
                    C.dma("pool", dout["pool_p"][seq * 15:(seq + 1) * 15, g * 128:(g + 1) * 128].rearrange("t c -> c t"),
                          self.uT[:, g, G + 1:G + 16], is_output=True)
        C.copy("pool", self.uT[:, :, 0:16], self.uT[:, :, G:G + 16])
        wq = self.wget(("w_in", "k8", 512, 512))[:].rearrange("p (kc n) -> p kc n", kc=8)
        wk = self.wget(("w_in", "k8", 1024, 512), keep=1)[:].rearrange("p (kc n) -> p kc n", kc=8)
        wv = self.wget(("w_in", "k8", 1536, 512), keep=2)[:].rearrange("p (kc n) -> p kc n", kc=8)
        for t in range(2):
            kt = 2 * qb + t
            self.qkv_tile(t, 128, wq, wk, wv,
                          dout["k_p"][r0 + t * 128:r0 + (t + 1) * 128, :],
                          dout["v_p"][r0 + t * 128:r0 + (t + 1) * 128, :],
                          self.VE[:, kt, :, 0:64])
            bk = self.bank()
            bv = bk[:].bitcast(BF16).rearrange("p (h n) -> p h n", h=8)
            C.transposes([(bv[0:64, h, :], self.kbf[:, h * 64:(h + 1) * 64]) for h in range(8)], self.ident)
            C.copy("dve", self.KT[0:64, :, kt * 128:(kt + 1) * 128], bv[0:64, :, :])
            C.memset("pool", self.qext[:, :, 64:72], 0.0)
            if qb >= 4:
                bk = self.bank()
                bv = bk[:].bitcast(BF16).rearrange("p (h n) -> p h n", h=8)
                C.transposes([(bv[0:64, h, :], self.qext[:, h, 0:64]) for h in range(8)], self.ident)
                C.copy("act", self.qT64[:], bv[0:64, :, :])
                bk2 = self.bank()
                sv = bk2[:, 0:64].rearrange("p (h b) -> p h b", h=8)
                for h in range(8):
                    C.mm(sv[:, h, 0:qb], [(self.qT64[:, h, :], self.kmT[:, h, 0:qb])])
                C.copy("dve", self.bsc[:, :, 0:qb], sv[:, :, 0:qb])
                S = self.bsc[:, :, 0:qb]
                C.tt("dve", self.cmp[:, :, 0:qb, 0:qb], S.unsq(2).bcast([128, 8, qb, qb]),
                     S.unsq(3).bcast([128, 8, qb, qb]), ALU.is_gt)
                C.reduce("dve", self.rank[:, :, 0:qb], self.cmp[:, :, 0:qb, 0:qb], ALU.add)
                C.ts("dve", self.qext[:, :, 64:64 + qb], self.rank[:, :, 0:qb], 2.5, NEG, ALU.is_ge, ALU.mult)
            bk = self.bank()
            bv = bk[:].bitcast(BF16).rearrange("p (h n) -> p h n", h=8)
            C.transposes([(bv[0:72, h, :], self.qext[:, h, :]) for h in range(8)], self.ident)
            C.copy("act", self.QT[:, :, t * 128:(t + 1) * 128], bv[0:72, :, :])
        if qb < NG - 1:
            C.reduce("dve", self.ntmp[0:64, 0:8], self.KT[0:64, :, qb * G:(qb + 1) * G], ALU.add)
            C.copy("dve", self.kmT[:, :, qb], self.ntmp[0:64, 0:8])
        self.attention(qb)
        self.mixer_back(G, 2, 128, xs)
        self.norm_to_hT(xs, 1, 2, 128, mod_b=seq)
        mid_cb = post_cb = None
        if nxt is not None and nxt[0] == seq:
            nxs = self.xs[(nxt[0] * NG + nxt[1]) % 2]
            mid_cb = lambda: self.norm_stats(nxs, 2, 128)
            post_cb = lambda: self.norm_transposes(0, 2, 128, mod_b=nxt[0])
            self.prenormed = True
        self.ffn(G, 2, 128, xs, mid_cb=mid_cb, post_cb=post_cb)
        C.dma("pool", dout["y_p"][r0:r0 + G, :].rearrange("(t p) d -> p t d", p=128), xs[:], is_output=True)
        if qb == NG - 1:
            with self.nc.allow_non_contiguous_dma(reason="2-row conv state, transposed store"):
                for t in range(2):
                    C.dma("pool", dout["conv_p"][seq * 2 + t:seq * 2 + t + 1, :].rearrange("o (j c) -> c (o j)", c=128),
                          self.chist[:, :, t], is_output=True)

    def attention(self, qb):
        C = self.C
        nkt = 2 * qb + 2
        obanks = [[self.bank(hold=True) for hg in range(2)] for qt in range(2)]
        work = [(h, a, min(a + 2, nkt)) for h in range(8) for a in range(0, nkt, 2)]
        pend_q = []
        for item in work + [None, None]:
            cur = None
            if item is not None:
                h, a, b = item
                bk = self.bank()
                pt = self.PT[(h * 8 + a // 2) % 4]
                for kt in range(a, b):
                    col = (kt - a) * G
                    kview = self.KT[0:72, h, kt * 128:(kt + 1) * 128]
                    if kt < 2 * qb:
                        C.mm(bk[:, col:col + G], [(kview, self.QT[:, h, :])])
                    elif kt == 2 * qb:
                        C.mm(bk[:, col:col + 128], [(kview, self.QT[:, h, 0:128]), (self.ident[:], self.tri[:])])
                        C.mm(bk[:, col + 128:col + 256], [(kview, self.QT[:, h, 128:256])])
                    else:
                        C.mm(bk[:, col + 128:col + 256], [(kview, self.QT[:, h, 128:256]), (self.ident[:], self.tri[:])])
                cur = (h, a, b, bk, pt)
            pend_q.append(cur)
            pend = pend_q.pop(0) if len(pend_q) > 2 else None
            if pend is not None:
                ph, pa, pb, pbk, ppt = pend
                if pb - 1 == 2 * qb + 1:
                    C.act(ppt[:, 0, :], pbk[:, 0:G], AF.Exp)
                    C.act(ppt[:, 1, 128:256], pbk[:, G + 128:G + 256], AF.Exp)
                else:
                    C.act(ppt[:].rearrange("p a n -> p (a n)"), pbk[:, 0:2 * G], AF.Exp)
                for kt in range(pa, pb):
                    for qt in range(2):
                        if kt > 2 * qb + qt:
                            continue
                        ob = obanks[qt][ph // 4]
                        C.mm(ob[:, (ph % 4) * 65:(ph % 4) * 65 + 65],
                             [(ppt[:, kt - pa, qt * 128:(qt + 1) * 128], self.VE[:, kt, ph, :])],
                             start=(kt == 0), stop=(kt == 2 * qb + qt))
        for qt in range(2):
            for hg in range(2):
                ob = obanks[qt][hg]
                ov = ob[:, 0:260].rearrange("p (h d) -> p h d", h=4)
                C.recip(self.rden[:, :], ov[:, :, 64])
                C.tt("dve", self.obf[:, hg * 256:(hg + 1) * 256].rearrange("p (h d) -> p h d", h=4),
                     ov[:, :, 0:64], self.rden[:, :].unsq(2).bcast([128, 4, 64]), ALU.mult)
                self.release(ob)
            bk = self.bank()
            bv = bk[:].bitcast(BF16).rearrange("p (c n) -> p c n", c=8)
            C.transposes([(bv[:, c, :], self.obf[:, c * 128:(c + 1) * 128]) for c in range(4)], self.ident)
            C.copy("act", self.oT[:, :, qt * 128:(qt + 1) * 128], bv[:, 0:4, :])

    def sample_conv(self, j, bk, t1, t2):
        C = self.C
        gTs = self.gT[j % 2][:, 0:24].rearrange("p (b t) -> p b t", b=4)
        v4 = lambda tl: tl[:, 0:16].rearrange("p (b t) -> p b t", b=4)
        C.copy("pool", gTs[:, :, 0:2], self.chs[:, j, :, :])
        C.copy("act", gTs[:, :, 2:6], bk[:, 0:16].rearrange("p (b t) -> p b t", b=4))
        C.copy("pool", self.cso[:, j, :, :], gTs[:, :, 4:6])
        C.ts("dve", v4(t1), gTs[:, :, 0:4], self.wconvT[:, j, 0:1], self.bconvT[:, j:j + 1], ALU.mult, ALU.add)
        C.stt("dve", v4(t2), gTs[:, :, 1:5], self.wconvT[:, j, 1:2], v4(t1), ALU.mult, ALU.add)
        C.stt("dve", v4(t1), gTs[:, :, 2:6], self.wconvT[:, j, 2:3], v4(t2), ALU.mult, ALU.add)

    def carve(self):
        ktf = self.KTf.ap
        self.S_all = Tile("S_all", ktf[:, 0:8256].bitcast(F32).rearrange("p (a n) -> p a n", n=32))
        self.PTs = Tile("PTs", ktf[:, 8256:12384].rearrange("p (a n) -> p a n", n=32))
        self.Dexp = Tile("Dexp", ktf[:, 12384:14432].rearrange("p (k n) -> p k n", n=32))
        self.IDX = Tile("IDX", ktf[:, 14432:15456].bitcast(I32))
        vef = self.VEf.ap
        self.kpg = [Tile("kpg%d" % i, vef[:, i * 512:(i + 1) * 512]) for i in range(6)]
        self.kTp = [Tile("kTp%d" % i, vef[:, 3072 + i * 512:3072 + (i + 1) * 512]) for i in range(3)]
        self.ptI = Tile("ptI", vef[:, 4608:5632].bitcast(I32))
        self.ptF = Tile("ptF", vef[:, 5632:6656].bitcast(F32))
        self.sc_t = Tile("sc_t", vef[:, 6656:7168].bitcast(F32).rearrange("p (s c t) -> p s c t", s=2, c=8))
        self.sh_t = Tile("sh_t", vef[:, 7168:7680].bitcast(F32).rearrange("p (s c t) -> p s c t", s=2, c=8))
        self.Qb = Tile("Qb", vef[:, 7680:8192].rearrange("p (b c n) -> p b c n", b=4, c=4))
        xf = self.xs1f.ap
        pos = [0]

        def f32(name, n, shape=None):
            ap = xf[:, pos[0]:pos[0] + n]
            pos[0] += n
            return Tile(name, ap)

        def bf(name, n):
            ap = xf[:, pos[0]:pos[0] + n // 2].bitcast(BF16)
            pos[0] += n // 2
            return Tile(name, ap)

        self.cso = Tile("cso", f32("cso_", 176).ap.rearrange("p (j b t) -> p j b t", j=NFC, b=4))
        self.qs_bf = bf("qs_bf", 512)
        self.vsb = bf("vsb", 512)
        self.qTs = Tile("qTs", bf("qTs_", 64).ap.rearrange("p (c t) -> p c t", c=4))
        self.kmTs = Tile("kmTs", bf("kmTs_", 256).ap.rearrange("p (c k) -> p c k", c=4))
        self.sc_sb = f32("sc_sb", 64)
        self.top8 = f32("top8", 8)
        self.negb = f32("negb", 64)
        self.omask = bf("omask", 512)
        self.rden_s = f32("rden_s", 2)
        self.ones_col = bf("ones_col", 4)
        self.selb = Tile("selb", bf("selb_", 512).ap.rearrange("p (b k) -> p b k", b=4))
        self.caus = f32("caus", 32)
        self.hmask = f32("hmask", 8)
        self.gsum = bf("gsum", 4)
        self.i32f = f32("i32f", 32)
        self.iota = f32("iota", 2)
        self.ones32 = bf("ones32", 128)
        assert pos[0] <= 2048, pos[0]
        self.uTs = View(self.uT, self.uT.ap[:, :, 0:76].rearrange("p g (b t) -> p g b t", b=4))

    def sample(self):
        C, nc, din, dout = self.C, self.nc, self.din, self.dout
        C.memset("dve", self.xs[1][0:1, 0, 0:1], 0.0)
        C.memset("dve", self.KT[0:1, 0, 0:1], 0.0)
        C.memset("dve", self.VE[0:1, 0, 0, 0:1], 0.0)
        C.barrier()
        self.carve()
        pg = self.plan_group(None)
        if self.mode == "A":
            pg = pg[:9]
        else:
            pg = pg[:6] + pg[9:]
        self.plan += self.plan_gate() + pg
        xs = self.xs[0]
        C.dma("sp", xs[0:16, 0, :], din["xs"])
        self.load_gate([2 + t // 4 for t in range(16)])
        for b in range(4):
            C.copy("dve", self.sc_t[:, :, :, 4 * b:4 * b + 4], self.scl[:, :, :, 2 + b:3 + b].bcast([128, 2, 8, 4]))
            for s, mi in ((0, 0), (1, 3)):
                C.copy("dve", self.sh_t[:, s, :, 4 * b:4 * b + 4],
                       self.modT[:, mi * 8:(mi + 1) * 8, 2 + b:3 + b].bcast([128, 8, 4]))
        self.norm_to_hT(xs, 0, 1, 16, mod_tiles=(self.sc_t, self.sh_t))
        C.copy("pool", self.uTs[:, :, :, 0:15], self.stH[:])
        self.mixer_front(16, first=False, sample=True)
        if self.mode == "A":
            with nc.allow_non_contiguous_dma(reason="15-row pooling state, transposed store"):
                for b in range(4):
                    for g in range(4):
                        C.dma("pool", dout["pool_s"][b * 15:(b + 1) * 15, g * 128:(g + 1) * 128].rearrange("t c -> c t"),
                              self.uTs[:, g, b, 4:19], is_output=True)
        if self.mode == "A":
            wq = self.wget(("w_in", "k8", 512, 512))[:].rearrange("p (kc n) -> p kc n", kc=8)
            wk = self.wget(("w_in", "k8", 1024, 512), keep=1)[:].rearrange("p (kc n) -> p kc n", kc=8)
            wv = self.wget(("w_in", "k8", 1536, 512), keep=2)[:].rearrange("p (kc n) -> p kc n", kc=8)
            self.qkv_tile(0, 16, wq, wk, wv, dout["k_s"], dout["v_s"],
                          self.vsb[0:16, :].rearrange("p (h d) -> p h d", h=8),
                          q_dst=self.rtmp[0][0:16, :].rearrange("p (h d) -> p h d", h=8))
            C.dma("pool", dout["q_s"], self.rtmp[0][0:16, :], is_output=True)
            return
        C.dma("sp", self.rtmp[0][0:16, :], din["o_in"])
        C.copy("dve", self.obf[0:16, :], self.rtmp[0][0:16, :])
        bk = self.bank()
        bv = bk[:].bitcast(BF16).rearrange("p (c n) -> p c n", c=8)
        C.transposes([(bv[:, c, 0:16], self.obf[0:16, c * 128:(c + 1) * 128]) for c in range(4)], self.ident)
        C.copy("act", self.oT[:, :, 0:16], bv[:, 0:4, 0:16])
        self.mixer_back(16, 1, 16, xs)
        self.norm_to_hT(xs, 1, 1, 16, mod_tiles=(self.sc_t, self.sh_t))
        self.ffn(16, 1, 16, xs, hist_cols=True)
        C.dma("pool", dout["y_s"], xs[0:16, 0, :], is_output=True)
        with nc.allow_non_contiguous_dma(reason="2-row conv state, transposed store"):
            for b in range(4):
                for t in range(2):
                    C.dma("pool", dout["conv_s"][b * 2 + t:b * 2 + t + 1, :].rearrange("o (j c) -> c (o j)", c=128),
                          self.cso[:, :, b, t], is_output=True)

    def build(self):
        self.setup()
        if self.mode == "A":
            for seq in range(NPB):
                self.prompt_seq(seq)
        self.sample()
        self.C.finish()


NBS = 32
NB_CORES = 4


def host_consts_B(nb=NBS):
    c = {}
    c["ident"] = np.eye(128, dtype=np.float32)
    sel = np.zeros((4 * nb, nb, 128), np.float32)
    for b in range(nb):
        for s in range(4):
            sel[4 * b + s, b, s] = 1.0
    c["selB"] = sel.reshape(4 * nb, nb * 128)
    hc = host_consts()
    for k in ("caus", "hmask", "gsum", "i32"):
        c[k] = hc[k]
    c["iota"] = (2.0 * (np.arange(128) % 64)).astype(np.float32).reshape(128, 1)
    return c


def B_in_shapes(nb, npool):
    nt = 4 * nb
    return {"cache_k": ([npool * 128, 512], F32), "cache_v": ([npool * 128, 512], F32), "ptab": ([1, nb * NPAGES], I32),
            "q_all": ([nt, 512], F32), "k_all": ([nt, 512], F32), "v_all": ([nt, 512], F32),
            "c_ident": ([128, 128], F32), "c_selB": ([nt, nb * 128], F32), "c_caus": ([128, 32], F32),
            "c_hmask": ([32, 8], F32), "c_gsum": ([32, 4], F32), "c_i32": ([32, 32], F32), "c_iota": ([128, 1], F32)}


class KernB:
    def __init__(self, nc, es, nb=NBS, npool=5120, debug=False):
        self.nc = nc
        self.nb = nb
        C = self.C = Ctx(nc, es)
        self.din = {}
        nt = 4 * nb
        self.nt = nt
        for k, (shp, dt) in B_in_shapes(nb, npool).items():
            self.din[k] = nc.dram_tensor(k, shp, dt, kind="ExternalInput").ap()
        self.o_all = nc.dram_tensor("o_all", [nt, 512], F32, kind="ExternalOutput").ap()
        self.debug = debug
        if debug:
            self.dbg = {k: nc.dram_tensor(k, shp, F32, kind="ExternalOutput").ap() for k, shp in
                        (("d_S", [128, 129 * 32]), ("d_S2", [128, 129 * 32]), ("d_sc", [32, 64]), ("d_negb", [32, 64]),
                         ("d_oacc", [32, 512]), ("d_den", [32, 2]))}
        sb = C.sbuf
        self.ident = sb("ident", [128, 128], BF16)
        self.selB = sb("selB", [nt, nb, 128], BF16)
        self.caus = sb("caus", [128, 32], F32)
        self.hmask = sb("hmask", [32, 8], F32)
        self.gsum = sb("gsum", [32, 4], BF16)
        self.i32f = sb("i32f", [32, 32], F32)
        self.iota = sb("iota", [128, 1], F32)
        self.ones_col = sb("ones_col", [128, 2], BF16)
        self.ones32 = sb("ones32", [32, 128], BF16)
        self.ptI = sb("ptI", [128, nb * NPAGES], I32)
        self.ptF = sb("ptF", [128, nb * NPAGES], F32)
        self.ptB = sb("ptB", [128, nb * 64], F32)
        self.IDX = sb("IDX", [128, nb * 64], I32)
        self.qbf = sb("qbf", [nt, 512], BF16)
        self.kbf = sb("kbf", [nt, 512], BF16)
        self.vbf = sb("vbf", [nt, 512], BF16)
        self.qTs = sb("qTs", [128, 4, nt], BF16)
        self.Qb = sb("Qb", [128, nb, 4, 32], BF16)
        self.S_all = sb("S_all", [128, NPAGES + 1, 32], F32)
        self.PTs = sb("PTs", [128, NPAGES + 1, 32], BF16)
        self.Dexp = sb("Dexp", [32, 64, 32], BF16)
        self.kpg = [sb("kpg%d" % i, [128, 1024], BF16) for i in range(6)]
        self.kTp = [sb("kTp%d" % i, [128, 512], BF16) for i in range(4)]
        self.kmTs = sb("kmTs", [128, 4, 64], BF16)
        self.sc_sb = sb("sc_sb", [32, 64], F32)
        self.top8 = sb("top8", [32, 8], F32)
        self.negb = sb("negb", [32, 64], F32)
        self.omask = sb("omask", [32, 512], BF16)
        self.rden_s = sb("rden_s", [32, 2], F32)
        self.osb = [sb("osb%d" % i, [4, 512], F32) for i in range(2)]
        self.banks = [C.psum("bank%d" % i, [128, 512], F32) for i in range(8)]
        self.held = set()
        self.bank_i = 0

    bank = Kern.bank
    release = Kern.release

    def build(self):
        C, nc, din = self.C, self.nc, self.din
        C.dma("pool", self.ident[:], din["c_ident"])
        C.dma("pool", self.selB[:].rearrange("p b k -> p (b k)"), din["c_selB"])
        C.dma("sp", self.caus[:], din["c_caus"])
        C.dma("sp", self.hmask[:], din["c_hmask"])
        C.dma("pool", self.gsum[:], din["c_gsum"])
        C.dma("sp", self.i32f[:], din["c_i32"])
        C.dma("sp", self.iota[:], din["c_iota"])
        C.dma("pool", self.qbf[:], din["q_all"])
        C.dma("pool", self.kbf[:], din["k_all"])
        C.dma("pool", self.vbf[:], din["v_all"])
        C.memset("dve", self.ones_col[:], 1.0)
        C.memset("dve", self.ones32[:], 1.0)
        C.dma("sp", self.ptI[:], din["ptab"].partition_broadcast(128))
        C.copy("dve", self.ptF[:], self.ptI[:])
        pf3 = self.ptF[:].rearrange("p (k two) -> p k two", two=2)
        for j in range(2):
            C.ts("dve", self.ptB[j * 64:(j + 1) * 64, :], pf3[j * 64:(j + 1) * 64, :, j], 128.0,
                 self.iota[j * 64:(j + 1) * 64, 0:1], ALU.mult, ALU.add)
        C.copy("dve", self.IDX[:], self.ptB[:])
        bk = self.bank()
        bv = bk[:].bitcast(BF16)
        nt = self.nt
        C.transposes([(bv[:, c * nt:(c + 1) * nt], self.qbf[:, c * 128:(c + 1) * 128]) for c in range(4)], self.ident)
        C.copy("dve", self.qTs[:], bv[:, 0:4 * nt].rearrange("p (c t) -> p c t", c=4))
        C.memset("dve", self.Qb[:], 0.0)
        qb5 = self.Qb[:].rearrange("p b c (s h) -> p b c s h", h=8)
        for c in range(4):
            for e in range(2):
                C.copy("pool" if e else "dve", qb5[e * 64:(e + 1) * 64, :, c, :, 2 * c + e],
                       self.qTs[e * 64:(e + 1) * 64, c, :].rearrange("p (b s) -> p b s", s=4))
        for b in range(self.nb):
            self.attn_b(b)
        C.finish()

    def attn_b(self, b):
        C, nc, din = self.C, self.nc, self.din
        kmacc = self.bank(hold=True)
        sbank = None
        sb_state = {"bank": None}

        def qk(lp, kt):
            if lp % 16 == 0:
                sb_state["bank"] = self.bank(hold=True)
            sbank = sb_state["bank"]
            C.mm(sbank[:, (lp % 16) * 32:(lp % 16 + 1) * 32],
                 [(kt[:, c * 128:(c + 1) * 128], self.Qb[:, b, c, :]) for c in range(4)])
            if lp % 16 == 15:
                C.copy("act", self.S_all[:, lp - 15:lp + 1, :], sbank[:, :].rearrange("p (a n) -> p a n", a=16))
                self.release(sbank)

        prev = None
        for lp in range(NPAGES):
            blk, e = lp // 2, lp % 2
            kp = self.kpg[blk % 6]
            if e == 0:
                C.dma("pool", kp[:, :], din["cache_k"], indirect_in=self.IDX[:, b * 64 + blk:b * 64 + blk + 1])
                for c in range(4):
                    col = c * 64 + blk
                    C.mm(kmacc[:, col:col + 1], [(kp[:, ee * 512 + c * 128:ee * 512 + (c + 1) * 128], self.ones_col[:, 0:1])
                                                 for ee in range(2)])
            tb = self.bank()
            tv = tb[:].bitcast(BF16)
            C.transposes([(tv[:, c * 128:(c + 1) * 128], kp[:, e * 512 + c * 128:e * 512 + (c + 1) * 128]) for c in range(4)],
                         self.ident)
            kt = self.kTp[lp % 4]
            C.copy("act" if lp % 2 else "dve", kt[:, :], tv[:, 0:512])
            if prev is not None:
                qk(*prev)
            prev = (lp, kt)
        qk(*prev)
        xb = self.bank()
        for c in range(4):
            C.mm(xb[:, c * 128:(c + 1) * 128], [(self.kbf[:, c * 128:(c + 1) * 128], self.selB[:, b, :])])
        kt = self.kTp[NPAGES % 4]
        C.copy("dve", kt[:, :], xb[:, :])
        sb2 = self.bank()
        C.mm(sb2[:, 0:32], [(kt[:, c * 128:(c + 1) * 128], self.Qb[:, b, c, :]) for c in range(4)])
        C.copy("act", self.S_all[:, NPAGES, :], sb2[:, 0:32])
        if self.debug and b == 0:
            C.dma("sp", self.dbg["d_S"], self.S_all[:].rearrange("p a n -> p (a n)"), is_output=True)
        C.copy("dve", self.kmTs[:], kmacc[:, 0:256].rearrange("p (c k) -> p c k", c=4))
        self.release(kmacc)
        scb = self.bank()
        C.mm(scb[0:32, 0:64], [(self.Qb[:, b, c, :], self.kmTs[:, c, :]) for c in range(4)])
        C.copy("dve", self.sc_sb[:, :], scb[0:32, 0:64])
        C.op("dve", lambda: nc.vector.max(out=self.top8.ap, in_=self.sc_sb.ap), [self.sc_sb.v()], [self.top8.v()])
        C.ts("dve", self.negb[:, :], self.sc_sb[:, :], self.top8[:, 2:3], NEG, ALU.is_lt, ALU.mult)
        C.tt("dve", self.Dexp[:], self.negb[:, :].unsq(2).bcast([32, 64, 32]),
             self.i32f[:, :].unsq(1).bcast([32, 64, 32]), ALU.mult)
        for q in range(4):
            bb = self.bank()
            C.mm(bb[:, :], [(self.ones32[:, :], self.Dexp[:, q * 16:(q + 1) * 16, :].rearrange("p k n -> p (k n)"))])
            Sv = self.S_all[:, q * 32:(q + 1) * 32, :].rearrange("p (k two) n -> p k two n", two=2)
            C.tt("dve", Sv, Sv, bb[:, :].rearrange("p (k n) -> p k n", k=16).unsq(2).bcast([128, 16, 2, 32]), ALU.add)
        C.tt("dve", self.S_all[:, NPAGES, :], self.S_all[:, NPAGES, :], self.caus[:, :], ALU.add)
        if self.debug and b == 0:
            C.dma("sp", self.dbg["d_S2"], self.S_all[:].rearrange("p a n -> p (a n)"), is_output=True)
            C.dma("sp", self.dbg["d_sc"], self.sc_sb[:], is_output=True)
            C.dma("sp", self.dbg["d_negb"], self.negb[:], is_output=True)
        for a in range(0, NPAGES + 1, 43):
            C.act(self.PTs[:, a:a + 43, :], self.S_all[:, a:a + 43, :], AF.Exp)
        oacc = self.bank(hold=True)
        dacc = self.bank(hold=True)
        for lp in range(NPAGES):
            blk, e = lp // 2, lp % 2
            vp = self.kpg[blk % 6]
            if e == 0:
                C.dma("pool", vp[:, :], din["cache_v"], indirect_in=self.IDX[:, b * 64 + blk:b * 64 + blk + 1])
            C.mm(oacc[0:32, :], [(self.PTs[:, lp, :], vp[:, e * 512:(e + 1) * 512])], start=(lp == 0), stop=False)
            C.mm(dacc[0:32, 0:1], [(self.PTs[:, lp, :], self.ones_col[:, 0:1])], start=(lp == 0), stop=False)
        vx = self.bank()
        C.mm(vx[:, :], [(self.selB[:, b, :], self.vbf[:, :])])
        vp = self.kpg[(NPAGES // 2) % 6]
        C.copy("dve", vp[:, 0:512], vx[:, :])
        C.mm(oacc[0:32, :], [(self.PTs[:, NPAGES, :], vp[:, 0:512])], start=False, stop=True)
        C.mm(dacc[0:32, 0:1], [(self.PTs[:, NPAGES, :], self.ones_col[:, 0:1])], start=False, stop=True)
        if self.debug and b == 0:
            C.copy("dve", self.sc_sb[:, 0:2], dacc[0:32, 0:2])
            C.dma("sp", self.dbg["d_den"], self.sc_sb[:, 0:2], is_output=True)
        C.recip(self.rden_s[:, 0:1], dacc[0:32, 0:1])
        C.stt("dve", self.omask[:, :].rearrange("p (h d) -> p h d", h=8),
              oacc[0:32, :].rearrange("p (h d) -> p h d", h=8), self.rden_s[:, 0:1],
              self.hmask[:, :].unsq(2).bcast([32, 8, 64]), ALU.mult, ALU.mult)
        self.release(oacc)
        self.release(dacc)
        ob = self.bank()
        C.mm(ob[0:4, :], [(self.gsum[:, :], self.omask[:, :])])
        osb = self.osb[b % 2]
        C.copy("act", osb[:, :], ob[0:4, :])
        C.dma("sp", self.o_all[4 * b:4 * b + 4, :], osb[:, :], is_output=True)


def build_program_B(nb=NBS, npool=5120, debug=False):
    nc = bass.Bass("TRN2", target_bir_lowering=False)
    es = ExitStack()
    with es:
        k = KernB(nc, es, nb=nb, npool=npool, debug=debug)
        k.build()
    return nc


def build_program(mode="A"):
    nc = bass.Bass("TRN2", target_bir_lowering=False)
    es = ExitStack()
    with es:
        k = Kern(nc, es, mode=mode)
        k.build()
    return nc


_CACHE = {}


def _in_maps(inputs, mode="A"):
    consts = host_consts()
    f = lambda a: np.ascontiguousarray(a, dtype=np.float32)
    maps = []
    shared = {
        "w_ada": f(inputs["w_ada"][0]), "b_ada": f(inputs["b_ada"][0]).reshape(1, -1),
        "g_mix": f(inputs["g_norm_mix"][0]).reshape(1, -1), "w_in": f(inputs["w_in"][0]),
        "g_q": f(inputs["g_q"][0]).reshape(1, -1), "g_k": f(inputs["g_k"][0]).reshape(1, -1),
        "w_pool": f(inputs["w_pool_group"][0]).reshape(512, 128),
        "pool_scale": f(inputs["pool_scale"][0]).reshape(1, -1),
        "w_bp": f(inputs["w_branch_pool"][0]), "w_ba": f(inputs["w_branch_attn"][0]),
        "w_out": f(inputs["w_out"][0]), "g_ffn": f(inputs["g_norm_ffn"][0]).reshape(1, -1),
        "w_up": f(inputs["w_up"][0]), "w_conv": f(inputs["w_conv"][0]),
        "b_conv": f(inputs["b_conv"][0]).reshape(1, -1), "w_down": f(inputs["w_down"][0]),
    }
    for k, v in consts.items():
        shared["c_" + k] = v
    for i in range(8):
        m = dict(shared)
        if mode == "A":
            m["xp"] = f(inputs["x_prompt"][NPB * i:NPB * (i + 1)]).reshape(NPB * SEQ, D)
        m["xs"] = f(inputs["x_sample"][NSB * i:NSB * (i + 1)]).reshape(NST, D)
        m["c_all"] = np.concatenate([f(inputs["c_prompt"][NPB * i:NPB * (i + 1)]),
                                     f(inputs["c_sample"][NSB * i:NSB * (i + 1)])], axis=0)
        m["st_pool"] = f(inputs["state_pool"][0, NSB * i:NSB * (i + 1)]).reshape(NSB * 15, 512)
        m["st_conv"] = f(inputs["state_ffn_conv"][0, NSB * i:NSB * (i + 1)]).reshape(NSB * 2, DFF)
        maps.append(m)
    return maps


def _prog(key, fn):
    if key not in _CACHE:
        _CACHE[key] = fn()
    return _CACHE[key]


def kernel(**inputs):
    cat = lambda res, k: np.concatenate([r[k] for r in res], axis=0)
    maps = _in_maps(inputs, "A")
    ra = run_bass_kernel_spmd(_prog("A", lambda: build_program("A")), maps, core_ids=list(range(8))).results
    q_all, k_all, v_all = cat(ra, "q_s"), cat(ra, "k_s"), cat(ra, "v_s")
    ck = np.ascontiguousarray(inputs["cache_k"][0], dtype=np.float32).reshape(5120 * 128, 512)
    cv = np.ascontiguousarray(inputs["cache_v"][0], dtype=np.float32).reshape(5120 * 128, 512)
    nbc = NBS // NB_CORES
    cb = host_consts_B(nbc)
    mbs = []
    for i in range(NB_CORES):
        r = slice(4 * nbc * i, 4 * nbc * (i + 1))
        mb = {"cache_k": ck, "cache_v": cv,
              "ptab": np.ascontiguousarray(inputs["page_table"][nbc * i:nbc * (i + 1)], dtype=np.int32).reshape(1, -1),
              "q_all": np.ascontiguousarray(q_all[r]), "k_all": np.ascontiguousarray(k_all[r]),
              "v_all": np.ascontiguousarray(v_all[r])}
        for k, v in cb.items():
            mb["c_" + k] = v
        mbs.append(mb)
    rb = run_bass_kernel_spmd(_prog("B", lambda: build_program_B(nb=nbc)), mbs, core_ids=list(range(NB_CORES))).results
    o_all = np.concatenate([r["o_all"] for r in rb], axis=0)
    maps = _in_maps(inputs, "C")
    for i in range(8):
        maps[i]["o_in"] = np.ascontiguousarray(o_all[NST * i:NST * (i + 1)])
    rc = run_bass_kernel_spmd(_prog("C", lambda: build_program("C")), maps, core_ids=list(range(8))).results
    y_p = cat(ra, "y_p").reshape(16, SEQ, D)
    y_s = cat(rc, "y_s").reshape(32, 4, D)
    k_p = cat(ra, "k_p").reshape(1, 16, SEQ, 8, 64)
    v_p = cat(ra, "v_p").reshape(1, 16, SEQ, 8, 64)
    pool_p = cat(ra, "pool_p").reshape(1, 16, 15, 512)
    conv_p = cat(ra, "conv_p").reshape(1, 16, 2, DFF)
    k_s = k_all.reshape(1, 32, 4, 8, 64)
    v_s = v_all.reshape(1, 32, 4, 8, 64)
    pool_s = cat(ra, "pool_s").reshape(1, 32, 15, 512)
    conv_s = cat(rc, "conv_s").reshape(1, 32, 2, DFF)
    return (y_p, y_s, k_p, v_p, pool_p, conv_p, k_s, v_s, pool_s, conv_s)
```

```python
import numpy as np
from contextlib import ExitStack
import concourse.bass as bass
import concourse.mybir as mybir
from concourse.bass_utils import run_bass_kernel_spmd

F32 = mybir.dt.float32
BF16 = mybir.dt.bfloat16
I32 = mybir.dt.int32
AF = mybir.ActivationFunctionType
ALU = mybir.AluOpType
AX = mybir.AxisListType

D = 1024
SEQ = 2048
NPB = 2
NSB = 4
NST = 16
DFF = 2816
NFC = 22
G = 256
NG = SEQ // G
EPS = 1e-6
NEG = -30000.0
NPAGES = 128
RING = 5


class View:
    __slots__ = ("tile", "ap")

    def __init__(self, tile, ap):
        self.tile = tile
        self.ap = ap

    def __getitem__(self, idx):
        return View(self.tile, self.ap[idx])

    def bitcast(self, dt):
        return View(self.tile, self.ap.bitcast(dt))

    def rearrange(self, pattern, **kw):
        return View(self.tile, self.ap.rearrange(pattern, **kw))

    def bcast(self, shape):
        return View(self.tile, self.ap.to_broadcast(list(shape)))

    def unsq(self, ax):
        return View(self.tile, self.ap.unsqueeze(ax))


class Tile:
    def __init__(self, name, ap):
        self.name = name
        self.ap = ap
        self.last_write = None
        self.readers = {}
        self.dsem = None

    def __getitem__(self, idx):
        return View(self, self.ap[idx])

    def v(self, ap=None):
        return View(self, self.ap if ap is None else ap)


class Ctx:
    def __init__(self, nc, es):
        self.nc = nc
        self.es = es
        self.E = {"pe": nc.tensor, "act": nc.scalar, "dve": nc.vector, "pool": nc.gpsimd, "sp": nc.sync}
        self.sems = {}
        self.semval = {}
        self.seen = {e: {} for e in self.E}
        for e in self.E:
            self.sems[e] = es.enter_context(nc.semaphore("sem_" + e))
            self.semval[e] = 0
        self.ndsem = 0
        self.out_events = {}
        self.dsem_pool = {}

    def sbuf(self, name, shape, dtype):
        t = self.es.enter_context(self.nc.sbuf_tensor(name, list(shape), dtype))
        return Tile(name, t[:])

    def psum(self, name, shape, dtype):
        t = self.es.enter_context(self.nc.psum_tensor(name, list(shape), dtype))
        return Tile(name, t[:])

    @staticmethod
    def _tiles(views):
        out = []
        for v in views:
            if v is None:
                continue
            t = v.tile if isinstance(v, View) else (v if isinstance(v, Tile) else None)
            if t is not None and t not in out:
                out.append(t)
        return out

    def _deps(self, rt, wt):
        deps = {}

        def add(d):
            if d is None:
                return
            k, v = d
            if deps.get(k, 0) < v:
                deps[k] = v

        for t in rt:
            add(t.last_write)
        for t in wt:
            add(t.last_write)
            for d in t.readers.values():
                add(d)
        return deps

    def _wait(self, eng, deps, skip_own=False):
        for k, v in deps.items():
            if skip_own and k == eng:
                continue
            if self.seen[eng].get(k, 0) >= v:
                continue
            self.E[eng].wait_ge(self.sems[k], v)
            self.seen[eng][k] = v

    def op(self, eng, fn, reads, writes):
        rt = self._tiles(reads)
        wt = self._tiles(writes)
        deps = self._deps(rt, wt)
        self._wait(eng, deps, skip_own=(eng == "pe"))
        ins = fn()
        self.semval[eng] += 1
        ins.then_inc(self.sems[eng], 1)
        ev = (eng, self.semval[eng])
        for t in rt:
            t.readers[eng] = ev
        for t in wt:
            t.last_write = ev
            t.readers = {}
        return ins

    def barrier(self):
        engs = ["pe", "act", "dve", "pool"]
        for e in engs:
            for f in engs:
                if e == f:
                    continue
                v = self.semval[f]
                if v > 0 and self.seen[e].get(f, 0) < v:
                    self.E[e].wait_ge(self.sems[f], v)
                    self.seen[e][f] = v

    def _dsem(self, t):
        if t.dsem is None:
            k = "d%d" % self.ndsem
            self.ndsem += 1
            self.sems[k] = self.es.enter_context(self.nc.semaphore("sem_" + k))
            self.semval[k] = 0
            t.dsem = k
        return t.dsem

    def dma(self, q, out, in_, is_output=False, indirect_in=None, extra_reads=(), no_waw=False, **kw):
        ot = out.tile if isinstance(out, View) else None
        it = in_.tile if isinstance(in_, View) else None
        rt = ([it] if it else []) + self._tiles(list(extra_reads) + ([indirect_in] if indirect_in is not None else []))
        wt = [ot] if ot else []
        deps = self._deps(rt, [] if no_waw else wt)
        self._wait(q, deps)
        st = ot or it
        k = self._dsem(st)
        self.semval[k] += 16
        oap = out.ap if isinstance(out, View) else out
        iap = in_.ap if isinstance(in_, View) else in_
        if indirect_in is not None:
            ins = self.E[q].indirect_dma_start(
                out=oap, out_offset=None, in_=iap,
                in_offset=bass.IndirectOffsetOnAxis(ap=indirect_in.ap, axis=0), **kw)
        else:
            ins = self.E[q].dma_start(out=oap, in_=iap, **kw)
        ins.then_inc(self.sems[k], 16)
        ev = (k, self.semval[k])
        for t in rt:
            t.readers[k] = ev
        if ot:
            ot.last_write = ev
            ot.readers = {}
        if is_output:
            self.out_events[k] = self.semval[k]
        return ins

    def finish(self, eng="sp"):
        for k, v in self.out_events.items():
            if self.seen[eng].get(k, 0) < v:
                self.E[eng].wait_ge(self.sems[k], v)
                self.seen[eng][k] = v

    def mm(self, out, pairs, start=True, stop=True):
        reads = []
        for l, r in pairs:
            reads += [l, r]
        n = len(pairs)

        def fn():
            ins = None
            for i, (l, r) in enumerate(pairs):
                ins = self.nc.tensor.matmul(out.ap, l.ap, r.ap,
                                            start=(start and i == 0), stop=(stop and i == n - 1))
            return ins

        return self.op("pe", fn, reads, [out])

    def transposes(self, items, ident):
        reads = [ident] + [i for _, i in items]
        writes = [o for o, _ in items]

        def fn():
            ins = None
            for o, i in items:
                k = i.ap.shape[0]
                ins = self.nc.tensor.transpose(o.ap, i.ap, ident.ap[0:k, 0:k])
            return ins

        return self.op("pe", fn, reads, writes)

    def act(self, out, in_, func, bias=None, scale=None, accum_out=None):
        kw = {}
        reads = [in_]
        for nm, val in (("bias", bias), ("scale", scale)):
            if val is None:
                continue
            if isinstance(val, View):
                kw[nm] = val.ap
                reads.append(val)
            else:
                kw[nm] = val
        writes = [out]
        if accum_out is not None:
            kw["accum_out"] = accum_out.ap
            writes.append(accum_out)
        return self.op("act", lambda: self.nc.scalar.activation(out=out.ap, in_=in_.ap, func=func, **kw), reads, writes)

    def _ve(self, eng):
        return self.nc.vector if eng == "dve" else self.nc.gpsimd

    def tt(self, eng, out, in0, in1, op):
        return self.op(eng, lambda: self._ve(eng).tensor_tensor(out=out.ap, in0=in0.ap, in1=in1.ap, op=op), [in0, in1], [out])

    def ts(self, eng, out, in0, s1, s2, op0, op1=None):
        reads = [in0]
        a1, a2 = s1, s2
        if isinstance(s1, View):
            a1 = s1.ap
            reads.append(s1)
        if isinstance(s2, View):
            a2 = s2.ap
            reads.append(s2)
        kw = {}
        if op1 is not None:
            kw["op1"] = op1
        return self.op(eng, lambda: self._ve(eng).tensor_scalar(out=out.ap, in0=in0.ap, scalar1=a1, scalar2=a2, op0=op0, **kw), reads, [out])

    def stt(self, eng, out, in0, scalar, in1, op0, op1):
        reads = [in0, in1]
        a = scalar
        if isinstance(scalar, View):
            a = scalar.ap
            reads.append(scalar)
        return self.op(eng, lambda: self._ve(eng).scalar_tensor_tensor(out=out.ap, in0=in0.ap, scalar=a, in1=in1.ap, op0=op0, op1=op1), reads, [out])

    def copy(self, eng, out, in_):
        if eng == "act":
            return self.op("act", lambda: self.nc.scalar.copy(out=out.ap, in_=in_.ap), [in_], [out])
        return self.op(eng, lambda: self._ve(eng).tensor_copy(out=out.ap, in_=in_.ap), [in_], [out])

    def memset(self, eng, out, val):
        return self.op(eng, lambda: self._ve(eng).memset(out.ap, val), [], [out])

    def reduce(self, eng, out, in_, op, axis=AX.X):
        return self.op(eng, lambda: self._ve(eng).tensor_reduce(out=out.ap, in_=in_.ap, axis=axis, op=op), [in_], [out])

    def recip(self, out, in_):
        return self.op("dve", lambda: self.nc.vector.reciprocal(out=out.ap, in_=in_.ap), [in_], [out])


def host_consts():
    c = {}
    c["ident"] = np.eye(128, dtype=np.float32)
    kk = np.arange(128)[:, None]
    qq = np.arange(128)[None, :]
    c["tri"] = np.where(kk <= qq, 0.0, NEG).astype(np.float32)
    c["kext"] = (np.arange(SEQ)[None, :] // G == np.arange(8)[:, None]).astype(np.float32)
    inv = np.zeros((128, 4, 16), np.float32)
    for g, w in enumerate((2, 4, 8, 16)):
        inv[:, g, :] = 1.0 / np.minimum(w, np.arange(16) + 1)
    c["invcnt"] = inv.reshape(128, 64)
    sel = np.zeros((16, 4, 128), np.float32)
    for b in range(4):
        for s in range(4):
            sel[4 * b + s, b, s] = 1.0
    c["sel"] = sel.reshape(16, 512)
    caus = np.full((128, 32), NEG, np.float32)
    for s in range(4):
        for h in range(8):
            for key in range(4):
                if key <= s:
                    caus[key, s * 8 + h] = 0.0
    c["caus"] = caus
    hm = np.zeros((32, 8), np.float32)
    gs = np.zeros((32, 4), np.float32)
    for s in range(4):
        for h in range(8):
            hm[s * 8 + h, h] = 1.0
            gs[s * 8 + h, s] = 1.0
    c["hmask"] = hm
    c["gsum"] = gs
    c["i32"] = np.eye(32, dtype=np.float32)
    c["iota"] = np.arange(128, dtype=np.float32).reshape(128, 1)
    return c


CONST_SHAPES = {"ident": [128, 128], "tri": [128, 128], "kext": [8, 2048], "invcnt": [128, 64],
                "sel": [16, 512], "caus": [128, 32], "hmask": [32, 8], "gsum": [32, 4], "i32": [32, 32],
                "iota": [128, 1]}

IN_SHAPES = {
    "xp": ([NPB * SEQ, D], F32), "xs": ([NST, D], F32), "c_all": ([NPB + NSB, D], F32),
    "w_ada": ([D, 6 * D], F32), "b_ada": ([1, 6 * D], F32), "g_mix": ([1, D], F32),
    "w_in": ([D, 4096], F32), "g_q": ([1, 512], F32), "g_k": ([1, 512], F32),
    "w_pool": ([512, 128], F32), "pool_scale": ([1, 512], F32),
    "w_bp": ([512, D], F32), "w_ba": ([512, D], F32), "w_out": ([D, D], F32), "g_ffn": ([1, D], F32),
    "w_up": ([D, 2 * DFF], F32), "w_conv": ([3, DFF], F32), "b_conv": ([1, DFF], F32), "w_down": ([DFF, D], F32),
    "st_pool": ([NSB * 15, 512], F32), "st_conv": ([NSB * 2, DFF], F32), "ptab": ([1, NSB * NPAGES], I32),
    "cache_k": ([5120 * 128, 512], F32), "cache_v": ([5120 * 128, 512], F32),
}
A_OUTS = ("y_p", "k_p", "v_p", "pool_p", "conv_p", "k_s", "v_s", "pool_s", "q_s")
C_OUTS = ("y_s", "conv_s")
OUT_SHAPES = {
    "q_s": [NST, 512],
    "y_p": [NPB * SEQ, D], "y_s": [NST, D], "k_p": [NPB * SEQ, 512], "v_p": [NPB * SEQ, 512],
    "pool_p": [NPB * 15, 512], "conv_p": [NPB * 2, DFF], "k_s": [NST, 512], "v_s": [NST, 512],
    "pool_s": [NSB * 15, 512], "conv_s": [NSB * 2, DFF],
}


class Kern:
    def __init__(self, nc, es, mode="A"):
        self.nc = nc
        self.C = Ctx(nc, es)
        self.mode = mode
        self.din = {}
        for k, (shp, dt) in IN_SHAPES.items():
            if k in ("cache_k", "cache_v", "ptab"):
                continue
            if mode == "C" and k == "xp":
                continue
            self.din[k] = nc.dram_tensor(k, shp, dt, kind="ExternalInput").ap()
        if mode == "C":
            self.din["o_in"] = nc.dram_tensor("o_in", [NST, 512], F32, kind="ExternalInput").ap()
        for k, shp in CONST_SHAPES.items():
            self.din["c_" + k] = nc.dram_tensor("c_" + k, shp, F32, kind="ExternalInput").ap()
        outs = A_OUTS if mode == "A" else C_OUTS
        self.dout = {k: nc.dram_tensor(k, OUT_SHAPES[k], F32, kind="ExternalOutput").ap() for k in outs}
        self.wscr = {}
        for k, shp in (("w_in", [D, 4096]), ("w_bp", [512, D]), ("w_ba", [512, D]), ("w_out", [D, D]),
                       ("w_up", [D, 2 * DFF]), ("w_down", [DFF, D]), ("w_adag", [D, 2 * D])):
            ap = nc.dram_tensor("scr_" + k, shp, BF16, kind="Internal").ap()
            self.wscr[k] = (ap, Tile("scr_" + k, ap))
        self.alloc()

    def alloc(self):
        C = self.C
        sb = C.sbuf
        self.ident = sb("ident", [128, 128], BF16)
        self.tri = sb("tri", [128, 128], BF16)
        self.invcnt = sb("invcnt", [128, 4, 16], F32)
        self.gq = sb("gq", [128, 512], F32)
        self.gk = sb("gk", [128, 512], F32)
        self.wpool = sb("wpool", [128, 4, 128], BF16)
        self.pscaleT = sb("pscaleT", [128, 4], F32)
        self.wconvT = sb("wconvT", [128, NFC, 3], F32)
        self.bconvT = sb("bconvT", [128, NFC], F32)
        self.gmixT = sb("gmixT", [128, 8], F32)
        self.gffnT = sb("gffnT", [128, 8], F32)
        self.modT = sb("modT", [128, 48, 6], F32)
        self.scl = sb("scl", [128, 2, 8, 6], F32)
        self.badaT = sb("badaT", [128, 48], F32)
        self.cT = sb("cT", [128, 8, 6], F32)
        self.siluT = sb("siluT", [128, 8, 6], BF16)
        self.silurep = sb("silurep", [128, 8, 128], BF16)
        self.bada_g = sb("bada_g", [1, 2 * D], BF16)
        self.ones1 = sb("ones1", [1, 128], BF16)
        self.gate = sb("gate", [128, 2, D], F32)
        self.KTf = sb("KT", [128, 8 * SEQ], BF16)
        self.KT = Tile("KTv", self.KTf.ap.rearrange("p (h t) -> p h t", h=8))
        self.KT = self.KTf if False else self.KT
        self.stH = sb("stH", [128, 4, 4, 15], F32)
        self.chs = sb("chs", [128, NFC, 4, 2], F32)
        self.VEf = sb("VE", [128, 16 * 8 * 65], BF16)
        self.VE = Tile("VEv", self.VEf.ap.rearrange("p (a h d) -> p a h d", a=16, h=8))
        self.kmT = sb("kmT", [64, 8, 8], BF16)
        self.ring = [sb("ring%d" % i, [128, 4096], BF16) for i in range(RING)]
        self.xs0f = sb("xs0", [128, 2 * D], F32)
        self.xs1f = sb("xs1", [128, 2 * D], F32)
        self.xs = [Tile("xsv%d" % i, f.ap.rearrange("p (t d) -> p t d", t=2)) for i, f in enumerate((self.xs0f, self.xs1f))]
        self.xn = sb("xn", [128, 2, D], BF16)
        self.ss = sb("ss", [128, 2], F32)
        self.rstd = sb("rstd", [128, 2], F32)
        self.hT = sb("hT", [128, 8, G], BF16)
        self.uT = sb("uT", [128, 4, 16 + G], F32)
        self.ptmp = [sb("ptmp%d" % i, [128, 16 + G], F32) for i in range(2)]
        self.pooled = sb("pooled", [128, 4, G], BF16)
        self.ypool = sb("ypool", [128, 4, G], BF16)
        self.th = [sb("th%d" % i, [128, G], F32) for i in range(2)]
        self.m = sb("m", [128, 8, G], BF16)
        self.thb = sb("thb", [128, 8, G], BF16)
        self.vout = sb("vout", [128, 512], F32)
        self.kout = sb("kout", [128, 512], F32)

        self.ntmp = sb("ntmp", [128, 512], F32)
        self.ssq = sb("ssq", [128, 8], F32)
        self.rsq = sb("rsq", [128, 8], F32)
        self.kbf = sb("kbf", [128, 512], BF16)
        self.qext = sb("qext", [128, 8, 72], BF16)
        self.qT64 = sb("qT64", [64, 8, 128], BF16)
        self.bsc = sb("bsc", [128, 8, 8], F32)
        self.cmp = sb("cmp", [128, 8, 8, 8], F32)
        self.rank = sb("rank", [128, 8, 8], F32)
        self.QT = sb("QT", [72, 8, G], BF16)
        self.PT = [sb("PT%d" % i, [128, 2, G], BF16) for i in range(4)]
        self.rden = sb("rden", [128, 4], F32)
        self.obf = sb("obf", [128, 512], BF16)
        self.oT = sb("oT", [128, 4, G], BF16)
        self.mb = [sb("mb%d" % i, [128, G], F32) for i in range(2)]
        self.merged = sb("merged", [128, 8, G], BF16)
        self.rtmp = [sb("rtmp%d" % i, [128, 512], F32) for i in range(2)]
        self.sq = self.rtmp[1]
        self.gT = [sb("gT%d" % i, [128, 2 + G], F32) for i in range(2)]
        self.ct = [sb("ct%d" % i, [128, G], F32) for i in range(4)]
        self.actTs = [sb("actT%d" % i, [128, 4 if i < 5 else 2, G], BF16) for i in range(6)]
        self.chist = sb("chist", [128, NFC, 2], F32)
        self.banks = [C.psum("bank%d" % i, [128, 512], F32) for i in range(8)]
        self.held = set()
        self.bank_i = 0
        self.ring_i = 0
        self.ring_emit = 0
        self.plan = []
        self.prenormed = False

    def bank(self, hold=False):
        for _ in range(16):
            b = self.bank_i % 8
            self.bank_i += 1
            if b not in self.held:
                if hold:
                    self.held.add(b)
                return self.banks[b]
        raise RuntimeError("no free psum bank")

    def release(self, bank):
        self.held.discard(self.banks.index(bank))

    def plan_group(self, ctx):
        p = []
        p.append(("w_in", "k8", 0, 512))
        p.append(("w_in", "k8", 3072, 512))
        p.append(("w_in", "k8", 3584, 512))
        p.append(("w_bp", "k4", 0, 1024))
        p.append(("w_in", "k8", 2048, 512))
        p.append(("w_in", "k8", 2560, 512))
        p.append(("w_in", "k8", 512, 512))
        p.append(("w_in", "k8", 1024, 512))
        p.append(("w_in", "k8", 1536, 512))
        p.append(("w_ba", "k4", 0, 1024))
        p.append(("w_out", "k8", 0, 512))
        p.append(("w_out", "k8", 512, 512))
        for j in range(6):
            n = 512 if j < 5 else 256
            p.append(("w_up", "k8", 512 * j, n))
            p.append(("w_up", "k8", DFF + 512 * j, n))
        for j in range(6):
            p.append(("w_down", "d4", 4 * j, 4 if j < 5 else 2))
        return p

    def plan_gate(self):
        return [("w_adag", "k8", 512 * j, 512) for j in range(4)]

    def ring_fetch(self, base):
        C = self.C
        while self.ring_emit < min(len(self.plan), base + RING):
            name, kind, a, b = self.plan[self.ring_emit]
            slot = self.ring[self.ring_emit % RING]
            ap, wtile = self.wscr[name]
            rtiles = [wtile]
            q = "sp"
            if self.mode == "C":
                q = "pool"
                rtiles = []
                if name == "w_adag":
                    ap = self.din["w_ada"][:, 2 * D:3 * D] if a < D else self.din["w_ada"][:, 5 * D:6 * D]
                    a = a % D
                else:
                    ap = self.din[name]
            if kind == "k8":
                src = ap[:, a:a + b].rearrange("(kc p) n -> p kc n", p=128)
                dst = slot[:, 0:8 * b].rearrange("p (kc n) -> p kc n", kc=8)
                reads = rtiles
            elif kind == "k4":
                src = ap[:, a:a + b].rearrange("(kc p) n -> p kc n", p=128)
                dst = slot[:, 0:4 * b].rearrange("p (kc n) -> p kc n", kc=4)
                reads = rtiles
            else:
                src = ap[a * 128:(a + b) * 128, :].rearrange("(j p) n -> p j n", p=128)
                dst = slot[:, 0:b * 1024].rearrange("p (j n) -> p j n", j=b)
                reads = rtiles
            C.dma(q, dst, src, extra_reads=[t.v() for t in reads])
            self.ring_emit += 1

    def wget(self, expect, keep=0):
        assert self.plan[self.ring_i] == expect, (self.plan[self.ring_i], expect)
        self.ring_fetch(self.ring_i - keep)
        slot = self.ring[self.ring_i % RING]
        self.ring_i += 1
        return slot

    def setup(self):
        C, nc, din = self.C, self.nc, self.din
        C.dma("pool", self.ident[:], din["c_ident"])
        C.dma("pool", self.tri[:], din["c_tri"])
        C.dma("sp", self.invcnt[:].rearrange("p g t -> p (g t)"), din["c_invcnt"])
        C.dma("sp", self.gq[:], din["g_q"].partition_broadcast(128))
        C.dma("sp", self.gk[:], din["g_k"].partition_broadcast(128))
        C.dma("pool", self.wpool[:], din["w_pool"].rearrange("(g c) d -> c g d", c=128))
        for h in range(8):
            C.dma("pool", self.KT[64:72, h, :], din["c_kext"])
        with nc.allow_non_contiguous_dma(reason="tiny one-time transposed parameter loads"):
            C.dma("sp", self.pscaleT[:], din["pool_scale"].rearrange("o (g c) -> c (o g)", c=128))
            C.dma("sp", self.bconvT[:], din["b_conv"].rearrange("o (j c) -> c (o j)", c=128))
            for t in range(3):
                C.dma("sp", self.wconvT[:, :, t], din["w_conv"][t:t + 1, :].rearrange("o (j c) -> c (o j)", c=128))
            C.dma("sp", self.gmixT[:], din["g_mix"].rearrange("o (j c) -> c (o j)", c=128))
            C.dma("sp", self.gffnT[:], din["g_ffn"].rearrange("o (j c) -> c (o j)", c=128))
            C.dma("sp", self.badaT[:], din["b_ada"].rearrange("o (j c) -> c (o j)", c=128))
            for b in range(6):
                C.dma("sp", self.cT[:, :, b], din["c_all"][b:b + 1, :].rearrange("o (j c) -> c (o j)", c=128))
            for bb in range(NSB):
                for g in range(4):
                    C.dma("sp", self.stH[:, g, bb, :],
                          din["st_pool"][bb * 15:(bb + 1) * 15, g * 128:(g + 1) * 128].rearrange("t c -> c t"))
                for t in range(2):
                    C.dma("sp", self.chs[:, :, bb, t],
                          din["st_conv"][bb * 2 + t:bb * 2 + t + 1, :].rearrange("o (j c) -> c (o j)", c=128))
        C.dma("pool", self.bada_g[:, 0:D], din["b_ada"][:, 2 * D:3 * D])
        C.dma("pool", self.bada_g[:, D:2 * D], din["b_ada"][:, 5 * D:6 * D])
        C.memset("dve", self.ones1[:], 1.0)
        C.memset("dve", self.VE[:, :, :, 64:65], 1.0)
        C.ts("dve", self.gq[:], self.gq[:], 0.125, None, ALU.mult)
        th = self.ct[0][:, 0:48].rearrange("p (j b) -> p j b", j=8)
        C.act(th, self.cT[:], AF.Tanh, scale=0.5)
        C.ts("dve", th, th, 0.5, 0.5, ALU.mult, ALU.add)
        C.tt("dve", self.siluT[:], th, self.cT[:], ALU.mult)
        srcs = {"w_in": din["w_in"], "w_bp": din["w_bp"], "w_ba": din["w_ba"], "w_out": din["w_out"],
                "w_up": din["w_up"], "w_down": din["w_down"]}
        t = self.wscr["w_adag"][1]
        for r in range(8 if self.mode == "A" else 0):
            C.dma("pool", t[r * 128:(r + 1) * 128, 0:D], din["w_ada"][r * 128:(r + 1) * 128, 2 * D:3 * D], no_waw=True)
            C.dma("pool", t[r * 128:(r + 1) * 128, D:2 * D], din["w_ada"][r * 128:(r + 1) * 128, 5 * D:6 * D], no_waw=True)
        t = self.wscr["w_in"][1]
        for r in range(8 if self.mode == "A" else 0):
            C.dma("pool", t[r * 128:(r + 1) * 128, :], din["w_in"][r * 128:(r + 1) * 128, :], no_waw=True)
        for jn, j12 in enumerate((0, 1, 2, 3, 6, 7, 8, 9)):
            slot = self.ring[jn % RING]
            C.dma("pool", slot[:].rearrange("p (kc n) -> p kc n", kc=8),
                  din["w_ada"][:, j12 * 512:(j12 + 1) * 512].rearrange("(kc p) n -> p kc n", p=128))
            wv = slot[:].rearrange("p (kc n) -> p kc n", kc=8)
            bk = self.bank()
            for jj in range(4):
                j = j12 * 4 + jj
                C.mm(bk[:, jj * 8:jj * 8 + 6],
                     [(wv[:, kc, jj * 128:(jj + 1) * 128], self.siluT[:, kc, :]) for kc in range(8)])
            C.tt("dve", self.modT[:, j12 * 4:(j12 + 1) * 4, :],
                 bk[:, 0:32].rearrange("p (j b) -> p j b", j=4)[:, :, 0:6],
                 self.badaT[:, j12 * 4:(j12 + 1) * 4].unsq(2).bcast([128, 4, 6]), ALU.add)
        for name in (("w_bp", "w_ba", "w_out", "w_up", "w_down") if self.mode == "A" else ()):
            t = self.wscr[name][1]
            for r in range(srcs[name].shape[0] // 128):
                C.dma("pool", t[r * 128:(r + 1) * 128, :], srcs[name][r * 128:(r + 1) * 128, :], no_waw=True)
        for s, (gT, mi) in enumerate(((self.gmixT, 1), (self.gffnT, 4))):
            C.ts("dve", self.scl[:, s], self.modT[:, mi * 8:(mi + 1) * 8, :], 1.0, None, ALU.add)
            C.tt("dve", self.scl[:, s], self.scl[:, s], gT[:].unsq(2).bcast([128, 8, 6]), ALU.mult)

    def shiftT(self, s, c, b):
        mi = 0 if s == 0 else 3
        return self.modT[:, mi * 8 + c, b:b + 1]

    def load_gate(self, cols):
        C = self.C
        runs = []
        st = 0
        for i in range(1, len(cols) + 1):
            if i == len(cols) or cols[i] != cols[st]:
                runs.append((st, i, cols[st]))
                st = i
        for (a, b, cb) in runs:
            C.copy("dve", self.silurep[:, :, a:b], self.siluT[:, :, cb:cb + 1].bcast([128, 8, b - a]))
        n = len(cols)
        for j in range(4):
            slot = self.wget(("w_adag", "k8", 512 * j, 512))
            wv = slot[:].rearrange("p (kc n) -> p kc n", kc=8)
            bk = self.bank()
            C.mm(bk[0:n, :], [(self.silurep[:, kc, 0:n], wv[:, kc, :]) for kc in range(8)]
                 + [(self.ones1[:, 0:n], self.bada_g[:, j * 512:(j + 1) * 512])])
            C.ts("dve", self.gate[0:n, j // 2, (j % 2) * 512:(j % 2 + 1) * 512], bk[0:n, :], 0.5, None, ALU.mult)

    def norm_to_hT(self, xs, s, ntile, rows, mod_b=None, mod_tiles=None):
        self.norm_stats(xs, ntile, rows)
        self.norm_transposes(s, ntile, rows, mod_b=mod_b, mod_tiles=mod_tiles)

    def norm_stats(self, xs, ntile, rows):
        C = self.C
        for t in range(ntile):
            C.act(self.xn[0:rows, t, :], xs[0:rows, t, :], AF.Square, accum_out=self.ss[0:rows, t:t + 1])
        C.act(self.rstd[0:rows, 0:ntile], self.ss[0:rows, 0:ntile], AF.Sqrt, scale=1.0 / D, bias=EPS)
        C.recip(self.rstd[0:rows, 0:ntile], self.rstd[0:rows, 0:ntile])
        for t in range(ntile):
            C.ts("dve", self.xn[0:rows, t, :], xs[0:rows, t, :], self.rstd[0:rows, t:t + 1], None, ALU.mult)

    def norm_transposes(self, s, ntile, rows, mod_b=None, mod_tiles=None):
        C = self.C
        for half in range(2):
            bk = self.bank()
            bv = bk[:].bitcast(BF16).rearrange("p (c t n) -> p c t n", c=4, t=2)
            items = []
            for cc in range(4):
                c = half * 4 + cc
                for t in range(ntile):
                    items.append((bv[:, cc, t, 0:rows], self.xn[0:rows, t, c * 128:(c + 1) * 128]))
            C.transposes(items, self.ident)
            for cc in range(4):
                c = half * 4 + cc
                if mod_b is not None:
                    C.act(self.hT[:, c, :].rearrange("p (t n) -> p t n", t=ntile),
                          bv[:, cc, 0:ntile, :], AF.Identity,
                          bias=self.shiftT(s, c, mod_b), scale=self.scl[:, s, c, mod_b:mod_b + 1])
                else:
                    sc_t, sh_t = mod_tiles
                    C.tt("dve", self.ct[0][:, 0:rows], bv[:, cc, 0, 0:rows], sc_t[:, s, c, :], ALU.mult)
                    C.tt("dve", self.hT[:, c, 0:rows], self.ct[0][:, 0:rows], sh_t[:, s, c, :], ALU.add)

    def mixer_front(self, n, first, sample=False):
        C = self.C
        wu = self.wget(("w_in", "k8", 0, 512))[:].rearrange("p (kc n) -> p kc n", kc=8)
        for g in range(4):
            bk = self.bank()
            C.mm(bk[:, 0:n], [(wu[:, kc, g * 128:(g + 1) * 128], self.hT[:, kc, 0:n]) for kc in range(8)])
            if sample:
                C.copy("act", self.uTs[:, g, :, 15:19], bk[:, 0:n].rearrange("p (b t) -> p b t", b=4))
            else:
                C.copy("act", self.uT[:, g, 16:16 + n], bk[:, 0:n])
        L = 16 + n
        for g in range(4):
            if sample:
                cur = self.uTs[:, g]
                for l in range(1, g + 2):
                    sh = 1 << (l - 1)
                    lo = (1 << l) - 1
                    dst = self.ptmp[(l - 1) % 2][:, 0:76].rearrange("p (b t) -> p b t", b=4)
                    C.tt("pool", dst[:, :, lo:19], cur[:, :, lo:19], cur[:, :, lo - sh:19 - sh], ALU.add)
                    cur = dst
                C.ts("pool", cur[:, :, 15:19], cur[:, :, 15:19], 1.0 / (2 << g), None, ALU.mult)
                C.tt("pool", self.pooled[:, g, 0:n].rearrange("p (b t) -> p b t", b=4), cur[:, :, 15:19],
                     self.uTs[:, g, :, 15:19], ALU.subtract)
                continue
            cur = self.uT[:, g, :]
            for l in range(1, g + 2):
                sh = 1 << (l - 1)
                lo = (1 << l) - 1
                dst = self.ptmp[(l - 1) % 2]
                C.tt("dve", dst[:, lo:L], cur[:, lo:L], cur[:, lo - sh:L - sh], ALU.add)
                cur = dst[:]
            w = 2 << g
            if first:
                C.tt("dve", cur[:, 16:32], cur[:, 16:32], self.invcnt[:, g, :], ALU.mult)
                C.ts("dve", cur[:, 32:L], cur[:, 32:L], 1.0 / w, None, ALU.mult)
                C.tt("dve", self.pooled[:, g, 0:n], cur[:, 16:L], self.uT[:, g, 16:L], ALU.subtract)
            else:
                C.ts("dve", cur[:, 16:L], cur[:, 16:L], 1.0 / w, None, ALU.mult)
                C.tt("dve", self.pooled[:, g, 0:n], cur[:, 16:L], self.uT[:, g, 16:L], ALU.subtract)
        for half in range(2):
            wgb = self.wget(("w_in", "k8", 3072 + 512 * half, 512))[:].rearrange("p (kc n) -> p kc n", kc=8)
            for cc in range(4):
                c = half * 4 + cc
                bk = self.bank()
                C.mm(bk[:, 0:n], [(wgb[:, kc, cc * 128:(cc + 1) * 128], self.hT[:, kc, 0:n]) for kc in range(8)])
                C.act(self.thb[:, c, 0:n], bk[:, 0:n], AF.Tanh, scale=0.5)
        for g in range(4):
            bk = self.bank()
            C.mm(bk[:, 0:n], [(self.wpool[:, g, :], self.pooled[:, g, 0:n])])
            C.act(self.ypool[:, g, 0:n], bk[:, 0:n], AF.Copy, scale=self.pscaleT[:, g:g + 1])
        wbp = self.wget(("w_bp", "k4", 0, 1024))[:].rearrange("p (kc n) -> p kc n", kc=4)
        for half in range(2):
            wga = self.wget(("w_in", "k8", 2048 + 512 * half, 512), keep=1 + half)[:].rearrange("p (kc n) -> p kc n", kc=8)
            for cc in range(4):
                c = half * 4 + cc
                bk = self.bank()
                C.mm(bk[:, 0:n], [(wga[:, kc, cc * 128:(cc + 1) * 128], self.hT[:, kc, 0:n]) for kc in range(8)])
                th = self.th[c % 2]
                C.act(th[:, 0:n], bk[:, 0:n], AF.Tanh, scale=0.5)
                bk2 = self.bank()
                C.mm(bk2[:, 0:n], [(wbp[:, kc, c * 128:(c + 1) * 128], self.ypool[:, kc, 0:n]) for kc in range(4)])
                C.stt("dve", self.m[:, c, 0:n], th[:, 0:n], 1.0, bk2[:, 0:n], ALU.add, ALU.mult)

    def qkv_tile(self, t, rows, wq, wk, wv, k_dst, v_dst, v_bf_dst, q_dst=None):
        C = self.C
        bq, bk_, bv = self.bank(), self.bank(), self.bank()
        lhs = [self.hT[:, kc, t * 128:t * 128 + rows] for kc in range(8)]
        C.mm(bq[0:rows, :], [(lhs[kc], wq[:, kc, :]) for kc in range(8)])
        C.mm(bk_[0:rows, :], [(lhs[kc], wk[:, kc, :]) for kc in range(8)])
        C.mm(bv[0:rows, :], [(lhs[kc], wv[:, kc, :]) for kc in range(8)])
        C.copy("act", self.vout[0:rows, :], bv[0:rows, :])
        C.dma("pool", v_dst, self.vout[0:rows, :], is_output=True)
        C.copy("dve", v_bf_dst, self.vout[0:rows, :].rearrange("p (h d) -> p h d", h=8))
        for (bank, gain, which) in ((bk_, self.gk, "k"), (bq, self.gq, "q")):
            C.act(self.sq[0:rows, :], bank[0:rows, :], AF.Square)
            C.reduce("dve", self.ssq[0:rows, :], self.sq[0:rows, :].rearrange("p (h d) -> p h d", h=8), ALU.add)
            C.act(self.rsq[0:rows, :], self.ssq[0:rows, :], AF.Sqrt, scale=1.0 / 64, bias=EPS)
            C.recip(self.rsq[0:rows, :], self.rsq[0:rows, :])
            C.tt("dve", self.ntmp[0:rows, :].rearrange("p (h d) -> p h d", h=8),
                 bank[0:rows, :].rearrange("p (h d) -> p h d", h=8),
                 self.rsq[0:rows, :].unsq(2).bcast([rows, 8, 64]), ALU.mult)
            if which == "k":
                C.tt("dve", self.kout[0:rows, :], self.ntmp[0:rows, :], gain[0:rows, :], ALU.mult)
                C.dma("pool", k_dst, self.kout[0:rows, :], is_output=True)
                C.copy("act", self.kbf[0:rows, :], self.kout[0:rows, :])
            else:
                C.tt("dve", self.qext[0:rows, :, 0:64] if q_dst is None else q_dst,
                     self.ntmp[0:rows, :].rearrange("p (h d) -> p h d", h=8),
                     gain[0:rows, :].rearrange("p (h d) -> p h d", h=8), ALU.mult)

    def mixer_back(self, n, ntile, rows, xs):
        C = self.C
        wba = self.wget(("w_ba", "k4", 0, 1024))[:].rearrange("p (kc n) -> p kc n", kc=4)
        for c in range(8):
            bk = self.bank()
            C.mm(bk[:, 0:n], [(wba[:, kc, c * 128:(c + 1) * 128], self.oT[:, kc, 0:n]) for kc in range(4)])
            mb = self.mb[c % 2]
            C.stt("dve", mb[:, 0:n], self.thb[:, c, 0:n], 1.0, bk[:, 0:n], ALU.add, ALU.mult)
            C.tt("dve", self.merged[:, c, 0:n], mb[:, 0:n], self.m[:, c, 0:n], ALU.add)
        for hf in range(2):
            wo = self.wget(("w_out", "k8", 512 * hf, 512))[:].rearrange("p (kc n) -> p kc n", kc=8)
            for t in range(ntile):
                bk = self.bank()
                C.mm(bk[0:rows, :], [(self.merged[:, kc, t * 128:t * 128 + rows], wo[:, kc, :]) for kc in range(8)])
                rt = self.rtmp[(hf * 2 + t) % 2]
                C.tt("dve", rt[0:rows, :], bk[0:rows, :], self.gate[0:rows, 0, hf * 512:(hf + 1) * 512], ALU.mult)
                C.tt("dve", xs[0:rows, t, hf * 512:(hf + 1) * 512], xs[0:rows, t, hf * 512:(hf + 1) * 512], rt[0:rows, :], ALU.add)

    def ffn(self, n, ntile, rows, xs, hist_cols=None, mid_cb=None, post_cb=None):
        C = self.C
        pend = None

        def stage2(j, bk, t1, t2):
            C.act(t2[:, 0:n], t1[:, 0:n], AF.Tanh, scale=0.5)
            C.stt("dve", t2[:, 0:n], t2[:, 0:n], 1.0, t1[:, 0:n], ALU.add, ALU.mult)
            C.tt("dve", self.actTs[j // 4][:, j % 4, 0:n], t2[:, 0:n], bk[:, 256:256 + n], ALU.mult)

        for jj in range(6):
            ncols = 512 if jj < 5 else 256
            wg = self.wget(("w_up", "k8", 512 * jj, ncols))[:, 0:8 * ncols].rearrange("p (kc n) -> p kc n", kc=8)
            wv = self.wget(("w_up", "k8", DFF + 512 * jj, ncols), keep=1)[:, 0:8 * ncols].rearrange("p (kc n) -> p kc n", kc=8)
            for cc in range(ncols // 128):
                j = jj * 4 + cc
                bk = self.bank()
                C.mm(bk[:, 0:n], [(wg[:, kc, cc * 128:(cc + 1) * 128], self.hT[:, kc, 0:n]) for kc in range(8)])
                C.mm(bk[:, 256:256 + n], [(wv[:, kc, cc * 128:(cc + 1) * 128], self.hT[:, kc, 0:n]) for kc in range(8)])
                gT = self.gT[j % 2]
                t1, t2 = self.ct[(j % 2) * 2], self.ct[(j % 2) * 2 + 1]
                if hist_cols is None:
                    C.copy("pool", gT[:, 0:2], self.chist[:, j, :])
                    C.copy("act", gT[:, 2:2 + n], bk[:, 0:n])
                    C.copy("pool", self.chist[:, j, :], gT[:, n:n + 2])
                    C.act(t1[:, 0:n], gT[:, 0:n], AF.Identity, bias=self.bconvT[:, j:j + 1], scale=self.wconvT[:, j, 0:1])
                    C.stt("dve", t2[:, 0:n], gT[:, 1:1 + n], self.wconvT[:, j, 1:2], t1[:, 0:n], ALU.mult, ALU.add)
                    C.stt("dve", t1[:, 0:n], gT[:, 2:2 + n], self.wconvT[:, j, 2:3], t2[:, 0:n], ALU.mult, ALU.add)
                else:
                    self.sample_conv(j, bk, t1, t2)
                if pend is not None:
                    stage2(*pend)
                pend = (j, bk, t1, t2)
        stage2(*pend)
        if mid_cb is not None:
            mid_cb()
        banks = [[self.bank(hold=True) for t in range(ntile)] for hf in range(2)]
        for jj in range(6):
            nj = 4 if jj < 5 else 2
            wd = self.wget(("w_down", "d4", 4 * jj, nj))[:, 0:nj * 1024].rearrange("p (j n) -> p j n", j=nj)
            for hf in range(2):
                for t in range(ntile):
                    C.mm(banks[hf][t][0:rows, :],
                         [(self.actTs[jj][:, q, t * 128:t * 128 + rows], wd[:, q, hf * 512:(hf + 1) * 512]) for q in range(nj)],
                         start=(jj == 0), stop=(jj == 5))
        if post_cb is not None:
            post_cb()
        for hf in range(2):
            for t in range(ntile):
                bk = banks[hf][t]
                rt = self.rtmp[(hf * 2 + t) % 2]
                C.tt("dve", rt[0:rows, :], bk[0:rows, :], self.gate[0:rows, 1, hf * 512:(hf + 1) * 512], ALU.mult)
                C.tt("dve", xs[0:rows, t, hf * 512:(hf + 1) * 512], xs[0:rows, t, hf * 512:(hf + 1) * 512], rt[0:rows, :], ALU.add)
                self.release(bk)

    def prompt_seq(self, seq):
        C, nc, din, dout = self.C, self.nc, self.din, self.dout
        self.plan += self.plan_gate()
        for qb in range(NG):
            self.plan += self.plan_group(None)
        self.load_gate([seq] * 128)
        C.memset("pool", self.uT[:, :, 0:16], 0.0)
        C.memset("pool", self.chist[:], 0.0)
        for qb in range(NG):
            self.prompt_group(seq, qb)

    def load_x(self, seq, qb):
        r0 = seq * SEQ + qb * G
        xs = self.xs[(seq * NG + qb) % 2]
        self.C.dma("sp", xs[:], self.din["xp"][r0:r0 + G, :].rearrange("(t p) d -> p t d", p=128))
        return xs

    def prompt_group(self, seq, qb):
        C, din, dout = self.C, self.din, self.dout
        r0 = seq * SEQ + qb * G
        if seq == 0 and qb == 0:
            xs = self.load_x(seq, qb)
        else:
            xs = self.xs[(seq * NG + qb) % 2]
        nxt = (seq, qb + 1) if qb + 1 < NG else ((seq + 1, 0) if seq + 1 < NPB else None)
        if nxt is not None:
            self.load_x(*nxt)
        if not self.prenormed:
            self.norm_to_hT(xs, 0, 2, 128, mod_b=seq)
        self.prenormed = False
        self.mixer_front(G, first=(qb == 0))
        if qb == NG - 1:
            with self.nc.allow_non_contiguous_dma(reason="15-row pooling state, transposed store"):
                for g in range(4):
                    C.dma("pool", dout["pool_p"][seq * 15:(seq + 1) * 15, g * 128:(g + 1) * 128].rearrange("t c -> c t"),
                          self.uT[:, g, G + 1:G + 16], is_output=True)
        C.copy("pool", self.uT[:, :, 0:16], self.uT[:, :, G:G + 16])
        wq = self.wget(("w_in", "k8", 512, 512))[:].rearrange("p (kc n) -> p kc n", kc=8)
        wk = self.wget(("w_in", "k8", 1024, 512), keep=1)[:].rearrange("p (kc n) -> p kc n", kc=8)
        wv = self.wget(("w_in", "k8", 1536, 512), keep=2)[:].rearrange("p (kc n) -> p kc n", kc=8)
        for t in range(2):
            kt = 2 * qb + t
            self.qkv_tile(t, 128, wq, wk, wv,
                          dout["k_p"][r0 + t * 128:r0 + (t + 1) * 128, :],
                          dout["v_p"][r0 + t * 128:r0 + (t + 1) * 128, :],
                          self.VE[:, kt, :, 0:64])
            bk = self.bank()
            bv = bk[:].bitcast(BF16).rearrange("p (h n) -> p h n", h=8)
            C.transposes([(bv[0:64, h, :], self.kbf[:, h * 64:(h + 1) * 64]) for h in range(8)], self.ident)
            C.copy("dve", self.KT[0:64, :, kt * 128:(kt + 1) * 128], bv[0:64, :, :])
            C.memset("pool", self.qext[:, :, 64:72], 0.0)
            if qb >= 4:
                bk = self.bank()
                bv = bk[:].bitcast(BF16).rearrange("p (h n) -> p h n", h=8)
                C.transposes([(bv[0:64, h, :], self.qext[:, h, 0:64]) for h in range(8)], self.ident)
                C.copy("act", self.qT64[:], bv[0:64, :, :])
                bk2 = self.bank()
                sv = bk2[:, 0:64].rearrange("p (h b) -> p h b", h=8)
                for h in range(8):
                    C.mm(sv[:, h, 0:qb], [(self.qT64[:, h, :], self.kmT[:, h, 0:qb])])
                C.copy("dve", self.bsc[:, :, 0:qb], sv[:, :, 0:qb])
                S = self.bsc[:, :, 0:qb]
                C.tt("dve", self.cmp[:, :, 0:qb, 0:qb], S.unsq(2).bcast([128, 8, qb, qb]),
                     S.unsq(3).bcast([128, 8, qb, qb]), ALU.is_gt)
                C.reduce("dve", self.rank[:, :, 0:qb], self.cmp[:, :, 0:qb, 0:qb], ALU.add)
                C.ts("dve", self.qext[:, :, 64:64 + qb], self.rank[:, :, 0:qb], 2.5, NEG, ALU.is_ge, ALU.mult)
            bk = self.bank()
            bv = bk[:].bitcast(BF16).rearrange("p (h n) -> p h n", h=8)
            C.transposes([(bv[0:72, h, :], self.qext[:, h, :]) for h in range(8)], self.ident)
            C.copy("act", self.QT[:, :, t * 128:(t + 1) * 128], bv[0:72, :, :])
        if qb < NG - 1:
            C.reduce("dve", self.ntmp[0:64, 0:8], self.KT[0:64, :, qb * G:(qb + 1) * G], ALU.add)
            C.copy("dve", self.kmT[:, :, qb], self.ntmp[0:64, 0:8])
        self.attention(qb)
        self.mixer_back(G, 2, 128, xs)
        self.norm_to_hT(xs, 1, 2, 128, mod_b=seq)
        mid_cb = post_cb = None
        if nxt is not None and nxt[0] == seq:
            nxs = self.xs[(nxt[0] * NG + nxt[1]) % 2]
            mid_cb = lambda: self.norm_stats(nxs, 2, 128)
            post_cb = lambda: self.norm_transposes(0, 2, 128, mod_b=nxt[0])
            self.prenormed = True
        self.ffn(G, 2, 128, xs, mid_cb=mid_cb, post_cb=post_cb)
        C.dma("pool", dout["y_p"][r0:r0 + G, :].rearrange("(t p) d -> p t d", p=128), xs[:], is_output=True)
        if qb == NG - 1:
            with self.nc.allow_non_contiguous_dma(reason="2-row conv state, transposed store"):
                for t in range(2):
                    C.dma("pool", dout["conv_p"][seq * 2 + t:seq * 2 + t + 1, :].rearrange("o (j c) -> c (o j)", c=128),
                          self.chist[:, :, t], is_output=True)

    def attention(self, qb):
        C = self.C
        nkt = 2 * qb + 2
        obanks = [[self.bank(hold=True) for hg in range(2)] for qt in range(2)]
        work = [(h, a, min(a + 2, nkt)) for h in range(8) for a in range(0, nkt, 2)]
        pend_q = []
        for item in work + [None, None]:
            cur = None
            if item is not None:
                h, a, b = item
                bk = self.bank()
                pt = self.PT[(h * 8 + a // 2) % 4]
                for kt in range(a, b):
                    col = (kt - a) * G
                    kview = self.KT[0:72, h, kt * 128:(kt + 1) * 128]
                    if kt < 2 * qb:
                        C.mm(bk[:, col:col + G], [(kview, self.QT[:, h, :])])
                    elif kt == 2 * qb:
                        C.mm(bk[:, col:col + 128], [(kview, self.QT[:, h, 0:128]), (self.ident[:], self.tri[:])])
                        C.mm(bk[:, col + 128:col + 256], [(kview, self.QT[:, h, 128:256])])
                    else:
                        C.mm(bk[:, col + 128:col + 256], [(kview, self.QT[:, h, 128:256]), (self.ident[:], self.tri[:])])
                cur = (h, a, b, bk, pt)
            pend_q.append(cur)
            pend = pend_q.pop(0) if len(pend_q) > 2 else None
            if pend is not None:
                ph, pa, pb, pbk, ppt = pend
                if pb - 1 == 2 * qb + 1:
                    C.act(ppt[:, 0, :], pbk[:, 0:G], AF.Exp)
                    C.act(ppt[:, 1, 128:256], pbk[:, G + 128:G + 256], AF.Exp)
                else:
                    C.act(ppt[:].rearrange("p a n -> p (a n)"), pbk[:, 0:2 * G], AF.Exp)
                for kt in range(pa, pb):
                    for qt in range(2):
                        if kt > 2 * qb + qt:
                            continue
                        ob = obanks[qt][ph // 4]
                        C.mm(ob[:, (ph % 4) * 65:(ph % 4) * 65 + 65],
                             [(ppt[:, kt - pa, qt * 128:(qt + 1) * 128], self.VE[:, kt, ph, :])],
                             start=(kt == 0), stop=(kt == 2 * qb + qt))
        for qt in range(2):
            for hg in range(2):
                ob = obanks[qt][hg]
                ov = ob[:, 0:260].rearrange("p (h d) -> p h d", h=4)
                C.recip(self.rden[:, :], ov[:, :, 64])
                C.tt("dve", self.obf[:, hg * 256:(hg + 1) * 256].rearrange("p (h d) -> p h d", h=4),
                     ov[:, :, 0:64], self.rden[:, :].unsq(2).bcast([128, 4, 64]), ALU.mult)
                self.release(ob)
            bk = self.bank()
            bv = bk[:].bitcast(BF16).rearrange("p (c n) -> p c n", c=8)
            C.transposes([(bv[:, c, :], self.obf[:, c * 128:(c + 1) * 128]) for c in range(4)], self.ident)
            C.copy("act", self.oT[:, :, qt * 128:(qt + 1) * 128], bv[:, 0:4, :])

    def sample_conv(self, j, bk, t1, t2):
        C = self.C
        gTs = self.gT[j % 2][:, 0:24].rearrange("p (b t) -> p b t", b=4)
        v4 = lambda tl: tl[:, 0:16].rearrange("p (b t) -> p b t", b=4)
        C.copy("pool", gTs[:, :, 0:2], self.chs[:, j, :, :])
        C.copy("act", gTs[:, :, 2:6], bk[:, 0:16].rearrange("p (b t) -> p b t", b=4))
        C.copy("pool", self.cso[:, j, :, :], gTs[:, :, 4:6])
        C.ts("dve", v4(t1), gTs[:, :, 0:4], self.wconvT[:, j, 0:1], self.bconvT[:, j:j + 1], ALU.mult, ALU.add)
        C.stt("dve", v4(t2), gTs[:, :, 1:5], self.wconvT[:, j, 1:2], v4(t1), ALU.mult, ALU.add)
        C.stt("dve", v4(t1), gTs[:, :, 2:6], self.wconvT[:, j, 2:3], v4(t2), ALU.mult, ALU.add)

    def carve(self):
        ktf = self.KTf.ap
        self.S_all = Tile("S_all", ktf[:, 0:8256].bitcast(F32).rearrange("p (a n) -> p a n", n=32))
        self.PTs = Tile("PTs", ktf[:, 8256:12384].rearrange("p (a n) -> p a n", n=32))
        self.Dexp = Tile("Dexp", ktf[:, 12384:14432].rearrange("p (k n) -> p k n", n=32))
        self.IDX = Tile("IDX", ktf[:, 14432:15456].bitcast(I32))
        vef = self.VEf.ap
        self.kpg = [Tile("kpg%d" % i, vef[:, i * 512:(i + 1) * 512]) for i in range(6)]
        self.kTp = [Tile("kTp%d" % i, vef[:, 3072 + i * 512:3072 + (i + 1) * 512]) for i in range(3)]
        self.ptI = Tile("ptI", vef[:, 4608:5632].bitcast(I32))
        self.ptF = Tile("ptF", vef[:, 5632:6656].bitcast(F32))
        self.sc_t = Tile("sc_t", vef[:, 6656:7168].bitcast(F32).rearrange("p (s c t) -> p s c t", s=2, c=8))
        self.sh_t = Tile("sh_t", vef[:, 7168:7680].bitcast(F32).rearrange("p (s c t) -> p s c t", s=2, c=8))
        self.Qb = Tile("Qb", vef[:, 7680:8192].rearrange("p (b c n) -> p b c n", b=4, c=4))
        xf = self.xs1f.ap
        pos = [0]

        def f32(name, n, shape=None):
            ap = xf[:, pos[0]:pos[0] + n]
            pos[0] += n
            return Tile(name, ap)

        def bf(name, n):
            ap = xf[:, pos[0]:pos[0] + n // 2].bitcast(BF16)
            pos[0] += n // 2
            return Tile(name, ap)

        self.cso = Tile("cso", f32("cso_", 176).ap.rearrange("p (j b t) -> p j b t", j=NFC, b=4))
        self.qs_bf = bf("qs_bf", 512)
        self.vsb = bf("vsb", 512)
        self.qTs = Tile("qTs", bf("qTs_", 64).ap.rearrange("p (c t) -> p c t", c=4))
        self.kmTs = Tile("kmTs", bf("kmTs_", 256).ap.rearrange("p (c k) -> p c k", c=4))
        self.sc_sb = f32("sc_sb", 64)
        self.top8 = f32("top8", 8)
        self.negb = f32("negb", 64)
        self.omask = bf("omask", 512)
        self.rden_s = f32("rden_s", 2)
        self.ones_col = bf("ones_col", 4)
        self.selb = Tile("selb", bf("selb_", 512).ap.rearrange("p (b k) -> p b k", b=4))
        self.caus = f32("caus", 32)
        self.hmask = f32("hmask", 8)
        self.gsum = bf("gsum", 4)
        self.i32f = f32("i32f", 32)
        self.iota = f32("iota", 2)
        self.ones32 = bf("ones32", 128)
        assert pos[0] <= 2048, pos[0]
        self.uTs = View(self.uT, self.uT.ap[:, :, 0:76].rearrange("p g (b t) -> p g b t", b=4))

    def sample(self):
        C, nc, din, dout = self.C, self.nc, self.din, self.dout
        C.memset("dve", self.xs[1][0:1, 0, 0:1], 0.0)
        C.memset("dve", self.KT[0:1, 0, 0:1], 0.0)
        C.memset("dve", self.VE[0:1, 0, 0, 0:1], 0.0)
        C.barrier()
        self.carve()
        pg = self.plan_group(None)
        if self.mode == "A":
            pg = pg[:9]
        else:
            pg = pg[:6] + pg[9:]
        self.plan += self.plan_gate() + pg
        xs = self.xs[0]
        C.dma("sp", xs[0:16, 0, :], din["xs"])
        self.load_gate([2 + t // 4 for t in range(16)])
        for b in range(4):
            C.copy("dve", self.sc_t[:, :, :, 4 * b:4 * b + 4], self.scl[:, :, :, 2 + b:3 + b].bcast([128, 2, 8, 4]))
            for s, mi in ((0, 0), (1, 3)):
                C.copy("dve", self.sh_t[:, s, :, 4 * b:4 * b + 4],
                       self.modT[:, mi * 8:(mi + 1) * 8, 2 + b:3 + b].bcast([128, 8, 4]))
        self.norm_to_hT(xs, 0, 1, 16, mod_tiles=(self.sc_t, self.sh_t))
        C.copy("pool", self.uTs[:, :, :, 0:15], self.stH[:])
        self.mixer_front(16, first=False, sample=True)
        if self.mode == "A":
            with nc.allow_non_contiguous_dma(reason="15-row pooling state, transposed store"):
                for b in range(4):
                    for g in range(4):
                        C.dma("pool", dout["pool_s"][b * 15:(b + 1) * 15, g * 128:(g + 1) * 128].rearrange("t c -> c t"),
                              self.uTs[:, g, b, 4:19], is_output=True)
        if self.mode == "A":
            wq = self.wget(("w_in", "k8", 512, 512))[:].rearrange("p (kc n) -> p kc n", kc=8)
            wk = self.wget(("w_in", "k8", 1024, 512), keep=1)[:].rearrange("p (kc n) -> p kc n", kc=8)
            wv = self.wget(("w_in", "k8", 1536, 512), keep=2)[:].rearrange("p (kc n) -> p kc n", kc=8)
            self.qkv_tile(0, 16, wq, wk, wv, dout["k_s"], dout["v_s"],
                          self.vsb[0:16, :].rearrange("p (h d) -> p h d", h=8),
                          q_dst=self.rtmp[0][0:16, :].rearrange("p (h d) -> p h d", h=8))
            C.dma("pool", dout["q_s"], self.rtmp[0][0:16, :], is_output=True)
            return
        C.dma("sp", self.rtmp[0][0:16, :], din["o_in"])
        C.copy("dve", self.obf[0:16, :], self.rtmp[0][0:16, :])
        bk = self.bank()
        bv = bk[:].bitcast(BF16).rearrange("p (c n) -> p c n", c=8)
        C.transposes([(bv[:, c, 0:16], self.obf[0:16, c * 128:(c + 1) * 128]) for c in range(4)], self.ident)
        C.copy("act", self.oT[:, :, 0:16], bv[:, 0:4, 0:16])
        self.mixer_back(16, 1, 16, xs)
        self.norm_to_hT(xs, 1, 1, 16, mod_tiles=(self.sc_t, self.sh_t))
        self.ffn(16, 1, 16, xs, hist_cols=True)
        C.dma("pool", dout["y_s"], xs[0:16, 0, :], is_output=True)
        with nc.allow_non_contiguous_dma(reason="2-row conv state, transposed store"):
            for b in range(4):
                for t in range(2):
                    C.dma("pool", dout["conv_s"][b * 2 + t:b * 2 + t + 1, :].rearrange("o (j c) -> c (o j)", c=128),
                          self.cso[:, :, b, t], is_output=True)

    def build(self):
        self.setup()
        if self.mode == "A":
            for seq in range(NPB):
                self.prompt_seq(seq)
        self.sample()
        self.C.finish()


NBS = 32
NB_CORES = 4


def host_consts_B(nb=NBS):
    c = {}
    c["ident"] = np.eye(128, dtype=np.float32)
    sel = np.zeros((4 * nb, nb, 128), np.float32)
    for b in range(nb):
        for s in range(4):
            sel[4 * b + s, b, s] = 1.0
    c["selB"] = sel.reshape(4 * nb, nb * 128)
    hc = host_consts()
    for k in ("caus", "hmask", "gsum", "i32"):
        c[k] = hc[k]
    c["iota"] = (2.0 * (np.arange(128) % 64)).astype(np.float32).reshape(128, 1)
    return c


def B_in_shapes(nb, npool):
    nt = 4 * nb
    return {"cache_k": ([npool * 128, 512], F32), "cache_v": ([npool * 128, 512], F32), "ptab": ([1, nb * NPAGES], I32),
            "q_all": ([nt, 512], F32), "k_all": ([nt, 512], F32), "v_all": ([nt, 512], F32),
            "c_ident": ([128, 128], F32), "c_selB": ([nt, nb * 128], F32), "c_caus": ([128, 32], F32),
            "c_hmask": ([32, 8], F32), "c_gsum": ([32, 4], F32), "c_i32": ([32, 32], F32), "c_iota": ([128, 1], F32)}


class KernB:
    def __init__(self, nc, es, nb=NBS, npool=5120, debug=False):
        self.nc = nc
        self.nb = nb
        C = self.C = Ctx(nc, es)
        self.din = {}
        nt = 4 * nb
        self.nt = nt
        for k, (shp, dt) in B_in_shapes(nb, npool).items():
            self.din[k] = nc.dram_tensor(k, shp, dt, kind="ExternalInput").ap()
        self.o_all = nc.dram_tensor("o_all", [nt, 512], F32, kind="ExternalOutput").ap()
        self.debug = debug
        if debug:
            self.dbg = {k: nc.dram_tensor(k, shp, F32, kind="ExternalOutput").ap() for k, shp in
                        (("d_S", [128, 129 * 32]), ("d_S2", [128, 129 * 32]), ("d_sc", [32, 64]), ("d_negb", [32, 64]),
                         ("d_oacc", [32, 512]), ("d_den", [32, 2]))}
        sb = C.sbuf
        self.ident = sb("ident", [128, 128], BF16)
        self.selB = sb("selB", [nt, nb, 128], BF16)
        self.caus = sb("caus", [128, 32], F32)
        self.hmask = sb("hmask", [32, 8], F32)
        self.gsum = sb("gsum", [32, 4], BF16)
        self.i32f = sb("i32f", [32, 32], F32)
        self.iota = sb("iota", [128, 1], F32)
        self.ones_col = sb("ones_col", [128, 2], BF16)
        self.ones32 = sb("ones32", [32, 128], BF16)
        self.ptI = sb("ptI", [128, nb * NPAGES], I32)
        self.ptF = sb("ptF", [128, nb * NPAGES], F32)
        self.ptB = sb("ptB", [128, nb * 64], F32)
        self.IDX = sb("IDX", [128, nb * 64], I32)
        self.qbf = sb("qbf", [nt, 512], BF16)
        self.kbf = sb("kbf", [nt, 512], BF16)
        self.vbf = sb("vbf", [nt, 512], BF16)
        self.qTs = sb("qTs", [128, 4, nt], BF16)
        self.Qb = sb("Qb", [128, nb, 4, 32], BF16)
        self.S_all = sb("S_all", [128, NPAGES + 1, 32], F32)
        self.PTs = sb("PTs", [128, NPAGES + 1, 32], BF16)
        self.Dexp = sb("Dexp", [32, 64, 32], BF16)
        self.kpg = [sb("kpg%d" % i, [128, 1024], BF16) for i in range(6)]
        self.kTp = [sb("kTp%d" % i, [128, 512], BF16) for i in range(4)]
        self.kmTs = sb("kmTs", [128, 4, 64], BF16)
        self.sc_sb = sb("sc_sb", [32, 64], F32)
        self.top8 = sb("top8", [32, 8], F32)
        self.negb = sb("negb", [32, 64], F32)
        self.omask = sb("omask", [32, 512], BF16)
        self.rden_s = sb("rden_s", [32, 2], F32)
        self.osb = [sb("osb%d" % i, [4, 512], F32) for i in range(2)]
        self.banks = [C.psum("bank%d" % i, [128, 512], F32) for i in range(8)]
        self.held = set()
        self.bank_i = 0

    bank = Kern.bank
    release = Kern.release

    def build(self):
        C, nc, din = self.C, self.nc, self.din
        C.dma("pool", self.ident[:], din["c_ident"])
        C.dma("pool", self.selB[:].rearrange("p b k -> p (b k)"), din["c_selB"])
        C.dma("sp", self.caus[:], din["c_caus"])
        C.dma("sp", self.hmask[:], din["c_hmask"])
        C.dma("pool", self.gsum[:], din["c_gsum"])
        C.dma("sp", self.i32f[:], din["c_i32"])
        C.dma("sp", self.iota[:], din["c_iota"])
        C.dma("pool", self.qbf[:], din["q_all"])
        C.dma("pool", self.kbf[:], din["k_all"])
        C.dma("pool", self.vbf[:], din["v_all"])
        C.memset("dve", self.ones_col[:], 1.0)
        C.memset("dve", self.ones32[:], 1.0)
        C.dma("sp", self.ptI[:], din["ptab"].partition_broadcast(128))
        C.copy("dve", self.ptF[:], self.ptI[:])
        pf3 = self.ptF[:].rearrange("p (k two) -> p k two", two=2)
        for j in range(2):
            C.ts("dve", self.ptB[j * 64:(j + 1) * 64, :], pf3[j * 64:(j + 1) * 64, :, j], 128.0,
                 self.iota[j * 64:(j + 1) * 64, 0:1], ALU.mult, ALU.add)
        C.copy("dve", self.IDX[:], self.ptB[:])
        bk = self.bank()
        bv = bk[:].bitcast(BF16)
        nt = self.nt
        C.transposes([(bv[:, c * nt:(c + 1) * nt], self.qbf[:, c * 128:(c + 1) * 128]) for c in range(4)], self.ident)
        C.copy("dve", self.qTs[:], bv[:, 0:4 * nt].rearrange("p (c t) -> p c t", c=4))
        C.memset("dve", self.Qb[:], 0.0)
        qb5 = self.Qb[:].rearrange("p b c (s h) -> p b c s h", h=8)
        for c in range(4):
            for e in range(2):
                C.copy("pool" if e else "dve", qb5[e * 64:(e + 1) * 64, :, c, :, 2 * c + e],
                       self.qTs[e * 64:(e + 1) * 64, c, :].rearrange("p (b s) -> p b s", s=4))
        for b in range(self.nb):
            self.attn_b(b)
        C.finish()

    def attn_b(self, b):
        C, nc, din = self.C, self.nc, self.din
        kmacc = self.bank(hold=True)
        sbank = None
        sb_state = {"bank": None}

        def qk(lp, kt):
            if lp % 16 == 0:
                sb_state["bank"] = self.bank(hold=True)
            sbank = sb_state["bank"]
            C.mm(sbank[:, (lp % 16) * 32:(lp % 16 + 1) * 32],
                 [(kt[:, c * 128:(c + 1) * 128], self.Qb[:, b, c, :]) for c in range(4)])
            if lp % 16 == 15:
                C.copy("act", self.S_all[:, lp - 15:lp + 1, :], sbank[:, :].rearrange("p (a n) -> p a n", a=16))
                self.release(sbank)

        prev = None
        for lp in range(NPAGES):
            blk, e = lp // 2, lp % 2
            kp = self.kpg[blk % 6]
            if e == 0:
                C.dma("pool", kp[:, :], din["cache_k"], indirect_in=self.IDX[:, b * 64 + blk:b * 64 + blk + 1])
                for c in range(4):
                    col = c * 64 + blk
                    C.mm(kmacc[:, col:col + 1], [(kp[:, ee * 512 + c * 128:ee * 512 + (c + 1) * 128], self.ones_col[:, 0:1])
                                                 for ee in range(2)])
            tb = self.bank()
            tv = tb[:].bitcast(BF16)
            C.transposes([(tv[:, c * 128:(c + 1) * 128], kp[:, e * 512 + c * 128:e * 512 + (c + 1) * 128]) for c in range(4)],
                         self.ident)
            kt = self.kTp[lp % 4]
            C.copy("act" if lp % 2 else "dve", kt[:, :], tv[:, 0:512])
            if prev is not None:
                qk(*prev)
            prev = (lp, kt)
        qk(*prev)
        xb = self.bank()
        for c in range(4):
            C.mm(xb[:, c * 128:(c + 1) * 128], [(self.kbf[:, c * 128:(c + 1) * 128], self.selB[:, b, :])])
        kt = self.kTp[NPAGES % 4]
        C.copy("dve", kt[:, :], xb[:, :])
        sb2 = self.bank()
        C.mm(sb2[:, 0:32], [(kt[:, c * 128:(c + 1) * 128], self.Qb[:, b, c, :]) for c in range(4)])
        C.copy("act", self.S_all[:, NPAGES, :], sb2[:, 0:32])
        if self.debug and b == 0:
            C.dma("sp", self.dbg["d_S"], self.S_all[:].rearrange("p a n -> p (a n)"), is_output=True)
        C.copy("dve", self.kmTs[:], kmacc[:, 0:256].rearrange("p (c k) -> p c k", c=4))
        self.release(kmacc)
        scb = self.bank()
        C.mm(scb[0:32, 0:64], [(self.Qb[:, b, c, :], self.kmTs[:, c, :]) for c in range(4)])
        C.copy("dve", self.sc_sb[:, :], scb[0:32, 0:64])
        C.op("dve", lambda: nc.vector.max(out=self.top8.ap, in_=self.sc_sb.ap), [self.sc_sb.v()], [self.top8.v()])
        C.ts("dve", self.negb[:, :], self.sc_sb[:, :], self.top8[:, 2:3], NEG, ALU.is_lt, ALU.mult)
        C.tt("dve", self.Dexp[:], self.negb[:, :].unsq(2).bcast([32, 64, 32]),
             self.i32f[:, :].unsq(1).bcast([32, 64, 32]), ALU.mult)
        for q in range(4):
            bb = self.bank()
            C.mm(bb[:, :], [(self.ones32[:, :], self.Dexp[:, q * 16:(q + 1) * 16, :].rearrange("p k n -> p (k n)"))])
            Sv = self.S_all[:, q * 32:(q + 1) * 32, :].rearrange("p (k two) n -> p k two n", two=2)
            C.tt("dve", Sv, Sv, bb[:, :].rearrange("p (k n) -> p k n", k=16).unsq(2).bcast([128, 16, 2, 32]), ALU.add)
        C.tt("dve", self.S_all[:, NPAGES, :], self.S_all[:, NPAGES, :], self.caus[:, :], ALU.add)
        if self.debug and b == 0:
            C.dma("sp", self.dbg["d_S2"], self.S_all[:].rearrange("p a n -> p (a n)"), is_output=True)
            C.dma("sp", self.dbg["d_sc"], self.sc_sb[:], is_output=True)
            C.dma("sp", self.dbg["d_negb"], self.negb[:], is_output=True)
        for a in range(0, NPAGES + 1, 43):
            C.act(self.PTs[:, a:a + 43, :], self.S_all[:, a:a + 43, :], AF.Exp)
        oacc = self.bank(hold=True)
        dacc = self.bank(hold=True)
        for lp in range(NPAGES):
            blk, e = lp // 2, lp % 2
            vp = self.kpg[blk % 6]
            if e == 0:
                C.dma("pool", vp[:, :], din["cache_v"], indirect_in=self.IDX[:, b * 64 + blk:b * 64 + blk + 1])
            C.mm(oacc[0:32, :], [(self.PTs[:, lp, :], vp[:, e * 512:(e + 1) * 512])], start=(lp == 0), stop=False)
            C.mm(dacc[0:32, 0:1], [(self.PTs[:, lp, :], self.ones_col[:, 0:1])], start=(lp == 0), stop=False)
        vx = self.bank()
        C.mm(vx[:, :], [(self.selB[:, b, :], self.vbf[:, :])])
        vp = self.kpg[(NPAGES // 2) % 6]
        C.copy("dve", vp[:, 0:512], vx[:, :])
        C.mm(oacc[0:32, :], [(self.PTs[:, NPAGES, :], vp[:, 0:512])], start=False, stop=True)
        C.mm(dacc[0:32, 0:1], [(self.PTs[:, NPAGES, :], self.ones_col[:, 0:1])], start=False, stop=True)
        if self.debug and b == 0:
            C.copy("dve", self.sc_sb[:, 0:2], dacc[0:32, 0:2])
            C.dma("sp", self.dbg["d_den"], self.sc_sb[:, 0:2], is_output=True)
        C.recip(self.rden_s[:, 0:1], dacc[0:32, 0:1])
        C.stt("dve", self.omask[:, :].rearrange("p (h d) -> p h d", h=8),
              oacc[0:32, :].rearrange("p (h d) -> p h d", h=8), self.rden_s[:, 0:1],
              self.hmask[:, :].unsq(2).bcast([32, 8, 64]), ALU.mult, ALU.mult)
        self.release(oacc)
        self.release(dacc)
        ob = self.bank()
        C.mm(ob[0:4, :], [(self.gsum[:, :], self.omask[:, :])])
        osb = self.osb[b % 2]
        C.copy("act", osb[:, :], ob[0:4, :])
        C.dma("sp", self.o_all[4 * b:4 * b + 4, :], osb[:, :], is_output=True)


def build_program_B(nb=NBS, npool=5120, debug=False):
    nc = bass.Bass("TRN2", target_bir_lowering=False)
    es = ExitStack()
    with es:
        k = KernB(nc, es, nb=nb, npool=npool, debug=debug)
        k.build()
    return nc


def build_program(mode="A"):
    nc = bass.Bass("TRN2", target_bir_lowering=False)
    es = ExitStack()
    with es:
        k = Kern(nc, es, mode=mode)
        k.build()
    return nc


_CACHE = {}


def _in_maps(inputs, mode="A"):
    consts = host_consts()
    f = lambda a: np.ascontiguousarray(a, dtype=np.float32)
    maps = []
    shared = {
        "w_ada": f(inputs["w_ada"][0]), "b_ada": f(inputs["b_ada"][0]).reshape(1, -1),
        "g_mix": f(inputs["g_norm_mix"][0]).reshape(1, -1), "w_in": f(inputs["w_in"][0]),
        "g_q": f(inputs["g_q"][0]).reshape(1, -1), "g_k": f(inputs["g_k"][0]).reshape(1, -1),
        "w_pool": f(inputs["w_pool_group"][0]).reshape(512, 128),
        "pool_scale": f(inputs["pool_scale"][0]).reshape(1, -1),
        "w_bp": f(inputs["w_branch_pool"][0]), "w_ba": f(inputs["w_branch_attn"][0]),
        "w_out": f(inputs["w_out"][0]), "g_ffn": f(inputs["g_norm_ffn"][0]).reshape(1, -1),
        "w_up": f(inputs["w_up"][0]), "w_conv": f(inputs["w_conv"][0]),
        "b_conv": f(inputs["b_conv"][0]).reshape(1, -1), "w_down": f(inputs["w_down"][0]),
    }
    for k, v in consts.items():
        shared["c_" + k] = v
    for i in range(8):
        m = dict(shared)
        if mode == "A":
            m["xp"] = f(inputs["x_prompt"][NPB * i:NPB * (i + 1)]).reshape(NPB * SEQ, D)
        m["xs"] = f(inputs["x_sample"][NSB * i:NSB * (i + 1)]).reshape(NST, D)
        m["c_all"] = np.concatenate([f(inputs["c_prompt"][NPB * i:NPB * (i + 1)]),
                                     f(inputs["c_sample"][NSB * i:NSB * (i + 1)])], axis=0)
        m["st_pool"] = f(inputs["state_pool"][0, NSB * i:NSB * (i + 1)]).reshape(NSB * 15, 512)
        m["st_conv"] = f(inputs["state_ffn_conv"][0, NSB * i:NSB * (i + 1)]).reshape(NSB * 2, DFF)
        maps.append(m)
    return maps


def _prog(key, fn):
    if key not in _CACHE:
        _CACHE[key] = fn()
    return _CACHE[key]


def kernel(**inputs):
    cat = lambda res, k: np.concatenate([r[k] for r in res], axis=0)
    maps = _in_maps(inputs, "A")
    ra = run_bass_kernel_spmd(_prog("A", lambda: build_program("A")), maps, core_ids=list(range(8))).results
    q_all, k_all, v_all = cat(ra, "q_s"), cat(ra, "k_s"), cat(ra, "v_s")
    ck = np.ascontiguousarray(inputs["cache_k"][0], dtype=np.float32).reshape(5120 * 128, 512)
    cv = np.ascontiguousarray(inputs["cache_v"][0], dtype=np.float32).reshape(5120 * 128, 512)
    nbc = NBS // NB_CORES
    cb = host_consts_B(nbc)
    mbs = []
    for i in range(NB_CORES):
        r = slice(4 * nbc * i, 4 * nbc * (i + 1))
        mb = {"cache_k": ck, "cache_v": cv,
              "ptab": np.ascontiguousarray(inputs["page_table"][nbc * i:nbc * (i + 1)], dtype=np.int32).reshape(1, -1),
              "q_all": np.ascontiguousarray(q_all[r]), "k_all": np.ascontiguousarray(k_all[r]),
              "v_all": np.ascontiguousarray(v_all[r])}
        for k, v in cb.items():
            mb["c_" + k] = v
        mbs.append(mb)
    rb = run_bass_kernel_spmd(_prog("B", lambda: build_program_B(nb=nbc)), mbs, core_ids=list(range(NB_CORES))).results
    o_all = np.concatenate([r["o_all"] for r in rb], axis=0)
    maps = _in_maps(inputs, "C")
    for i in range(8):
        maps[i]["o_in"] = np.ascontiguousarray(o_all[NST * i:NST * (i + 1)])
    rc = run_bass_kernel_spmd(_prog("C", lambda: build_program("C")), maps, core_ids=list(range(8))).results
    y_p = cat(ra, "y_p").reshape(16, SEQ, D)
    y_s = cat(rc, "y_s").reshape(32, 4, D)
    k_p = cat(ra, "k_p").reshape(1, 16, SEQ, 8, 64)
    v_p = cat(ra, "v_p").reshape(1, 16, SEQ, 8, 64)
    pool_p = cat(ra, "pool_p").reshape(1, 16, 15, 512)
    conv_p = cat(ra, "conv_p").reshape(1, 16, 2, DFF)
    k_s = k_all.reshape(1, 32, 4, 8, 64)
    v_s = v_all.reshape(1, 32, 4, 8, 64)
    pool_s = cat(ra, "pool_s").reshape(1, 32, 15, 512)
    conv_s = cat(rc, "conv_s").reshape(1, 32, 2, DFF)
    return (y_p, y_s, k_p, v_p, pool_p, conv_p, k_s, v_s, pool_s, conv_s)
```

```python
import numpy as np
from contextlib import ExitStack
import concourse.bass as bass
import concourse.mybir as mybir
from concourse.bass_utils import run_bass_kernel_spmd

F32 = mybir.dt.float32
BF16 = mybir.dt.bfloat16
I32 = mybir.dt.int32
AF = mybir.ActivationFunctionType
ALU = mybir.AluOpType
AX = mybir.AxisListType

D = 1024
SEQ = 2048
NPB = 2
NSB = 4
NST = 16
DFF = 2816
NFC = 22
G = 256
NG = SEQ // G
EPS = 1e-6
NEG = -30000.0
NPAGES = 128
RING = 5


class View:
    __slots__ = ("tile", "ap")

    def __init__(self, tile, ap):
        self.tile = tile
        self.ap = ap

    def __getitem__(self, idx):
        return View(self.tile, self.ap[idx])

    def bitcast(self, dt):
        return View(self.tile, self.ap.bitcast(dt))

    def rearrange(self, pattern, **kw):
        return View(self.tile, self.ap.rearrange(pattern, **kw))

    def bcast(self, shape):
        return View(self.tile, self.ap.to_broadcast(list(shape)))

    def unsq(self, ax):
        return View(self.tile, self.ap.unsqueeze(ax))


class Tile:
    def __init__(self, name, ap):
        self.name = name
        self.ap = ap
        self.last_write = None
        self.readers = {}
        self.dsem = None

    def __getitem__(self, idx):
        return View(self, self.ap[idx])

    def v(self, ap=None):
        return View(self, self.ap if ap is None else ap)


class Ctx:
    def __init__(self, nc, es):
        self.nc = nc
        self.es = es
        self.E = {"pe": nc.tensor, "act": nc.scalar, "dve": nc.vector, "pool": nc.gpsimd, "sp": nc.sync}
        self.sems = {}
        self.semval = {}
        self.seen = {e: {} for e in self.E}
        for e in self.E:
            self.sems[e] = es.enter_context(nc.semaphore("sem_" + e))
            self.semval[e] = 0
        self.ndsem = 0
        self.out_events = {}
        self.dsem_pool = {}

    def sbuf(self, name, shape, dtype):
        t = self.es.enter_context(self.nc.sbuf_tensor(name, list(shape), dtype))
        return Tile(name, t[:])

    def psum(self, name, shape, dtype):
        t = self.es.enter_context(self.nc.psum_tensor(name, list(shape), dtype))
        return Tile(name, t[:])

    @staticmethod
    def _tiles(views):
        out = []
        for v in views:
            if v is None:
                continue
            t = v.tile if isinstance(v, View) else (v if isinstance(v, Tile) else None)
            if t is not None and t not in out:
                out.append(t)
        return out

    def _deps(self, rt, wt):
        deps = {}

        def add(d):
            if d is None:
                return
            k, v = d
            if deps.get(k, 0) < v:
                deps[k] = v

        for t in rt:
            add(t.last_write)
        for t in wt:
            add(t.last_write)
            for d in t.readers.values():
                add(d)
        return deps

    def _wait(self, eng, deps, skip_own=False):
        for k, v in deps.items():
            if skip_own and k == eng:
                continue
            if self.seen[eng].get(k, 0) >= v:
                continue
            self.E[eng].wait_ge(self.sems[k], v)
            self.seen[eng][k] = v

    def op(self, eng, fn, reads, writes):
        rt = self._tiles(reads)
        wt = self._tiles(writes)
        deps = self._deps(rt, wt)
        self._wait(eng, deps, skip_own=(eng == "pe"))
        ins = fn()
        self.semval[eng] += 1
        ins.then_inc(self.sems[eng], 1)
        ev = (eng, self.semval[eng])
        for t in rt:
            t.readers[eng] = ev
        for t in wt:
            t.last_write = ev
            t.readers = {}
        return ins

    def barrier(self):
        engs = ["pe", "act", "dve", "pool"]
        for e in engs:
            for f in engs:
                if e == f:
                    continue
                v = self.semval[f]
                if v > 0 and self.seen[e].get(f, 0) < v:
                    self.E[e].wait_ge(self.sems[f], v)
                    self.seen[e][f] = v

    def _dsem(self, t):
        if t.dsem is None:
            k = "d%d" % self.ndsem
            self.ndsem += 1
            self.sems[k] = self.es.enter_context(self.nc.semaphore("sem_" + k))
            self.semval[k] = 0
            t.dsem = k
        return t.dsem

    def dma(self, q, out, in_, is_output=False, indirect_in=None, extra_reads=(), no_waw=False, **kw):
        ot = out.tile if isinstance(out, View) else None
        it = in_.tile if isinstance(in_, View) else None
        rt = ([it] if it else []) + self._tiles(list(extra_reads) + ([indirect_in] if indirect_in is not None else []))
        wt = [ot] if ot else []
        deps = self._deps(rt, [] if no_waw else wt)
        self._wait(q, deps)
        st = ot or it
        k = self._dsem(st)
        self.semval[k] += 16
        oap = out.ap if isinstance(out, View) else out
        iap = in_.ap if isinstance(in_, View) else in_
        if indirect_in is not None:
            ins = self.E[q].indirect_dma_start(
                out=oap, out_offset=None, in_=iap,
                in_offset=bass.IndirectOffsetOnAxis(ap=indirect_in.ap, axis=0), **kw)
        else:
            ins = self.E[q].dma_start(out=oap, in_=iap, **kw)
        ins.then_inc(self.sems[k], 16)
        ev = (k, self.semval[k])
        for t in rt:
            t.readers[k] = ev
        if ot:
            ot.last_write = ev
            ot.readers = {}
        if is_output:
            self.out_events[k] = self.semval[k]
        return ins

    def finish(self, eng="sp"):
        for k, v in self.out_events.items():
            if self.seen[eng].get(k, 0) < v:
                self.E[eng].wait_ge(self.sems[k], v)
                self.seen[eng][k] = v

    def mm(self, out, pairs, start=True, stop=True):
        reads = []
        for l, r in pairs:
            reads += [l, r]
        n = len(pairs)

        def fn():
            ins = None
            for i, (l, r) in enumerate(pairs):
                ins = self.nc.tensor.matmul(out.ap, l.ap, r.ap,
                                            start=(start and i == 0), stop=(stop and i == n - 1))
            return ins

        return self.op("pe", fn, reads, [out])

    def transposes(self, items, ident):
        reads = [ident] + [i for _, i in items]
        writes = [o for o, _ in items]

        def fn():
            ins = None
            for o, i in items:
                k = i.ap.shape[0]
                ins = self.nc.tensor.transpose(o.ap, i.ap, ident.ap[0:k, 0:k])
            return ins

        return self.op("pe", fn, reads, writes)

    def act(self, out, in_, func, bias=None, scale=None, accum_out=None):
        kw = {}
        reads = [in_]
        for nm, val in (("bias", bias), ("scale", scale)):
            if val is None:
                continue
            if isinstance(val, View):
                kw[nm] = val.ap
                reads.append(val)
            else:
                kw[nm] = val
        writes = [out]
        if accum_out is not None:
            kw["accum_out"] = accum_out.ap
            writes.append(accum_out)
        return self.op("act", lambda: self.nc.scalar.activation(out=out.ap, in_=in_.ap, func=func, **kw), reads, writes)

    def _ve(self, eng):
        return self.nc.vector if eng == "dve" else self.nc.gpsimd

    def tt(self, eng, out, in0, in1, op):
        return self.op(eng, lambda: self._ve(eng).tensor_tensor(out=out.ap, in0=in0.ap, in1=in1.ap, op=op), [in0, in1], [out])

    def ts(self, eng, out, in0, s1, s2, op0, op1=None):
        reads = [in0]
        a1, a2 = s1, s2
        if isinstance(s1, View):
            a1 = s1.ap
            reads.append(s1)
        if isinstance(s2, View):
            a2 = s2.ap
            reads.append(s2)
        kw = {}
        if op1 is not None:
            kw["op1"] = op1
        return self.op(eng, lambda: self._ve(eng).tensor_scalar(out=out.ap, in0=in0.ap, scalar1=a1, scalar2=a2, op0=op0, **kw), reads, [out])

    def stt(self, eng, out, in0, scalar, in1, op0, op1):
        reads = [in0, in1]
        a = scalar
        if isinstance(scalar, View):
            a = scalar.ap
            reads.append(scalar)
        return self.op(eng, lambda: self._ve(eng).scalar_tensor_tensor(out=out.ap, in0=in0.ap, scalar=a, in1=in1.ap, op0=op0, op1=op1), reads, [out])

    def copy(self, eng, out, in_):
        if eng == "act":
            return self.op("act", lambda: self.nc.scalar.copy(out=out.ap, in_=in_.ap), [in_], [out])
        return self.op(eng, lambda: self._ve(eng).tensor_copy(out=out.ap, in_=in_.ap), [in_], [out])

    def memset(self, eng, out, val):
        return self.op(eng, lambda: self._ve(eng).memset(out.ap, val), [], [out])

    def reduce(self, eng, out, in_, op, axis=AX.X):
        return self.op(eng, lambda: self._ve(eng).tensor_reduce(out=out.ap, in_=in_.ap, axis=axis, op=op), [in_], [out])

    def recip(self, out, in_):
        return self.op("dve", lambda: self.nc.vector.reciprocal(out=out.ap, in_=in_.ap), [in_], [out])


def host_consts():
    c = {}
    c["ident"] = np.eye(128, dtype=np.float32)
    kk = np.arange(128)[:, None]
    qq = np.arange(128)[None, :]
    c["tri"] = np.where(kk <= qq, 0.0, NEG).astype(np.float32)
    c["kext"] = (np.arange(SEQ)[None, :] // G == np.arange(8)[:, None]).astype(np.float32)
    inv = np.zeros((128, 4, 16), np.float32)
    for g, w in enumerate((2, 4, 8, 16)):
        inv[:, g, :] = 1.0 / np.minimum(w, np.arange(16) + 1)
    c["invcnt"] = inv.reshape(128, 64)
    sel = np.zeros((16, 4, 128), np.float32)
    for b in range(4):
        for s in range(4):
            sel[4 * b + s, b, s] = 1.0
    c["sel"] = sel.reshape(16, 512)
    caus = np.full((128, 32), NEG, np.float32)
    for s in range(4):
        for h in range(8):
            for key in range(4):
                if key <= s:
                    caus[key, s * 8 + h] = 0.0
    c["caus"] = caus
    hm = np.zeros((32, 8), np.float32)
    gs = np.zeros((32, 4), np.float32)
    for s in range(4):
        for h in range(8):
            hm[s * 8 + h, h] = 1.0
            gs[s * 8 + h, s] = 1.0
    c["hmask"] = hm
    c["gsum"] = gs
    c["i32"] = np.eye(32, dtype=np.float32)
    c["iota"] = np.arange(128, dtype=np.float32).reshape(128, 1)
    return c


CONST_SHAPES = {"ident": [128, 128], "tri": [128, 128], "kext": [8, 2048], "invcnt": [128, 64],
                "sel": [16, 512], "caus": [128, 32], "hmask": [32, 8], "gsum": [32, 4], "i32": [32, 32],
                "iota": [128, 1]}

IN_SHAPES = {
    "xp": ([NPB * SEQ, D], F32), "xs": ([NST, D], F32), "c_all": ([NPB + NSB, D], F32),
    "w_ada": ([D, 6 * D], F32), "b_ada": ([1, 6 * D], F32), "g_mix": ([1, D], F32),
    "w_in": ([D, 4096], F32), "g_q": ([1, 512], F32), "g_k": ([1, 512], F32),
    "w_pool": ([512, 128], F32), "pool_scale": ([1, 512], F32),
    "w_bp": ([512, D], F32), "w_ba": ([512, D], F32), "w_out": ([D, D], F32), "g_ffn": ([1, D], F32),
    "w_up": ([D, 2 * DFF], F32), "w_conv": ([3, DFF], F32), "b_conv": ([1, DFF], F32), "w_down": ([DFF, D], F32),
    "st_pool": ([NSB * 15, 512], F32), "st_conv": ([NSB * 2, DFF], F32), "ptab": ([1, NSB * NPAGES], I32),
    "cache_k": ([5120 * 128, 512], F32), "cache_v": ([5120 * 128, 512], F32),
}
A_OUTS = ("y_p", "k_p", "v_p", "pool_p", "conv_p", "k_s", "v_s", "pool_s", "q_s")
C_OUTS = ("y_s", "conv_s")
OUT_SHAPES = {
    "q_s": [NST, 512],
    "y_p": [NPB * SEQ, D], "y_s": [NST, D], "k_p": [NPB * SEQ, 512], "v_p": [NPB * SEQ, 512],
    "pool_p": [NPB * 15, 512], "conv_p": [NPB * 2, DFF], "k_s": [NST, 512], "v_s": [NST, 512],
    "pool_s": [NSB * 15, 512], "conv_s": [NSB * 2, DFF],
}


class Kern:
    def __init__(self, nc, es, mode="A"):
        self.nc = nc
        self.C = Ctx(nc, es)
        self.mode = mode
        self.din = {}
        for k, (shp, dt) in IN_SHAPES.items():
            if k in ("cache_k", "cache_v", "ptab"):
                continue
            if mode == "C" and k == "xp":
                continue
            self.din[k] = nc.dram_tensor(k, shp, dt, kind="ExternalInput").ap()
        if mode == "C":
            self.din["o_in"] = nc.dram_tensor("o_in", [NST, 512], F32, kind="ExternalInput").ap()
        for k, shp in CONST_SHAPES.items():
            self.din["c_" + k] = nc.dram_tensor("c_" + k, shp, F32, kind="ExternalInput").ap()
        outs = A_OUTS if mode == "A" else C_OUTS
        self.dout = {k: nc.dram_tensor(k, OUT_SHAPES[k], F32, kind="ExternalOutput").ap() for k in outs}
        self.wscr = {}
        for k, shp in (("w_in", [D, 4096]), ("w_bp", [512, D]), ("w_ba", [512, D]), ("w_out", [D, D]),
                       ("w_up", [D, 2 * DFF]), ("w_down", [DFF, D]), ("w_adag", [D, 2 * D])):
            ap = nc.dram_tensor("scr_" + k, shp, BF16, kind="Internal").ap()
            self.wscr[k] = (ap, Tile("scr_" + k, ap))
        self.alloc()

    def alloc(self):
        C = self.C
        sb = C.sbuf
        self.ident = sb("ident", [128, 128], BF16)
        self.tri = sb("tri", [128, 128], BF16)
        self.invcnt = sb("invcnt", [128, 4, 16], F32)
        self.gq = sb("gq", [128, 512], F32)
        self.gk = sb("gk", [128, 512], F32)
        self.wpool = sb("wpool", [128, 4, 128], BF16)
        self.pscaleT = sb("pscaleT", [128, 4], F32)
        self.wconvT = sb("wconvT", [128, NFC, 3], F32)
        self.bconvT = sb("bconvT", [128, NFC], F32)
        self.gmixT = sb("gmixT", [128, 8], F32)
        self.gffnT = sb("gffnT", [128, 8], F32)
        self.modT = sb("modT", [128, 48, 6], F32)
        self.scl = sb("scl", [128, 2, 8, 6], F32)
        self.badaT = sb("badaT", [128, 48], F32)
        self.cT = sb("cT", [128, 8, 6], F32)
        self.siluT = sb("siluT", [128, 8, 6], BF16)
        self.silurep = sb("silurep", [128, 8, 128], BF16)
        self.bada_g = sb("bada_g", [1, 2 * D], BF16)
        self.ones1 = sb("ones1", [1, 128], BF16)
        self.gate = sb("gate", [128, 2, D], F32)
        self.KTf = sb("KT", [128, 8 * SEQ], BF16)
        self.KT = Tile("KTv", self.KTf.ap.rearrange("p (h t) -> p h t", h=8))
        self.KT = self.KTf if False else self.KT
        self.stH = sb("stH", [128, 4, 4, 15], F32)
        self.chs = sb("chs", [128, NFC, 4, 2], F32)
        self.VEf = sb("VE", [128, 16 * 8 * 65], BF16)
        self.VE = Tile("VEv", self.VEf.ap.rearrange("p (a h d) -> p a h d", a=16, h=8))
        self.kmT = sb("kmT", [64, 8, 8], BF16)
        self.ring = [sb("ring%d" % i, [128, 4096], BF16) for i in range(RING)]
        self.xs0f = sb("xs0", [128, 2 * D], F32)
        self.xs1f = sb("xs1", [128, 2 * D], F32)
        self.xs = [Tile("xsv%d" % i, f.ap.rearrange("p (t d) -> p t d", t=2)) for i, f in enumerate((self.xs0f, self.xs1f))]
        self.xn = sb("xn", [128, 2, D], BF16)
        self.ss = sb("ss", [128, 2], F32)
        self.rstd = sb("rstd", [128, 2], F32)
        self.hT = sb("hT", [128, 8, G], BF16)
        self.uT = sb("uT", [128, 4, 16 + G], F32)
        self.ptmp = [sb("ptmp%d" % i, [128, 16 + G], F32) for i in range(2)]
        self.pooled = sb("pooled", [128, 4, G], BF16)
        self.ypool = sb("ypool", [128, 4, G], BF16)
        self.th = [sb("th%d" % i, [128, G], F32) for i in range(2)]
        self.m = sb("m", [128, 8, G], BF16)
        self.thb = sb("thb", [128, 8, G], BF16)
        self.vout = sb("vout", [128, 512], F32)
        self.kout = sb("kout", [128, 512], F32)

        self.ntmp = sb("ntmp", [128, 512], F32)
        self.ssq = sb("ssq", [128, 8], F32)
        self.rsq = sb("rsq", [128, 8], F32)
        self.kbf = sb("kbf", [128, 512], BF16)
        self.qext = sb("qext", [128, 8, 72], BF16)
        self.qT64 = sb("qT64", [64, 8, 128], BF16)
        self.bsc = sb("bsc", [128, 8, 8], F32)
        self.cmp = sb("cmp", [128, 8, 8, 8], F32)
        self.rank = sb("rank", [128, 8, 8], F32)
        self.QT = sb("QT", [72, 8, G], BF16)
        self.PT = [sb("PT%d" % i, [128, 2, G], BF16) for i in range(4)]
        self.rden = sb("rden", [128, 4], F32)
        self.obf = sb("obf", [128, 512], BF16)
        self.oT = sb("oT", [128, 4, G], BF16)
        self.mb = [sb("mb%d" % i, [128, G], F32) for i in range(2)]
        self.merged = sb("merged", [128, 8, G], BF16)
        self.rtmp = [sb("rtmp%d" % i, [128, 512], F32) for i in range(2)]
        self.sq = self.rtmp[1]
        self.gT = [sb("gT%d" % i, [128, 2 + G], F32) for i in range(2)]
        self.ct = [sb("ct%d" % i, [128, G], F32) for i in range(4)]
        self.actTs = [sb("actT%d" % i, [128, 4 if i < 5 else 2, G], BF16) for i in range(6)]
        self.chist = sb("chist", [128, NFC, 2], F32)
        self.banks = [C.psum("bank%d" % i, [128, 512], F32) for i in range(8)]
        self.held = set()
        self.bank_i = 0
        self.ring_i = 0
        self.ring_emit = 0
        self.plan = []
        self.prenormed = False

    def bank(self, hold=False):
        for _ in range(16):
            b = self.bank_i % 8
            self.bank_i += 1
            if b not in self.held:
                if hold:
                    self.held.add(b)
                return self.banks[b]
        raise RuntimeError("no free psum bank")

    def release(self, bank):
        self.held.discard(self.banks.index(bank))

    def plan_group(self, ctx):
        p = []
        p.append(("w_in", "k8", 0, 512))
        p.append(("w_in", "k8", 3072, 512))
        p.append(("w_in", "k8", 3584, 512))
        p.append(("w_bp", "k4", 0, 1024))
        p.append(("w_in", "k8", 2048, 512))
        p.append(("w_in", "k8", 2560, 512))
        p.append(("w_in", "k8", 512, 512))
        p.append(("w_in", "k8", 1024, 512))
        p.append(("w_in", "k8", 1536, 512))
        p.append(("w_ba", "k4", 0, 1024))
        p.append(("w_out", "k8", 0, 512))
        p.append(("w_out", "k8", 512, 512))
        for j in range(6):
            n = 512 if j < 5 else 256
            p.append(("w_up", "k8", 512 * j, n))
            p.append(("w_up", "k8", DFF + 512 * j, n))
        for j in range(6):
            p.append(("w_down", "d4", 4 * j, 4 if j < 5 else 2))
        return p

    def plan_gate(self):
        return [("w_adag", "k8", 512 * j, 512) for j in range(4)]

    def ring_fetch(self, base):
        C = self.C
        while self.ring_emit < min(len(self.plan), base + RING):
            name, kind, a, b = self.plan[self.ring_emit]
            slot = self.ring[self.ring_emit % RING]
            ap, wtile = self.wscr[name]
            rtiles = [wtile]
            q = "sp"
            if self.mode == "C":
                q = "pool"
                rtiles = []
                if name == "w_adag":
                    ap = self.din["w_ada"][:, 2 * D:3 * D] if a < D else self.din["w_ada"][:, 5 * D:6 * D]
                    a = a % D
                else:
                    ap = self.din[name]
            if kind == "k8":
                src = ap[:, a:a + b].rearrange("(kc p) n -> p kc n", p=128)
                dst = slot[:, 0:8 * b].rearrange("p (kc n) -> p kc n", kc=8)
                reads = rtiles
            elif kind == "k4":
                src = ap[:, a:a + b].rearrange("(kc p) n -> p kc n", p=128)
                dst = slot[:, 0:4 * b].rearrange("p (kc n) -> p kc n", kc=4)
                reads = rtiles
            else:
                src = ap[a * 128:(a + b) * 128, :].rearrange("(j p) n -> p j n", p=128)
                dst = slot[:, 0:b * 1024].rearrange("p (j n) -> p j n", j=b)
                reads = rtiles
            C.dma(q, dst, src, extra_reads=[t.v() for t in reads])
            self.ring_emit += 1

    def wget(self, expect, keep=0):
        assert self.plan[self.ring_i] == expect, (self.plan[self.ring_i], expect)
        self.ring_fetch(self.ring_i - keep)
        slot = self.ring[self.ring_i % RING]
        self.ring_i += 1
        return slot

    def setup(self):
        C, nc, din = self.C, self.nc, self.din
        C.dma("pool", self.ident[:], din["c_ident"])
        C.dma("pool", self.tri[:], din["c_tri"])
        C.dma("sp", self.invcnt[:].rearrange("p g t -> p (g t)"), din["c_invcnt"])
        C.dma("sp", self.gq[:], din["g_q"].partition_broadcast(128))
        C.dma("sp", self.gk[:], din["g_k"].partition_broadcast(128))
        C.dma("pool", self.wpool[:], din["w_pool"].rearrange("(g c) d -> c g d", c=128))
        for h in range(8):
            C.dma("pool", self.KT[64:72, h, :], din["c_kext"])
        with nc.allow_non_contiguous_dma(reason="tiny one-time transposed parameter loads"):
            C.dma("sp", self.pscaleT[:], din["pool_scale"].rearrange("o (g c) -> c (o g)", c=128))
            C.dma("sp", self.bconvT[:], din["b_conv"].rearrange("o (j c) -> c (o j)", c=128))
            for t in range(3):
                C.dma("sp", self.wconvT[:, :, t], din["w_conv"][t:t + 1, :].rearrange("o (j c) -> c (o j)", c=128))
            C.dma("sp", self.gmixT[:], din["g_mix"].rearrange("o (j c) -> c (o j)", c=128))
            C.dma("sp", self.gffnT[:], din["g_ffn"].rearrange("o (j c) -> c (o j)", c=128))
            C.dma("sp", self.badaT[:], din["b_ada"].rearrange("o (j c) -> c (o j)", c=128))
            for b in range(6):
                C.dma("sp", self.cT[:, :, b], din["c_all"][b:b + 1, :].rearrange("o (j c) -> c (o j)", c=128))
            for bb in range(NSB):
                for g in range(4):
                    C.dma("sp", self.stH[:, g, bb, :],
                          din["st_pool"][bb * 15:(bb + 1) * 15, g * 128:(g + 1) * 128].rearrange("t c -> c t"))
                for t in range(2):
                    C.dma("sp", self.chs[:, :, bb, t],
                          din["st_conv"][bb * 2 + t:bb * 2 + t + 1, :].rearrange("o (j c) -> c (o j)", c=128))
        C.dma("pool", self.bada_g[:, 0:D], din["b_ada"][:, 2 * D:3 * D])
        C.dma("pool", self.bada_g[:, D:2 * D], din["b_ada"][:, 5 * D:6 * D])
        C.memset("dve", self.ones1[:], 1.0)
        C.memset("dve", self.VE[:, :, :, 64:65], 1.0)
        C.ts("dve", self.gq[:], self.gq[:], 0.125, None, ALU.mult)
        th = self.ct[0][:, 0:48].rearrange("p (j b) -> p j b", j=8)
        C.act(th, self.cT[:], AF.Tanh, scale=0.5)
        C.ts("dve", th, th, 0.5, 0.5, ALU.mult, ALU.add)
        C.tt("dve", self.siluT[:], th, self.cT[:], ALU.mult)
        srcs = {"w_in": din["w_in"], "w_bp": din["w_bp"], "w_ba": din["w_ba"], "w_out": din["w_out"],
                "w_up": din["w_up"], "w_down": din["w_down"]}
        t = self.wscr["w_adag"][1]
        for r in range(8 if self.mode == "A" else 0):
            C.dma("pool", t[r * 128:(r + 1) * 128, 0:D], din["w_ada"][r * 128:(r + 1) * 128, 2 * D:3 * D], no_waw=True)
            C.dma("pool", t[r * 128:(r + 1) * 128, D:2 * D], din["w_ada"][r * 128:(r + 1) * 128, 5 * D:6 * D], no_waw=True)
        t = self.wscr["w_in"][1]
        for r in range(8 if self.mode == "A" else 0):
            C.dma("pool", t[r * 128:(r + 1) * 128, :], din["w_in"][r * 128:(r + 1) * 128, :], no_waw=True)
        for jn, j12 in enumerate((0, 1, 2, 3, 6, 7, 8, 9)):
            slot = self.ring[jn % RING]
            C.dma("pool", slot[:].rearrange("p (kc n) -> p kc n", kc=8),
                  din["w_ada"][:, j12 * 512:(j12 + 1) * 512].rearrange("(kc p) n -> p kc n", p=128))
            wv = slot[:].rearrange("p (kc n) -> p kc n", kc=8)
            bk = self.bank()
            for jj in range(4):
                j = j12 * 4 + jj
                C.mm(bk[:, jj * 8:jj * 8 + 6],
                     [(wv[:, kc, jj * 128:(jj + 1) * 128], self.siluT[:, kc, :]) for kc in range(8)])
            C.tt("dve", self.modT[:, j12 * 4:(j12 + 1) * 4, :],
                 bk[:, 0:32].rearrange("p (j b) -> p j b", j=4)[:, :, 0:6],
                 self.badaT[:, j12 * 4:(j12 + 1) * 4].unsq(2).bcast([128, 4, 6]), ALU.add)
        for name in (("w_bp", "w_ba", "w_out", "w_up", "w_down") if self.mode == "A" else ()):
            t = self.wscr[name][1]
            for r in range(srcs[name].shape[0] // 128):
                C.dma("pool", t[r * 128:(r + 1) * 128, :], srcs[name][r * 128:(r + 1) * 128, :], no_waw=True)
        for s, (gT, mi) in enumerate(((self.gmixT, 1), (self.gffnT, 4))):
            C.ts("dve", self.scl[:, s], self.modT[:, mi * 8:(mi + 1) * 8, :], 1.0, None, ALU.add)
            C.tt("dve", self.scl[:, s], self.scl[:, s], gT[:].unsq(2).bcast([128, 8, 6]), ALU.mult)

    def shiftT(self, s, c, b):
        mi = 0 if s == 0 else 3
        return self.modT[:, mi * 8 + c, b:b + 1]

    def load_gate(self, cols):
        C = self.C
        runs = []
        st = 0
        for i in range(1, len(cols) + 1):
            if i == len(cols) or cols[i] != cols[st]:
                runs.append((st, i, cols[st]))
                st = i
        for (a, b, cb) in runs:
            C.copy("dve", self.silurep[:, :, a:b], self.siluT[:, :, cb:cb + 1].bcast([128, 8, b - a]))
        n = len(cols)
        for j in range(4):
            slot = self.wget(("w_adag", "k8", 512 * j, 512))
            wv = slot[:].rearrange("p (kc n) -> p kc n", kc=8)
            bk = self.bank()
            C.mm(bk[0:n, :], [(self.silurep[:, kc, 0:n], wv[:, kc, :]) for kc in range(8)]
                 + [(self.ones1[:, 0:n], self.bada_g[:, j * 512:(j + 1) * 512])])
            C.ts("dve", self.gate[0:n, j // 2, (j % 2) * 512:(j % 2 + 1) * 512], bk[0:n, :], 0.5, None, ALU.mult)

    def norm_to_hT(self, xs, s, ntile, rows, mod_b=None, mod_tiles=None):
        self.norm_stats(xs, ntile, rows)
        self.norm_transposes(s, ntile, rows, mod_b=mod_b, mod_tiles=mod_tiles)

    def norm_stats(self, xs, ntile, rows):
        C = self.C
        for t in range(ntile):
            C.act(self.xn[0:rows, t, :], xs[0:rows, t, :], AF.Square, accum_out=self.ss[0:rows, t:t + 1])
        C.act(self.rstd[0:rows, 0:ntile], self.ss[0:rows, 0:ntile], AF.Sqrt, scale=1.0 / D, bias=EPS)
        C.recip(self.rstd[0:rows, 0:ntile], self.rstd[0:rows, 0:ntile])
        for t in range(ntile):
            C.ts("dve", self.xn[0:rows, t, :], xs[0:rows, t, :], self.rstd[0:rows, t:t + 1], None, ALU.mult)

    def norm_transposes(self, s, ntile, rows, mod_b=None, mod_tiles=None):
        C = self.C
        for half in range(2):
            bk = self.bank()
            bv = bk[:].bitcast(BF16).rearrange("p (c t n) -> p c t n", c=4, t=2)
            items = []
            for cc in range(4):
                c = half * 4 + cc
                for t in range(ntile):
                    items.append((bv[:, cc, t, 0:rows], self.xn[0:rows, t, c * 128:(c + 1) * 128]))
            C.transposes(items, self.ident)
            for cc in range(4):
                c = half * 4 + cc
                if mod_b is not None:
                    C.act(self.hT[:, c, :].rearrange("p (t n) -> p t n", t=ntile),
                          bv[:, cc, 0:ntile, :], AF.Identity,
                          bias=self.shiftT(s, c, mod_b), scale=self.scl[:, s, c, mod_b:mod_b + 1])
                else:
                    sc_t, sh_t = mod_tiles
                    C.tt("dve", self.ct[0][:, 0:rows], bv[:, cc, 0, 0:rows], sc_t[:, s, c, :], ALU.mult)
                    C.tt("dve", self.hT[:, c, 0:rows], self.ct[0][:, 0:rows], sh_t[:, s, c, :], ALU.add)

    def mixer_front(self, n, first, sample=False):
        C = self.C
        wu = self.wget(("w_in", "k8", 0, 512))[:].rearrange("p (kc n) -> p kc n", kc=8)
        for g in range(4):
            bk = self.bank()
            C.mm(bk[:, 0:n], [(wu[:, kc, g * 128:(g + 1) * 128], self.hT[:, kc, 0:n]) for kc in range(8)])
            if sample:
                C.copy("act", self.uTs[:, g, :, 15:19], bk[:, 0:n].rearrange("p (b t) -> p b t", b=4))
            else:
                C.copy("act", self.uT[:, g, 16:16 + n], bk[:, 0:n])
        L = 16 + n
        for g in range(4):
            if sample:
                cur = self.uTs[:, g]
                for l in range(1, g + 2):
                    sh = 1 << (l - 1)
                    lo = (1 << l) - 1
                    dst = self.ptmp[(l - 1) % 2][:, 0:76].rearrange("p (b t) -> p b t", b=4)
                    C.tt("pool", dst[:, :, lo:19], cur[:, :, lo:19], cur[:, :, lo - sh:19 - sh], ALU.add)
                    cur = dst
                C.ts("pool", cur[:, :, 15:19], cur[:, :, 15:19], 1.0 / (2 << g), None, ALU.mult)
                C.tt("pool", self.pooled[:, g, 0:n].rearrange("p (b t) -> p b t", b=4), cur[:, :, 15:19],
                     self.uTs[:, g, :, 15:19], ALU.subtract)
                continue
            cur = self.uT[:, g, :]
            for l in range(1, g + 2):
                sh = 1 << (l - 1)
                lo = (1 << l) - 1
                dst = self.ptmp[(l - 1) % 2]
                C.tt("dve", dst[:, lo:L], cur[:, lo:L], cur[:, lo - sh:L - sh], ALU.add)
                cur = dst[:]
            w = 2 << g
            if first:
                C.tt("dve", cur[:, 16:32], cur[:, 16:32], self.invcnt[:, g, :], ALU.mult)
                C.ts("dve", cur[:, 32:L], cur[:, 32:L], 1.0 / w, None, ALU.mult)
                C.tt("dve", self.pooled[:, g, 0:n], cur[:, 16:L], self.uT[:, g, 16:L], ALU.subtract)
            else:
                C.ts("dve", cur[:, 16:L], cur[:, 16:L], 1.0 / w, None, ALU.mult)
                C.tt("dve", self.pooled[:, g, 0:n], cur[:, 16:L], self.uT[:, g, 16:L], ALU.subtract)
        for half in range(2):
            wgb = self.wget(("w_in", "k8", 3072 + 512 * half, 512))[:].rearrange("p (kc n) -> p kc n", kc=8)
            for cc in range(4):
                c = half * 4 + cc
                bk = self.bank()
                C.mm(bk[:, 0:n], [(wgb[:, kc, cc * 128:(cc + 1) * 128], self.hT[:, kc, 0:n]) for kc in range(8)])
                C.act(self.thb[:, c, 0:n], bk[:, 0:n], AF.Tanh, scale=0.5)
        for g in range(4):
            bk = self.bank()
            C.mm(bk[:, 0:n], [(self.wpool[:, g, :], self.pooled[:, g, 0:n])])
            C.act(self.ypool[:, g, 0:n], bk[:, 0:n], AF.Copy, scale=self.pscaleT[:, g:g + 1])
        wbp = self.wget(("w_bp", "k4", 0, 1024))[:].rearrange("p (kc n) -> p kc n", kc=4)
        for half in range(2):
            wga = self.wget(("w_in", "k8", 2048 + 512 * half, 512), keep=1 + half)[:].rearrange("p (kc n) -> p kc n", kc=8)
            for cc in range(4):
                c = half * 4 + cc
                bk = self.bank()
                C.mm(bk[:, 0:n], [(wga[:, kc, cc * 128:(cc + 1) * 128], self.hT[:, kc, 0:n]) for kc in range(8)])
                th = self.th[c % 2]
                C.act(th[:, 0:n], bk[:, 0:n], AF.Tanh, scale=0.5)
                bk2 = self.bank()
                C.mm(bk2[:, 0:n], [(wbp[:, kc, c * 128:(c + 1) * 128], self.ypool[:, kc, 0:n]) for kc in range(4)])
                C.stt("dve", self.m[:, c, 0:n], th[:, 0:n], 1.0, bk2[:, 0:n], ALU.add, ALU.mult)

    def qkv_tile(self, t, rows, wq, wk, wv, k_dst, v_dst, v_bf_dst, q_dst=None):
        banks = self.qkv_mm(t, rows, wq, wk, wv)
        self.qkv_post(rows, banks, k_dst, v_dst, v_bf_dst, q_dst)

    def qkv_mm(self, t, rows, wq, wk, wv):
        C = self.C
        bq, bk_, bv = self.bank(), self.bank(), self.bank()
        lhs = [self.hT[:, kc, t * 128:t * 128 + rows] for kc in range(8)]
        C.mm(bq[0:rows, :], [(lhs[kc], wq[:, kc, :]) for kc in range(8)])
        C.mm(bk_[0:rows, :], [(lhs[kc], wk[:, kc, :]) for kc in range(8)])
        C.mm(bv[0:rows, :], [(lhs[kc], wv[:, kc, :]) for kc in range(8)])
        return bq, bk_, bv

    def qkv_post(self, rows, banks, k_dst, v_dst, v_bf_dst, q_dst=None):
        C = self.C
        bq, bk_, bv = banks
        C.copy("act", self.vout[0:rows, :], bv[0:rows, :])
        C.dma("pool", v_dst, self.vout[0:rows, :], is_output=True)
        C.copy("dve", v_bf_dst, self.vout[0:rows, :].rearrange("p (h d) -> p h d", h=8))
        for (bank, gain, which) in ((bk_, self.gk, "k"), (bq, self.gq, "q")):
            C.act(self.sq[0:rows, :], bank[0:rows, :], AF.Square)
            C.reduce("dve", self.ssq[0:rows, :], self.sq[0:rows, :].rearrange("p (h d) -> p h d", h=8), ALU.add)
            C.act(self.rsq[0:rows, :], self.ssq[0:rows, :], AF.Sqrt, scale=1.0 / 64, bias=EPS)
            C.recip(self.rsq[0:rows, :], self.rsq[0:rows, :])
            C.tt("dve", self.ntmp[0:rows, :].rearrange("p (h d) -> p h d", h=8),
                 bank[0:rows, :].rearrange("p (h d) -> p h d", h=8),
                 self.rsq[0:rows, :].unsq(2).bcast([rows, 8, 64]), ALU.mult)
            if which == "k":
                C.tt("dve", self.kout[0:rows, :], self.ntmp[0:rows, :], gain[0:rows, :], ALU.mult)
                C.dma("pool", k_dst, self.kout[0:rows, :], is_output=True)
                C.copy("act", self.kbf[0:rows, :], self.kout[0:rows, :])
            else:
                C.tt("dve", self.qext[0:rows, :, 0:64] if q_dst is None else q_dst,
                     self.ntmp[0:rows, :].rearrange("p (h d) -> p h d", h=8),
                     gain[0:rows, :].rearrange("p (h d) -> p h d", h=8), ALU.mult)

    def mixer_back(self, n, ntile, rows, xs):
        C = self.C
        wba = self.wget(("w_ba", "k4", 0, 1024))[:].rearrange("p (kc n) -> p kc n", kc=4)
        for c in range(8):
            bk = self.bank()
            C.mm(bk[:, 0:n], [(wba[:, kc, c * 128:(c + 1) * 128], self.oT[:, kc, 0:n]) for kc in range(4)])
            mb = self.mb[c % 2]
            C.stt("dve", mb[:, 0:n], self.thb[:, c, 0:n], 1.0, bk[:, 0:n], ALU.add, ALU.mult)
            C.tt("dve", self.merged[:, c, 0:n], mb[:, 0:n], self.m[:, c, 0:n], ALU.add)
        for hf in range(2):
            wo = self.wget(("w_out", "k8", 512 * hf, 512))[:].rearrange("p (kc n) -> p kc n", kc=8)
            for t in range(ntile):
                bk = self.bank()
                C.mm(bk[0:rows, :], [(self.merged[:, kc, t * 128:t * 128 + rows], wo[:, kc, :]) for kc in range(8)])
                rt = self.rtmp[(hf * 2 + t) % 2]
                C.tt("dve", rt[0:rows, :], bk[0:rows, :], self.gate[0:rows, 0, hf * 512:(hf + 1) * 512], ALU.mult)
                C.tt("dve", xs[0:rows, t, hf * 512:(hf + 1) * 512], xs[0:rows, t, hf * 512:(hf + 1) * 512], rt[0:rows, :], ALU.add)

    def ffn(self, n, ntile, rows, xs, hist_cols=None, mid_cb=None, post_cb=None):
        C = self.C
        pend = None

        def stage2(j, bk, t1, t2):
            C.act(t2[:, 0:n], t1[:, 0:n], AF.Tanh, scale=0.5)
            C.stt("dve", t2[:, 0:n], t2[:, 0:n], 1.0, t1[:, 0:n], ALU.add, ALU.mult)
            C.tt("dve", self.actTs[j // 4][:, j % 4, 0:n], t2[:, 0:n], bk[:, 256:256 + n], ALU.mult)

        for jj in range(6):
            ncols = 512 if jj < 5 else 256
            wg = self.wget(("w_up", "k8", 512 * jj, ncols))[:, 0:8 * ncols].rearrange("p (kc n) -> p kc n", kc=8)
            wv = self.wget(("w_up", "k8", DFF + 512 * jj, ncols), keep=1)[:, 0:8 * ncols].rearrange("p (kc n) -> p kc n", kc=8)
            for cc in range(ncols // 128):
                j = jj * 4 + cc
                bk = self.bank()
                C.mm(bk[:, 0:n], [(wg[:, kc, cc * 128:(cc + 1) * 128], self.hT[:, kc, 0:n]) for kc in range(8)])
                C.mm(bk[:, 256:256 + n], [(wv[:, kc, cc * 128:(cc + 1) * 128], self.hT[:, kc, 0:n]) for kc in range(8)])
                gT = self.gT[j % 2]
                t1, t2 = self.ct[(j % 2) * 2], self.ct[(j % 2) * 2 + 1]
                if hist_cols is None:
                    C.copy("pool", gT[:, 0:2], self.chist[:, j, :])
                    C.copy("act", gT[:, 2:2 + n], bk[:, 0:n])
                    C.copy("pool", self.chist[:, j, :], gT[:, n:n + 2])
                    C.act(t1[:, 0:n], gT[:, 0:n], AF.Identity, bias=self.bconvT[:, j:j + 1], scale=self.wconvT[:, j, 0:1])
                    C.stt("dve", t2[:, 0:n], gT[:, 1:1 + n], self.wconvT[:, j, 1:2], t1[:, 0:n], ALU.mult, ALU.add)
                    C.stt("dve", t1[:, 0:n], gT[:, 2:2 + n], self.wconvT[:, j, 2:3], t2[:, 0:n], ALU.mult, ALU.add)
                else:
                    self.sample_conv(j, bk, t1, t2)
                if pend is not None:
                    stage2(*pend)
                pend = (j, bk, t1, t2)
        stage2(*pend)
        if mid_cb is not None:
            mid_cb()
        banks = [[self.bank(hold=True) for t in range(ntile)] for hf in range(2)]
        for jj in range(6):
            nj = 4 if jj < 5 else 2
            wd = self.wget(("w_down", "d4", 4 * jj, nj))[:, 0:nj * 1024].rearrange("p (j n) -> p j n", j=nj)
            for hf in range(2):
                for t in range(ntile):
                    C.mm(banks[hf][t][0:rows, :],
                         [(self.actTs[jj][:, q, t * 128:t * 128 + rows], wd[:, q, hf * 512:(hf + 1) * 512]) for q in range(nj)],
                         start=(jj == 0), stop=(jj == 5))
        if post_cb is not None:
            post_cb()
        for hf in range(2):
            for t in range(ntile):
                bk = banks[hf][t]
                rt = self.rtmp[(hf * 2 + t) % 2]
                C.tt("dve", rt[0:rows, :], bk[0:rows, :], self.gate[0:rows, 1, hf * 512:(hf + 1) * 512], ALU.mult)
                C.tt("dve", xs[0:rows, t, hf * 512:(hf + 1) * 512], xs[0:rows, t, hf * 512:(hf + 1) * 512], rt[0:rows, :], ALU.add)
                self.release(bk)

    def prompt_seq(self, seq):
        C, nc, din, dout = self.C, self.nc, self.din, self.dout
        self.plan += self.plan_gate()
        for qb in range(NG):
            self.plan += self.plan_group(None)
        self.load_gate([seq] * 128)
        C.memset("pool", self.uT[:, :, 0:16], 0.0)
        C.memset("pool", self.chist[:], 0.0)
        for qb in range(NG):
            self.prompt_group(seq, qb)

    def load_x(self, seq, qb):
        r0 = seq * SEQ + qb * G
        xs = self.xs[(seq * NG + qb) % 2]
        self.C.dma("sp", xs[:], self.din["xp"][r0:r0 + G, :].rearrange("(t p) d -> p t d", p=128))
        return xs

    def prompt_group(self, seq, qb):
        C, din, dout = self.C, self.din, self.dout
        r0 = seq * SEQ + qb * G
        if seq == 0 and qb == 0:
            xs = self.load_x(seq, qb)
        else:
            xs = self.xs[(seq * NG + qb) % 2]
        nxt = (seq, qb + 1) if qb + 1 < NG else ((seq + 1, 0) if seq + 1 < NPB else None)
        if nxt is not None:
            self.load_x(*nxt)
        if not self.prenormed:
            self.norm_to_hT(xs, 0, 2, 128, mod_b=seq)
        self.prenormed = False
        self.mixer_front(G, first=(qb == 0))
        if qb == NG - 1:
            with self.nc.allow_non_contiguous_dma(reason="15-row pooling state, transposed store"):
                for g in range(4):
                    C.dma("pool", dout["pool_p"][seq * 15:(seq + 1) * 15, g * 128:(g + 1) * 128].rearrange("t c -> c t"),
                          self.uT[:, g, G + 1:G + 16], is_output=True)
        C.copy("pool", self.uT[:, :, 0:16], self.uT[:, :, G:G + 16])
        wq = self.wget(("w_in", "k8", 512, 512))[:].rearrange("p (kc n) -> p kc n", kc=8)
        wk = self.wget(("w_in", "k8", 1024, 512), keep=1)[:].rearrange("p (kc n) -> p kc n", kc=8)
        wv = self.wget(("w_in", "k8", 1536, 512), keep=2)[:].rearrange("p (kc n) -> p kc n", kc=8)
        qkv_banks = [self.qkv_mm(t, 128, wq, wk, wv) for t in range(2)]
        for t in range(2):
            kt = 2 * qb + t
            self.qkv_post(128, qkv_banks[t],
                          dout["k_p"][r0 + t * 128:r0 + (t + 1) * 128, :],
                          dout["v_p"][r0 + t * 128:r0 + (t + 1) * 128, :],
                          self.VE[:, kt, :, 0:64])
            bk = self.bank()
            bv = bk[:].bitcast(BF16).rearrange("p (h n) -> p h n", h=8)
            C.transposes([(bv[0:64, h, :], self.kbf[:, h * 64:(h + 1) * 64]) for h in range(8)], self.ident)
            C.copy("dve", self.KT[0:64, :, kt * 128:(kt + 1) * 128], bv[0:64, :, :])
            C.memset("pool", self.qext[:, :, 64:72], 0.0)
            if qb >= 4:
                bk = self.bank()
                bv = bk[:].bitcast(BF16).rearrange("p (h n) -> p h n", h=8)
                C.transposes([(bv[0:64, h, :], self.qext[:, h, 0:64]) for h in range(8)], self.ident)
                C.copy("act", self.qT64[:], bv[0:64, :, :])
                bk2 = self.bank()
                sv = bk2[:, 0:64].rearrange("p (h b) -> p h b", h=8)
                for h in range(8):
                    C.mm(sv[:, h, 0:qb], [(self.qT64[:, h, :], self.kmT[:, h, 0:qb])])
                C.copy("dve", self.bsc[:, :, 0:qb], sv[:, :, 0:qb])
                S = self.bsc[:, :, 0:qb]
                C.tt("dve", self.cmp[:, :, 0:qb, 0:qb], S.unsq(2).bcast([128, 8, qb, qb]),
                     S.unsq(3).bcast([128, 8, qb, qb]), ALU.is_gt)
                C.reduce("dve", self.rank[:, :, 0:qb], self.cmp[:, :, 0:qb, 0:qb], ALU.add)
                C.ts("dve", self.qext[:, :, 64:64 + qb], self.rank[:, :, 0:qb], 2.5, NEG, ALU.is_ge, ALU.mult)
            bk = self.bank()
            bv = bk[:].bitcast(BF16).rearrange("p (h n) -> p h n", h=8)
            C.transposes([(bv[0:72, h, :], self.qext[:, h, :]) for h in range(8)], self.ident)
            C.copy("act", self.QT[:, :, t * 128:(t + 1) * 128], bv[0:72, :, :])
        if qb < NG - 1:
            C.reduce("dve", self.ntmp[0:64, 0:8], self.KT[0:64, :, qb * G:(qb + 1) * G], ALU.add)
            C.copy("dve", self.kmT[:, :, qb], self.ntmp[0:64, 0:8])
        self.attention(qb)
        self.mixer_back(G, 2, 128, xs)
        self.norm_to_hT(xs, 1, 2, 128, mod_b=seq)
        mid_cb = post_cb = None
        if nxt is not None and nxt[0] == seq:
            nxs = self.xs[(nxt[0] * NG + nxt[1]) % 2]
            mid_cb = lambda: self.norm_stats(nxs, 2, 128)
            post_cb = lambda: self.norm_transposes(0, 2, 128, mod_b=nxt[0])
            self.prenormed = True
        self.ffn(G, 2, 128, xs, mid_cb=mid_cb, post_cb=post_cb)
        C.dma("pool", dout["y_p"][r0:r0 + G, :].rearrange("(t p) d -> p t d", p=128), xs[:], is_output=True)
        if qb == NG - 1:
            with self.nc.allow_non_contiguous_dma(reason="2-row conv state, transposed store"):
                for t in range(2):
                    C.dma("pool", dout["conv_p"][seq * 2 + t:seq * 2 + t + 1, :].rearrange("o (j c) -> c (o j)", c=128),
                          self.chist[:, :, t], is_output=True)

    def attention(self, qb):
        C = self.C
        nkt = 2 * qb + 2
        obanks = [[self.bank(hold=True) for hg in range(2)] for qt in range(2)]
        work = [(h, a, min(a + 2, nkt)) for h in range(8) for a in range(0, nkt, 2)]
        pend_q = []
        for item in work + [None, None]:
            cur = None
            if item is not None:
                h, a, b = item
                bk = self.bank()
                pt = self.PT[(h * 8 + a // 2) % 4]
                for kt in range(a, b):
                    col = (kt - a) * G
                    kview = self.KT[0:72, h, kt * 128:(kt + 1) * 128]
                    if kt < 2 * qb:
                        C.mm(bk[:, col:col + G], [(kview, self.QT[:, h, :])])
                    elif kt == 2 * qb:
                        C.mm(bk[:, col:col + 128], [(kview, self.QT[:, h, 0:128]), (self.ident[:], self.tri[:])])
                        C.mm(bk[:, col + 128:col + 256], [(kview, self.QT[:, h, 128:256])])
                    else:
                        C.mm(bk[:, col + 128:col + 256], [(kview, self.QT[:, h, 128:256]), (self.ident[:], self.tri[:])])
                cur = (h, a, b, bk, pt)
            pend_q.append(cur)
            pend = pend_q.pop(0) if len(pend_q) > 2 else None
            if pend is not None:
                ph, pa, pb, pbk, ppt = pend
                if pb - 1 == 2 * qb + 1:
                    C.act(ppt[:, 0, :], pbk[:, 0:G], AF.Exp)
                    C.act(ppt[:, 1, 128:256], pbk[:, G + 128:G + 256], AF.Exp)
                else:
                    C.act(ppt[:].rearrange("p a n -> p (a n)"), pbk[:, 0:2 * G], AF.Exp)
                for kt in range(pa, pb):
                    for qt in range(2):
                        if kt > 2 * qb + qt:
                            continue
                        ob = obanks[qt][ph // 4]
                        C.mm(ob[:, (ph % 4) * 65:(ph % 4) * 65 + 65],
                             [(ppt[:, kt - pa, qt * 128:(qt + 1) * 128], self.VE[:, kt, ph, :])],
                             start=(kt == 0), stop=(kt == 2 * qb + qt))
        for qt in range(2):
            for hg in range(2):
                ob = obanks[qt][hg]
                ov = ob[:, 0:260].rearrange("p (h d) -> p h d", h=4)
                C.recip(self.rden[:, :], ov[:, :, 64])
                C.tt("dve", self.obf[:, hg * 256:(hg + 1) * 256].rearrange("p (h d) -> p h d", h=4),
                     ov[:, :, 0:64], self.rden[:, :].unsq(2).bcast([128, 4, 64]), ALU.mult)
                self.release(ob)
            bk = self.bank()
            bv = bk[:].bitcast(BF16).rearrange("p (c n) -> p c n", c=8)
            C.transposes([(bv[:, c, :], self.obf[:, c * 128:(c + 1) * 128]) for c in range(4)], self.ident)
            C.copy("act", self.oT[:, :, qt * 128:(qt + 1) * 128], bv[:, 0:4, :])

    def sample_conv(self, j, bk, t1, t2):
        C = self.C
        gTs = self.gT[j % 2][:, 0:24].rearrange("p (b t) -> p b t", b=4)
        v4 = lambda tl: tl[:, 0:16].rearrange("p (b t) -> p b t", b=4)
        C.copy("pool", gTs[:, :, 0:2], self.chs[:, j, :, :])
        C.copy("act", gTs[:, :, 2:6], bk[:, 0:16].rearrange("p (b t) -> p b t", b=4))
        C.copy("pool", self.cso[:, j, :, :], gTs[:, :, 4:6])
        C.ts("dve", v4(t1), gTs[:, :, 0:4], self.wconvT[:, j, 0:1], self.bconvT[:, j:j + 1], ALU.mult, ALU.add)
        C.stt("dve", v4(t2), gTs[:, :, 1:5], self.wconvT[:, j, 1:2], v4(t1), ALU.mult, ALU.add)
        C.stt("dve", v4(t1), gTs[:, :, 2:6], self.wconvT[:, j, 2:3], v4(t2), ALU.mult, ALU.add)

    def carve(self):
        ktf = self.KTf.ap
        self.S_all = Tile("S_all", ktf[:, 0:8256].bitcast(F32).rearrange("p (a n) -> p a n", n=32))
        self.PTs = Tile("PTs", ktf[:, 8256:12384].rearrange("p (a n) -> p a n", n=32))
        self.Dexp = Tile("Dexp", ktf[:, 12384:14432].rearrange("p (k n) -> p k n", n=32))
        self.IDX = Tile("IDX", ktf[:, 14432:15456].bitcast(I32))
        vef = self.VEf.ap
        self.kpg = [Tile("kpg%d" % i, vef[:, i * 512:(i + 1) * 512]) for i in range(6)]
        self.kTp = [Tile("kTp%d" % i, vef[:, 3072 + i * 512:3072 + (i + 1) * 512]) for i in range(3)]
        self.ptI = Tile("ptI", vef[:, 4608:5632].bitcast(I32))
        self.ptF = Tile("ptF", vef[:, 5632:6656].bitcast(F32))
        self.sc_t = Tile("sc_t", vef[:, 6656:7168].bitcast(F32).rearrange("p (s c t) -> p s c t", s=2, c=8))
        self.sh_t = Tile("sh_t", vef[:, 7168:7680].bitcast(F32).rearrange("p (s c t) -> p s c t", s=2, c=8))
        self.Qb = Tile("Qb", vef[:, 7680:8192].rearrange("p (b c n) -> p b c n", b=4, c=4))
        xf = self.xs1f.ap
        pos = [0]

        def f32(name, n, shape=None):
            ap = xf[:, pos[0]:pos[0] + n]
            pos[0] += n
            return Tile(name, ap)

        def bf(name, n):
            ap = xf[:, pos[0]:pos[0] + n // 2].bitcast(BF16)
            pos[0] += n // 2
            return Tile(name, ap)

        self.cso = Tile("cso", f32("cso_", 176).ap.rearrange("p (j b t) -> p j b t", j=NFC, b=4))
        self.qs_bf = bf("qs_bf", 512)
        self.vsb = bf("vsb", 512)
        self.qTs = Tile("qTs", bf("qTs_", 64).ap.rearrange("p (c t) -> p c t", c=4))
        self.kmTs = Tile("kmTs", bf("kmTs_", 256).ap.rearrange("p (c k) -> p c k", c=4))
        self.sc_sb = f32("sc_sb", 64)
        self.top8 = f32("top8", 8)
        self.negb = f32("negb", 64)
        self.omask = bf("omask", 512)
        self.rden_s = f32("rden_s", 2)
        self.ones_col = bf("ones_col", 4)
        self.selb = Tile("selb", bf("selb_", 512).ap.rearrange("p (b k) -> p b k", b=4))
        self.caus = f32("caus", 32)
        self.hmask = f32("hmask", 8)
        self.gsum = bf("gsum", 4)
        self.i32f = f32("i32f", 32)
        self.iota = f32("iota", 2)
        self.ones32 = bf("ones32", 128)
        assert pos[0] <= 2048, pos[0]
        self.uTs = View(self.uT, self.uT.ap[:, :, 0:76].rearrange("p g (b t) -> p g b t", b=4))

    def sample(self):
        C, nc, din, dout = self.C, self.nc, self.din, self.dout
        C.memset("dve", self.xs[1][0:1, 0, 0:1], 0.0)
        C.memset("dve", self.KT[0:1, 0, 0:1], 0.0)
        C.memset("dve", self.VE[0:1, 0, 0, 0:1], 0.0)
        C.barrier()
        self.carve()
        pg = self.plan_group(None)
        if self.mode == "A":
            pg = pg[:9]
        else:
            pg = pg[:6] + pg[9:]
        self.plan += self.plan_gate() + pg
        xs = self.xs[0]
        C.dma("sp", xs[0:16, 0, :], din["xs"])
        self.load_gate([2 + t // 4 for t in range(16)])
        for b in range(4):
            C.copy("dve", self.sc_t[:, :, :, 4 * b:4 * b + 4], self.scl[:, :, :, 2 + b:3 + b].bcast([128, 2, 8, 4]))
            for s, mi in ((0, 0), (1, 3)):
                C.copy("dve", self.sh_t[:, s, :, 4 * b:4 * b + 4],
                       self.modT[:, mi * 8:(mi + 1) * 8, 2 + b:3 + b].bcast([128, 8, 4]))
        self.norm_to_hT(xs, 0, 1, 16, mod_tiles=(self.sc_t, self.sh_t))
        C.copy("pool", self.uTs[:, :, :, 0:15], self.stH[:])
        self.mixer_front(16, first=False, sample=True)
        if self.mode == "A":
            with nc.allow_non_contiguous_dma(reason="15-row pooling state, transposed store"):
                for b in range(4):
                    for g in range(4):
                        C.dma("pool", dout["pool_s"][b * 15:(b + 1) * 15, g * 128:(g + 1) * 128].rearrange("t c -> c t"),
                              self.uTs[:, g, b, 4:19], is_output=True)
        if self.mode == "A":
            wq = self.wget(("w_in", "k8", 512, 512))[:].rearrange("p (kc n) -> p kc n", kc=8)
            wk = self.wget(("w_in", "k8", 1024, 512), keep=1)[:].rearrange("p (kc n) -> p kc n", kc=8)
            wv = self.wget(("w_in", "k8", 1536, 512), keep=2)[:].rearrange("p (kc n) -> p kc n", kc=8)
            self.qkv_tile(0, 16, wq, wk, wv, dout["k_s"], dout["v_s"],
                          self.vsb[0:16, :].rearrange("p (h d) -> p h d", h=8),
                          q_dst=self.rtmp[0][0:16, :].rearrange("p (h d) -> p h d", h=8))
            C.dma("pool", dout["q_s"], self.rtmp[0][0:16, :], is_output=True)
            return
        C.dma("sp", self.rtmp[0][0:16, :], din["o_in"])
        C.copy("dve", self.obf[0:16, :], self.rtmp[0][0:16, :])
        bk = self.bank()
        bv = bk[:].bitcast(BF16).rearrange("p (c n) -> p c n", c=8)
        C.transposes([(bv[:, c, 0:16], self.obf[0:16, c * 128:(c + 1) * 128]) for c in range(4)], self.ident)
        C.copy("act", self.oT[:, :, 0:16], bv[:, 0:4, 0:16])
        self.mixer_back(16, 1, 16, xs)
        self.norm_to_hT(xs, 1, 1, 16, mod_tiles=(self.sc_t, self.sh_t))
        self.ffn(16, 1, 16, xs, hist_cols=True)
        C.dma("pool", dout["y_s"], xs[0:16, 0, :], is_output=True)
        with nc.allow_non_contiguous_dma(reason="2-row conv state, transposed store"):
            for b in range(4):
                for t in range(2):
                    C.dma("pool", dout["conv_s"][b * 2 + t:b * 2 + t + 1, :].rearrange("o (j c) -> c (o j)", c=128),
                          self.cso[:, :, b, t], is_output=True)

    def build(self):
        self.setup()
        if self.mode == "A":
            for seq in range(NPB):
                self.prompt_seq(seq)
        self.sample()
        self.C.finish()


NBS = 32
NB_CORES = 4


def host_consts_B(nb=NBS):
    c = {}
    c["ident"] = np.eye(128, dtype=np.float32)
    sel = np.zeros((4 * nb, nb, 128), np.float32)
    for b in range(nb):
        for s in range(4):
            sel[4 * b + s, b, s] = 1.0
    c["selB"] = sel.reshape(4 * nb, nb * 128)
    hc = host_consts()
    for k in ("caus", "hmask", "gsum", "i32"):
        c[k] = hc[k]
    c["iota"] = (2.0 * (np.arange(128) % 64)).astype(np.float32).reshape(128, 1)
    return c


def B_in_shapes(nb, npool):
    nt = 4 * nb
    return {"cache_k": ([npool * 128, 512], F32), "cache_v": ([npool * 128, 512], F32), "ptab": ([1, nb * NPAGES], I32),
            "q_all": ([nt, 512], F32), "k_all": ([nt, 512], F32), "v_all": ([nt, 512], F32),
            "c_ident": ([128, 128], F32), "c_selB": ([nt, nb * 128], F32), "c_caus": ([128, 32], F32),
            "c_hmask": ([32, 8], F32), "c_gsum": ([32, 4], F32), "c_i32": ([32, 32], F32), "c_iota": ([128, 1], F32)}


class KernB:
    def __init__(self, nc, es, nb=NBS, npool=5120, debug=False):
        self.nc = nc
        self.nb = nb
        C = self.C = Ctx(nc, es)
        self.din = {}
        nt = 4 * nb
        self.nt = nt
        for k, (shp, dt) in B_in_shapes(nb, npool).items():
            self.din[k] = nc.dram_tensor(k, shp, dt, kind="ExternalInput").ap()
        self.o_all = nc.dram_tensor("o_all", [nt, 512], F32, kind="ExternalOutput").ap()
        self.debug = debug
        if debug:
            self.dbg = {k: nc.dram_tensor(k, shp, F32, kind="ExternalOutput").ap() for k, shp in
                        (("d_S", [128, 129 * 32]), ("d_S2", [128, 129 * 32]), ("d_sc", [32, 64]), ("d_negb", [32, 64]),
                         ("d_oacc", [32, 512]), ("d_den", [32, 2]))}
        sb = C.sbuf
        self.ident = sb("ident", [128, 128], BF16)
        self.selB = sb("selB", [nt, nb, 128], BF16)
        self.caus = sb("caus", [128, 32], F32)
        self.hmask = sb("hmask", [32, 8], F32)
        self.gsum = sb("gsum", [32, 4], BF16)
        self.i32f = sb("i32f", [32, 32], F32)
        self.iota = sb("iota", [128, 1], F32)
        self.ones_col = sb("ones_col", [128, 2], BF16)
        self.ones32 = sb("ones32", [32, 128], BF16)
        self.ptI = sb("ptI", [128, nb * NPAGES], I32)
        self.ptF = sb("ptF", [128, nb * NPAGES], F32)
        self.ptB = sb("ptB", [128, nb * 64], F32)
        self.IDX = sb("IDX", [128, nb * 64], I32)
        self.qbf = sb("qbf", [nt, 512], BF16)
        self.kbf = sb("kbf", [nt, 512], BF16)
        self.vbf = sb("vbf", [nt, 512], BF16)
        self.qTs = sb("qTs", [128, 4, nt], BF16)
        self.Qb = sb("Qb", [128, nb, 4, 32], BF16)
        self.S_all = sb("S_all", [128, NPAGES + 1, 32], F32)
        self.PTs = sb("PTs", [128, NPAGES + 1, 32], BF16)
        self.Dexp = sb("Dexp", [32, 64, 32], BF16)
        self.kpg = [sb("kpg%d" % i, [128, 1024], BF16) for i in range(6)]
        self.kTp = [sb("kTp%d" % i, [128, 512], BF16) for i in range(4)]
        self.kmTs = sb("kmTs", [128, 4, 64], BF16)
        self.sc_sb = sb("sc_sb", [32, 64], F32)
        self.top8 = sb("top8", [32, 8], F32)
        self.negb = sb("negb", [32, 64], F32)
        self.omask = sb("omask", [32, 512], BF16)
        self.rden_s = sb("rden_s", [32, 2], F32)
        self.osb = [sb("osb%d" % i, [4, 512], F32) for i in range(2)]
        self.banks = [C.psum("bank%d" % i, [128, 512], F32) for i in range(8)]
        self.held = set()
        self.bank_i = 0

    bank = Kern.bank
    release = Kern.release

    def build(self):
        C, nc, din = self.C, self.nc, self.din
        C.dma("pool", self.ident[:], din["c_ident"])
        C.dma("pool", self.selB[:].rearrange("p b k -> p (b k)"), din["c_selB"])
        C.dma("sp", self.caus[:], din["c_caus"])
        C.dma("sp", self.hmask[:], din["c_hmask"])
        C.dma("pool", self.gsum[:], din["c_gsum"])
        C.dma("sp", self.i32f[:], din["c_i32"])
        C.dma("sp", self.iota[:], din["c_iota"])
        C.dma("pool", self.qbf[:], din["q_all"])
        C.dma("pool", self.kbf[:], din["k_all"])
        C.dma("pool", self.vbf[:], din["v_all"])
        C.memset("dve", self.ones_col[:], 1.0)
        C.memset("dve", self.ones32[:], 1.0)
        C.dma("sp", self.ptI[:], din["ptab"].partition_broadcast(128))
        C.copy("dve", self.ptF[:], self.ptI[:])
        pf3 = self.ptF[:].rearrange("p (k two) -> p k two", two=2)
        for j in range(2):
            C.ts("dve", self.ptB[j * 64:(j + 1) * 64, :], pf3[j * 64:(j + 1) * 64, :, j], 128.0,
                 self.iota[j * 64:(j + 1) * 64, 0:1], ALU.mult, ALU.add)
        C.copy("dve", self.IDX[:], self.ptB[:])
        bk = self.bank()
        bv = bk[:].bitcast(BF16)
        nt = self.nt
        C.transposes([(bv[:, c * nt:(c + 1) * nt], self.qbf[:, c * 128:(c + 1) * 128]) for c in range(4)], self.ident)
        C.copy("dve", self.qTs[:], bv[:, 0:4 * nt].rearrange("p (c t) -> p c t", c=4))
        C.memset("dve", self.Qb[:], 0.0)
        qb5 = self.Qb[:].rearrange("p b c (s h) -> p b c s h", h=8)
        for c in range(4):
            for e in range(2):
                C.copy("pool" if e else "dve", qb5[e * 64:(e + 1) * 64, :, c, :, 2 * c + e],
                       self.qTs[e * 64:(e + 1) * 64, c, :].rearrange("p (b s) -> p b s", s=4))
        for b in range(self.nb):
            self.attn_b(b)
        C.finish()

    def attn_b(self, b):
        C, nc, din = self.C, self.nc, self.din
        kmacc = self.bank(hold=True)
        sbank = None
        sb_state = {"bank": None}

        def qk(lp, kt):
            if lp % 16 == 0:
                sb_state["bank"] = self.bank(hold=True)
            sbank = sb_state["bank"]
            C.mm(sbank[:, (lp % 16) * 32:(lp % 16 + 1) * 32],
                 [(kt[:, c * 128:(c + 1) * 128], self.Qb[:, b, c, :]) for c in range(4)])
            if lp % 16 == 15:
                C.copy("act", self.S_all[:, lp - 15:lp + 1, :], sbank[:, :].rearrange("p (a n) -> p a n", a=16))
                self.release(sbank)

        prev = None
        for lp in range(NPAGES):
            blk, e = lp // 2, lp % 2
            kp = self.kpg[blk % 6]
            if e == 0:
                C.dma("pool", kp[:, :], din["cache_k"], indirect_in=self.IDX[:, b * 64 + blk:b * 64 + blk + 1])
                for c in range(4):
                    col = c * 64 + blk
                    C.mm(kmacc[:, col:col + 1], [(kp[:, ee * 512 + c * 128:ee * 512 + (c + 1) * 128], self.ones_col[:, 0:1])
                                                 for ee in range(2)])
            tb = self.bank()
            tv = tb[:].bitcast(BF16)
            C.transposes([(tv[:, c * 128:(c + 1) * 128], kp[:, e * 512 + c * 128:e * 512 + (c + 1) * 128]) for c in range(4)],
                         self.ident)
            kt = self.kTp[lp % 4]
            C.copy("act" if lp % 2 else "dve", kt[:, :], tv[:, 0:512])
            if prev is not None:
                qk(*prev)
            prev = (lp, kt)
        qk(*prev)
        xb = self.bank()
        for c in range(4):
            C.mm(xb[:, c * 128:(c + 1) * 128], [(self.kbf[:, c * 128:(c + 1) * 128], self.selB[:, b, :])])
        kt = self.kTp[NPAGES % 4]
        C.copy("dve", kt[:, :], xb[:, :])
        sb2 = self.bank()
        C.mm(sb2[:, 0:32], [(kt[:, c * 128:(c + 1) * 128], self.Qb[:, b, c, :]) for c in range(4)])
        C.copy("act", self.S_all[:, NPAGES, :], sb2[:, 0:32])
        if self.debug and b == 0:
            C.dma("sp", self.dbg["d_S"], self.S_all[:].rearrange("p a n -> p (a n)"), is_output=True)
        C.copy("dve", self.kmTs[:], kmacc[:, 0:256].rearrange("p (c k) -> p c k", c=4))
        self.release(kmacc)
        scb = self.bank()
        C.mm(scb[0:32, 0:64], [(self.Qb[:, b, c, :], self.kmTs[:, c, :]) for c in range(4)])
        C.copy("dve", self.sc_sb[:, :], scb[0:32, 0:64])
        C.op("dve", lambda: nc.vector.max(out=self.top8.ap, in_=self.sc_sb.ap), [self.sc_sb.v()], [self.top8.v()])
        C.ts("dve", self.negb[:, :], self.sc_sb[:, :], self.top8[:, 2:3], NEG, ALU.is_lt, ALU.mult)
        C.tt("dve", self.Dexp[:], self.negb[:, :].unsq(2).bcast([32, 64, 32]),
             self.i32f[:, :].unsq(1).bcast([32, 64, 32]), ALU.mult)
        for q in range(4):
            bb = self.bank()
            C.mm(bb[:, :], [(self.ones32[:, :], self.Dexp[:, q * 16:(q + 1) * 16, :].rearrange("p k n -> p (k n)"))])
            Sv = self.S_all[:, q * 32:(q + 1) * 32, :].rearrange("p (k two) n -> p k two n", two=2)
            C.tt("dve", Sv, Sv, bb[:, :].rearrange("p (k n) -> p k n", k=16).unsq(2).bcast([128, 16, 2, 32]), ALU.add)
        C.tt("dve", self.S_all[:, NPAGES, :], self.S_all[:, NPAGES, :], self.caus[:, :], ALU.add)
        if self.debug and b == 0:
            C.dma("sp", self.dbg["d_S2"], self.S_all[:].rearrange("p a n -> p (a n)"), is_output=True)
            C.dma("sp", self.dbg["d_sc"], self.sc_sb[:], is_output=True)
            C.dma("sp", self.dbg["d_negb"], self.negb[:], is_output=True)
        for a in range(0, NPAGES + 1, 43):
            C.act(self.PTs[:, a:a + 43, :], self.S_all[:, a:a + 43, :], AF.Exp)
        oacc = self.bank(hold=True)
        dacc = self.bank(hold=True)
        for lp in range(NPAGES):
            blk, e = lp // 2, lp % 2
            vp = self.kpg[blk % 6]
            if e == 0:
                C.dma("pool", vp[:, :], din["cache_v"], indirect_in=self.IDX[:, b * 64 + blk:b * 64 + blk + 1])
            C.mm(oacc[0:32, :], [(self.PTs[:, lp, :], vp[:, e * 512:(e + 1) * 512])], start=(lp == 0), stop=False)
            C.mm(dacc[0:32, 0:1], [(self.PTs[:, lp, :], self.ones_col[:, 0:1])], start=(lp == 0), stop=False)
        vx = self.bank()
        C.mm(vx[:, :], [(self.selB[:, b, :], self.vbf[:, :])])
        vp = self.kpg[(NPAGES // 2) % 6]
        C.copy("dve", vp[:, 0:512], vx[:, :])
        C.mm(oacc[0:32, :], [(self.PTs[:, NPAGES, :], vp[:, 0:512])], start=False, stop=True)
        C.mm(dacc[0:32, 0:1], [(self.PTs[:, NPAGES, :], self.ones_col[:, 0:1])], start=False, stop=True)
        if self.debug and b == 0:
            C.copy("dve", self.sc_sb[:, 0:2], dacc[0:32, 0:2])
            C.dma("sp", self.dbg["d_den"], self.sc_sb[:, 0:2], is_output=True)
        C.recip(self.rden_s[:, 0:1], dacc[0:32, 0:1])
        C.stt("dve", self.omask[:, :].rearrange("p (h d) -> p h d", h=8),
              oacc[0:32, :].rearrange("p (h d) -> p h d", h=8), self.rden_s[:, 0:1],
              self.hmask[:, :].unsq(2).bcast([32, 8, 64]), ALU.mult, ALU.mult)
        self.release(oacc)
        self.release(dacc)
        ob = self.bank()
        C.mm(ob[0:4, :], [(self.gsum[:, :], self.omask[:, :])])
        osb = self.osb[b % 2]
        C.copy("act", osb[:, :], ob[0:4, :])
        C.dma("sp", self.o_all[4 * b:4 * b + 4, :], osb[:, :], is_output=True)


def build_program_B(nb=NBS, npool=5120, debug=False):
    nc = bass.Bass("TRN2", target_bir_lowering=False)
    es = ExitStack()
    with es:
        k = KernB(nc, es, nb=nb, npool=npool, debug=debug)
        k.build()
    return nc


def build_program(mode="A"):
    nc = bass.Bass("TRN2", target_bir_lowering=False)
    es = ExitStack()
    with es:
        k = Kern(nc, es, mode=mode)
        k.build()
    return nc


_CACHE = {}


def _in_maps(inputs, mode="A"):
    consts = host_consts()
    f = lambda a: np.ascontiguousarray(a, dtype=np.float32)
    maps = []
    shared = {
        "w_ada": f(inputs["w_ada"][0]), "b_ada": f(inputs["b_ada"][0]).reshape(1, -1),
        "g_mix": f(inputs["g_norm_mix"][0]).reshape(1, -1), "w_in": f(inputs["w_in"][0]),
        "g_q": f(inputs["g_q"][0]).reshape(1, -1), "g_k": f(inputs["g_k"][0]).reshape(1, -1),
        "w_pool": f(inputs["w_pool_group"][0]).reshape(512, 128),
        "pool_scale": f(inputs["pool_scale"][0]).reshape(1, -1),
        "w_bp": f(inputs["w_branch_pool"][0]), "w_ba": f(inputs["w_branch_attn"][0]),
        "w_out": f(inputs["w_out"][0]), "g_ffn": f(inputs["g_norm_ffn"][0]).reshape(1, -1),
        "w_up": f(inputs["w_up"][0]), "w_conv": f(inputs["w_conv"][0]),
        "b_conv": f(inputs["b_conv"][0]).reshape(1, -1), "w_down": f(inputs["w_down"][0]),
    }
    for k, v in consts.items():
        shared["c_" + k] = v
    for i in range(8):
        m = dict(shared)
        if mode == "A":
            m["xp"] = f(inputs["x_prompt"][NPB * i:NPB * (i + 1)]).reshape(NPB * SEQ, D)
        m["xs"] = f(inputs["x_sample"][NSB * i:NSB * (i + 1)]).reshape(NST, D)
        m["c_all"] = np.concatenate([f(inputs["c_prompt"][NPB * i:NPB * (i + 1)]),
                                     f(inputs["c_sample"][NSB * i:NSB * (i + 1)])], axis=0)
        m["st_pool"] = f(inputs["state_pool"][0, NSB * i:NSB * (i + 1)]).reshape(NSB * 15, 512)
        m["st_conv"] = f(inputs["state_ffn_conv"][0, NSB * i:NSB * (i + 1)]).reshape(NSB * 2, DFF)
        maps.append(m)
    return maps


def _prog(key, fn):
    if key not in _CACHE:
        _CACHE[key] = fn()
    return _CACHE[key]


def kernel(**inputs):
    cat = lambda res, k: np.concatenate([r[k] for r in res], axis=0)
    maps = _in_maps(inputs, "A")
    ra = run_bass_kernel_spmd(_prog("A", lambda: build_program("A")), maps, core_ids=list(range(8))).results
    q_all, k_all, v_all = cat(ra, "q_s"), cat(ra, "k_s"), cat(ra, "v_s")
    ck = np.ascontiguousarray(inputs["cache_k"][0], dtype=np.float32).reshape(5120 * 128, 512)
    cv = np.ascontiguousarray(inputs["cache_v"][0], dtype=np.float32).reshape(5120 * 128, 512)
    nbc = NBS // NB_CORES
    cb = host_consts_B(nbc)
    mbs = []
    for i in range(NB_CORES):
        r = slice(4 * nbc * i, 4 * nbc * (i + 1))
        mb = {"cache_k": ck, "cache_v": cv,
              "ptab": np.ascontiguousarray(inputs["page_table"][nbc * i:nbc * (i + 1)], dtype=np.int32).reshape(1, -1),
              "q_all": np.ascontiguousarray(q_all[r]), "k_all": np.ascontiguousarray(k_all[r]),
              "v_all": np.ascontiguousarray(v_all[r])}
        for k, v in cb.items():
            mb["c_" + k] = v
        mbs.append(mb)
    rb = run_bass_kernel_spmd(_prog("B", lambda: build_program_B(nb=nbc)), mbs, core_ids=list(range(NB_CORES))).results
    o_all = np.concatenate([r["o_all"] for r in rb], axis=0)
    maps = _in_maps(inputs, "C")
    for i in range(8):
        maps[i]["o_in"] = np.ascontiguousarray(o_all[NST * i:NST * (i + 1)])
    rc = run_bass_kernel_spmd(_prog("C", lambda: build_program("C")), maps, core_ids=list(range(8))).results
    y_p = cat(ra, "y_p").reshape(16, SEQ, D)
    y_s = cat(rc, "y_s").reshape(32, 4, D)
    k_p = cat(ra, "k_p").reshape(1, 16, SEQ, 8, 64)
    v_p = cat(ra, "v_p").reshape(1, 16, SEQ, 8, 64)
    pool_p = cat(ra, "pool_p").reshape(1, 16, 15, 512)
    conv_p = cat(ra, "conv_p").reshape(1, 16, 2, DFF)
    k_s = k_all.reshape(1, 32, 4, 8, 64)
    v_s = v_all.reshape(1, 32, 4, 8, 64)
    pool_s = cat(ra, "pool_s").reshape(1, 32, 15, 512)
    conv_s = cat(rc, "conv_s").reshape(1, 32, 2, DFF)
    return (y_p, y_s, k_p, v_p, pool_p, conv_p, k_s, v_s, pool_s, conv_s)
```
